# Optimizing a Trainium2 kernel written in Bass

```python
import jax, jax.numpy as jnp
from jax import lax
import numpy as np

D_MODEL = 1024
BATCH = 2
SEQ = 16384
DEPTH = 1

GLA_HEADS = 4
GLA_DK = D_MODEL // 2 // GLA_HEADS
GLA_DV = D_MODEL // GLA_HEADS
GLA_RANK = 16
GLA_GATE_NORM = 16.0
GDN_HEADS = 8
GDN_DK = D_MODEL // GDN_HEADS
GDN_DV = D_MODEL // GDN_HEADS
CONV_K = 4
CHUNK = 64
D_FF = 4 * D_MODEL
EPS = 1e-6

GLA_QK = GLA_HEADS * GLA_DK
GLA_V = GLA_HEADS * GLA_DV
GDN_QK = GDN_HEADS * GDN_DK
GDN_V = GDN_HEADS * GDN_DV
GDN_CONV_DIM = 2 * GDN_QK + GDN_V
IN_WIDTHS = (GLA_QK, GLA_QK, GLA_V, GLA_V, GLA_RANK,
             GDN_CONV_DIM, GDN_V, GDN_HEADS, GDN_HEADS,
             D_MODEL, D_MODEL)
IN_DIM = 2 * GLA_QK + 2 * GLA_V + GLA_RANK + GDN_CONV_DIM + GDN_V + 2 * GDN_HEADS + 2 * D_MODEL

kernel_name = 'hybrid_gla_gdn_gated_merge_block'


def rms_norm(x, w):
    xf = x.astype(jnp.float32)
    y = xf * lax.rsqrt(jnp.mean(xf * xf, axis=-1, keepdims=True) + EPS)
    return (y * w.astype(jnp.float32)).astype(x.dtype)


def l2_norm(x):
    xf = x.astype(jnp.float32)
    return (xf * lax.rsqrt(jnp.sum(xf * xf, axis=-1, keepdims=True) + EPS)).astype(x.dtype)


def causal_short_conv(x, w):
    s = x.shape[1]
    k = w.shape[0]
    xp = jnp.pad(x, ((0, 0), (k - 1, 0), (0, 0)))
    y = xp[:, 0:s] * w[0]
    for j in range(1, k):
        y = y + xp[:, j:j + s] * w[j]
    return y


def to_chunks(t):
    b, s, h = t.shape[:3]
    t = t.reshape((b, s // CHUNK, CHUNK, h) + t.shape[3:])
    return jnp.moveaxis(t, 3, 1)


def from_chunks(t):
    b, h, n, c, d = t.shape
    return jnp.transpose(t, (0, 2, 3, 1, 4)).reshape(b, n * c, h, d)


def gla_chunked(q, k, v, gk):
    out_dtype = v.dtype
    dk = q.shape[-1]
    q = to_chunks(q.astype(jnp.float32)) * (dk ** -0.5)
    k = to_chunks(k.astype(jnp.float32))
    v = to_chunks(v.astype(jnp.float32))
    b = jnp.cumsum(to_chunks(gk.astype(jnp.float32)), axis=3)
    b_last = b[:, :, :, -1:, :]
    q_dec = q * jnp.exp(b)
    k_inv = k * jnp.exp(-b)
    causal = jnp.tril(jnp.ones((CHUNK, CHUNK), dtype=bool))
    att = jnp.einsum('bhnid,bhnjd->bhnij', q_dec, k_inv)
    att = jnp.where(causal, att, 0.0)
    o_intra = jnp.einsum('bhnij,bhnjv->bhniv', att, v)
    k_state = k * jnp.exp(b_last - b)
    dec = jnp.exp(b_last[:, :, :, 0, :])

    def step(state, inp):
        qd, ks, vv, dc = inp
        o = jnp.einsum('bhcd,bhdv->bhcv', qd, state)
        state = dc[..., None] * state + jnp.einsum('bhcd,bhcv->bhdv', ks, vv)
        return state, o

    bsz, h = q.shape[0], q.shape[1]
    s0 = jnp.zeros((bsz, h, dk, v.shape[-1]), jnp.float32)
    xs = (jnp.moveaxis(q_dec, 2, 0), jnp.moveaxis(k_state, 2, 0),
          jnp.moveaxis(v, 2, 0), jnp.moveaxis(dec, 2, 0))
    _, o_inter = lax.scan(step, s0, xs)
    o = o_intra + jnp.moveaxis(o_inter, 0, 2)
    return from_chunks(o).astype(out_dtype)


def gated_delta_chunked(q, k, v, beta, g):
    out_dtype = v.dtype
    dk = q.shape[-1]
    q = to_chunks(q.astype(jnp.float32)) * (dk ** -0.5)
    k = to_chunks(k.astype(jnp.float32))
    v = to_chunks(v.astype(jnp.float32))
    beta = to_chunks(beta.astype(jnp.float32))
    gc = jnp.cumsum(to_chunks(g.astype(jnp.float32)), axis=-1)
    causal = jnp.tril(jnp.ones((CHUNK, CHUNK), dtype=bool))
    strict = jnp.tril(jnp.ones((CHUNK, CHUNK), dtype=bool), k=-1)
    diff = gc[..., :, None] - gc[..., None, :]
    decay_mat = jnp.exp(jnp.where(causal, diff, -jnp.inf))
    kb = k * beta[..., None]
    a_mat = jnp.where(strict, jnp.einsum('bhnid,bhnjd->bhnij', kb, k) * decay_mat, 0.0)
    eye = jnp.eye(CHUNK, dtype=jnp.float32)
    rhs = jnp.concatenate([v * beta[..., None], kb * jnp.exp(gc)[..., None]], axis=-1)
    sol = lax.linalg.triangular_solve(a_mat + eye, rhs, left_side=True, lower=True)
    dv = v.shape[-1]
    u = sol[..., :dv]
    w = sol[..., dv:]
    att = jnp.einsum('bhnid,bhnjd->bhnij', q, k) * decay_mat
    q_dec = q * jnp.exp(gc)[..., None]
    g_last = gc[..., -1]
    k_state = k * jnp.exp(g_last[..., None] - gc)[..., None]
    dec = jnp.exp(g_last)

    def step(state, inp):
        qd, at, uu, ww, ks, dc = inp
        v_new = uu - jnp.einsum('bhcd,bhdv->bhcv', ww, state)
        o = jnp.einsum('bhcd,bhdv->bhcv', qd, state) + jnp.einsum('bhij,bhjv->bhiv', at, v_new)
        state = dc[..., None, None] * state + jnp.einsum('bhcd,bhcv->bhdv', ks, v_new)
        return state, o

    bsz, h = q.shape[0], q.shape[1]
    s0 = jnp.zeros((bsz, h, dk, dv), jnp.float32)
    xs = tuple(jnp.moveaxis(t, 2, 0) for t in (q_dec, att, u, w, k_state, dec))
    _, o = lax.scan(step, s0, xs)
    return from_chunks(jnp.moveaxis(o, 0, 2)).astype(out_dtype)


def hybrid_layer(x, c, ada_w, ada_b, pre_mix_w, post_mix_w, pre_mlp_w, post_mlp_w,
                 w_in, gla_w_lr, gla_b_lr, gla_onorm_w, gdn_conv_w, gdn_a_log, gdn_dt_bias,
                 gdn_onorm_w, w_branch_gla, w_branch_gdn, w_out, mlp_w1, mlp_w2):
    bsz, s, d = x.shape
    mod = jax.nn.silu(c) @ ada_w + ada_b
    shift1, scale1, gate1, shift2, scale2, gate2 = [m[:, None, :] for m in jnp.split(mod, 6, axis=-1)]

    h = rms_norm(x, pre_mix_w) * (1.0 + scale1) + shift1
    proj = h @ w_in
    idx = [int(i) for i in np.cumsum(np.array(IN_WIDTHS))[:-1]]
    (a_q, a_k, a_v, a_g, a_lr, d_qkv, d_g, d_beta, d_a, m_a, m_d) = jnp.split(proj, idx, axis=-1)

    a_gk = jax.nn.log_sigmoid(a_lr @ gla_w_lr + gla_b_lr) / GLA_GATE_NORM
    o_a = gla_chunked(a_q.reshape(bsz, s, GLA_HEADS, GLA_DK),
                      a_k.reshape(bsz, s, GLA_HEADS, GLA_DK),
                      a_v.reshape(bsz, s, GLA_HEADS, GLA_DV),
                      a_gk.reshape(bsz, s, GLA_HEADS, GLA_DK))
    o_a = rms_norm(o_a, gla_onorm_w) * jax.nn.silu(a_g.reshape(bsz, s, GLA_HEADS, GLA_DV))
    y_a = o_a.reshape(bsz, s, GLA_V) @ w_branch_gla

    d_qkv = jax.nn.silu(causal_short_conv(d_qkv, gdn_conv_w))
    d_q, d_k, d_v = jnp.split(d_qkv, [GDN_QK, 2 * GDN_QK], axis=-1)
    d_q = l2_norm(d_q.reshape(bsz, s, GDN_HEADS, GDN_DK))
    d_k = l2_norm(d_k.reshape(bsz, s, GDN_HEADS, GDN_DK))
    beta = jax.nn.sigmoid(d_beta)
    g = -jnp.exp(gdn_a_log) * jax.nn.softplus(d_a + gdn_dt_bias)
    o_d = gated_delta_chunked(d_q, d_k, d_v.reshape(bsz, s, GDN_HEADS, GDN_DV), beta, g)
    o_d = rms_norm(o_d, gdn_onorm_w) * jax.nn.silu(d_g.reshape(bsz, s, GDN_HEADS, GDN_DV))
    y_d = o_d.reshape(bsz, s, GDN_V) @ w_branch_gdn

    merged = jax.nn.sigmoid(m_a) * y_a + jax.nn.sigmoid(m_d) * y_d
    x = x + gate1 * rms_norm(merged @ w_out, post_mix_w)

    h2 = rms_norm(x, pre_mlp_w) * (1.0 + scale2) + shift2
    y = jnp.square(jax.nn.relu(h2 @ mlp_w1)) @ mlp_w2
    x = x + gate2 * rms_norm(y, post_mlp_w)
    return x


def setup_inputs(seed: int = 0) -> dict:
    key = jax.random.key(seed)
    ks = jax.random.split(key, 24)
    L, D = DEPTH, D_MODEL
    f32 = jnp.float32

    def nrm(k, shape, scale):
        return jax.random.normal(k, shape, f32) * scale

    def gain(k, n):
        return 1.0 + 0.02 * jax.random.normal(k, (L, n), f32)

    dt = jnp.exp(jax.random.uniform(ks[13], (L, GDN_HEADS), f32, np.log(1e-3), np.log(1e-1)))
    return {
        'x': jax.random.normal(ks[0], (BATCH, SEQ, D), f32),
        'c': jax.random.normal(ks[1], (BATCH, D), f32),
        'ada_w': nrm(ks[2], (L, D, 6 * D), D ** -0.5),
        'ada_b': nrm(ks[3], (L, 6 * D), 0.02),
        'pre_mix_w': gain(ks[4], D),
        'post_mix_w': gain(ks[5], D),
        'pre_mlp_w': gain(ks[6], D),
        'post_mlp_w': gain(ks[7], D),
        'w_in': nrm(ks[8], (L, D, IN_DIM), D ** -0.5),
        'gla_w_lr': nrm(ks[9], (L, GLA_RANK, GLA_QK), GLA_RANK ** -0.5),
        'gla_b_lr': nrm(ks[10], (L, GLA_QK), 0.1),
        'gla_onorm_w': gain(ks[11], GLA_DV),
        'gdn_conv_w': nrm(ks[12], (L, CONV_K, GDN_CONV_DIM), CONV_K ** -0.5),
        'gdn_a_log': jnp.log(jax.random.uniform(ks[14], (L, GDN_HEADS), f32, 1.0, 16.0)),
        'gdn_dt_bias': dt + jnp.log(-jnp.expm1(-dt)),
        'gdn_onorm_w': gain(ks[15], GDN_DV),
        'w_branch_gla': nrm(ks[16], (L, GLA_V, D), GLA_V ** -0.5),
        'w_branch_gdn': nrm(ks[17], (L, GDN_V, D), GDN_V ** -0.5),
        'w_out': nrm(ks[18], (L, D, D), D ** -0.5),
        'mlp_w1': nrm(ks[19], (L, D, D_FF), D ** -0.5),
        'mlp_w2': nrm(ks[20], (L, D_FF, D), D_FF ** -0.5),
    }


def reference(x, c, ada_w, ada_b, pre_mix_w, post_mix_w, pre_mlp_w, post_mlp_w,
              w_in, gla_w_lr, gla_b_lr, gla_onorm_w, gdn_conv_w, gdn_a_log, gdn_dt_bias,
              gdn_onorm_w, w_branch_gla, w_branch_gdn, w_out, mlp_w1, mlp_w2):
    for l in range(DEPTH):
        x = hybrid_layer(x, c, ada_w[l], ada_b[l], pre_mix_w[l], post_mix_w[l], pre_mlp_w[l],
                         post_mlp_w[l], w_in[l], gla_w_lr[l], gla_b_lr[l], gla_onorm_w[l],
                         gdn_conv_w[l], gdn_a_log[l], gdn_dt_bias[l], gdn_onorm_w[l],
                         w_branch_gla[l], w_branch_gdn[l], w_out[l], mlp_w1[l], mlp_w2[l])
    return x
```

```python
import numpy as np
from contextlib import ExitStack
import concourse.bass as bass
import concourse.mybir as mybir
from concourse.bass_utils import run_bass_kernel_spmd

F32 = mybir.dt.float32
BF16 = mybir.dt.bfloat16
AF = mybir.ActivationFunctionType
ALU = mybir.AluOpType

D = 1024
NCH = 8
EPS = 1e-6
NFM = 1536
NTM = 288
BIG = 30000.0
C_ID, C_MU, C_SL, C_BIGC, C_TRI, C_BLK, C_SEL0, C_SEL1, C_ONES = range(9)
NCONST = 9


import os
SKIP = set(os.environ.get("KSKIP", "").split(","))


class Sched:
    ENGS = ("pe", "act", "dve", "pool", "sp")

    def __init__(self, nc, es):
        self.nc = nc
        self.es = es
        self.sem = {e: es.enter_context(nc.semaphore("s_" + e)) for e in self.ENGS}
        self.cnt = {e: 0 for e in self.ENGS}
        self.prog = {e: [] for e in self.ENGS}
        self.waited = {e: {} for e in self.ENGS}
        self.lastw = {}
        self.readers = {}
        self.dsem = {}
        self.dcnt = {}

    def _need(self, eng, waits, tok):
        if tok is None:
            return
        kind, key, val = tok
        if kind == "e" and key == eng and eng == "pe":
            return
        w = self.waited[eng]
        if w.get((kind, key), 0) >= val:
            return
        w[(kind, key)] = val
        waits[(kind, key)] = max(waits.get((kind, key), 0), val)

    skip = False

    def stage(self, name):
        self.skip = name in SKIP

    def op(self, eng, fn, reads=(), writes=(), dma_key=None):
        if self.skip:
            return None
        banks = set()
        for k in list(reads) + list(writes):
            for pfx in ("pp0", "pp1", "pss", "ptm", "pg", "pi", "pw", "ptr", "mps"):
                if k == pfx or k.startswith(pfx + ".") or (pfx == "ptr" and k.startswith("ptr")):
                    banks.add("B:" + pfx)
        writes = list(writes) + sorted(banks)
        waits = {}
        for k in reads:
            self._need(eng, waits, self.lastw.get(k))
        for k in writes:
            self._need(eng, waits, self.lastw.get(k))
            for t in self.readers.get(k, ()):
                self._need(eng, waits, t)
        if dma_key is None:
            self.cnt[eng] += 1
            tok = ("e", eng, self.cnt[eng])
        else:
            if dma_key not in self.dsem:
                self.dsem[dma_key] = self.es.enter_context(self.nc.semaphore("d_" + str(dma_key)))
                self.dcnt[dma_key] = 0
            self.dcnt[dma_key] += 16
            tok = ("d", dma_key, self.dcnt[dma_key])
        self.prog[eng].append((waits, fn, tok))
        for k in reads:
            self.readers.setdefault(k, []).append(tok)
        for k in writes:
            self.lastw[k] = tok
            self.readers[k] = []
        return tok

    def barrier(self, engs=None):
        toks = [("e", e, self.cnt[e]) for e in self.ENGS if self.cnt[e] > 0]
        toks += [("d", k, v) for k, v in self.dcnt.items()]
        for e in (engs or self.ENGS):
            waits = {}
            for t in toks:
                self._need(e, waits, t)
            if waits:
                self.prog[e].append((waits, None, None))

    def emit(self):
        nc = self.nc
        prog = self.prog
        self.prog = {e: [] for e in self.ENGS}
        with nc.Block() as block:
            def body(e):
                def f(engine):
                    for waits, fn, tok in prog[e]:
                        for (kind, key), val in waits.items():
                            s = self.sem[key] if kind == "e" else self.dsem[key]
                            engine.wait_ge(s, val)
                        if fn is None:
                            continue
                        ins = fn(engine)
                        if tok[0] == "e":
                            ins.then_inc(self.sem[tok[1]], 1)
                        else:
                            ins.then_inc(self.dsem[tok[1]], 16)
                return f
            block.tensor(body("pe"))
            block.scalar(body("act"))
            block.vector(body("dve"))
            block.gpsimd(body("pool"))
            block.sync(body("sp"))


class Ctx:
    pass


def _mm(S, out, lhsT, rhs, start, stop, reads, writes):
    S.op("pe", lambda e: e.matmul(out, lhsT=lhsT, rhs=rhs, start=start, stop=stop), reads=reads, writes=writes)


def _act(S, out, in_, func, reads, writes, scale=None, bias=None):
    kw = {}
    if scale is not None:
        kw["scale"] = scale
    if bias is not None:
        kw["bias"] = bias
    S.op("act", lambda e: e.activation(out=out, in_=in_, func=func, **kw), reads=reads, writes=writes)


def _rstd(S, out, ss, inv_n, reads, writes):
    _act(S, out, ss, AF.Ln, reads, writes, scale=inv_n, bias=EPS)
    _act(S, out, out, AF.Exp, writes, writes, scale=-0.5)


def emit_common(g, S, es, nc, need_mix):
    sb = lambda name, shape, dt: es.enter_context(nc.sbuf_tensor("cm_" + name, shape, dt))
    g.cf = sb("cf", [128, NCONST, 128], F32)
    g.cb = sb("cb", [128, NCONST, 128], BF16)
    S.op("sp", lambda e: e.dma_start(out=g.cf[:], in_=g.d_consts), writes=["cf"], dma_key="cf")
    S.op("dve", lambda e: e.tensor_copy(out=g.cb[:], in_=g.cf[:]), reads=["cf"], writes=["cb"])
    g.ones_bf = g.cb[:, C_ONES, :]
    cT = sb("cT", [128, NCH], F32)
    scT = sb("scT", [128, NCH], F32)
    adab = sb("adab", [128, 48], F32)
    nw = sb("nw", [128, 4, NCH], F32)
    mod = sb("mod", [128, 48], F32)
    S.op("sp", lambda e: e.dma_start(out=cT[:], in_=g.d_cT), writes=["cT"], dma_key="small")
    S.op("sp", lambda e: e.dma_start(out=adab[:], in_=g.d_adab), writes=["adab"], dma_key="small")
    S.op("sp", lambda e: e.dma_start(out=nw[:], in_=g.d_nw), writes=["nw"], dma_key="small")
    _act(S, scT[:], cT[:], AF.Silu, ["cT"], ["scT"])
    S.stage("ada")
    with ExitStack() as es2:
        adaw = [es2.enter_context(nc.sbuf_tensor(f"cm_adaw{i}", [128, NCH, 1024], F32)) for i in range(2)]
        mps = es2.enter_context(nc.psum_tensor("cm_mps", [128, 512], F32))
        for k in range(6):
            aw = adaw[k % 2]
            key = f"adaw{k % 2}"
            for c in range(NCH):
                S.op("sp", lambda e, aw=aw, c=c, k=k: e.dma_start(out=aw[:, c, :], in_=g.d_adaw[:, c, k * 1024:(k + 1) * 1024]),
                     writes=[key], dma_key=key)
            for oc in range(NCH):
                col = k * 8 + oc
                for c in range(NCH):
                    _mm(S, mps[:, col:col + 1], aw[:, c, oc * 128:(oc + 1) * 128], scT[:, c:c + 1], c == 0, c == NCH - 1,
                        [key, "scT"], ["mps"])
        S.stage("none")
        S.op("dve", lambda e: e.tensor_tensor(out=mod[:], in0=mps[:, 0:48], in1=adab[:], op=ALU.add), reads=["mps", "adab"], writes=["mod"])
        S.barrier()
        S.emit()
    g.mod = mod
    g.AB = sb("AB", [128, 6, NCH], F32)
    def mk(idx, scale_k, w_k, plus1):
        if plus1:
            S.op("dve", lambda e: e.scalar_tensor_tensor(out=g.AB[:, idx, :], in0=mod[:, scale_k * 8:scale_k * 8 + 8], scalar=1.0,
                                                         in1=nw[:, w_k, :], op0=ALU.add, op1=ALU.mult), reads=["mod", "nw"], writes=["AB"])
        else:
            S.op("dve", lambda e: e.tensor_tensor(out=g.AB[:, idx, :], in0=mod[:, scale_k * 8:scale_k * 8 + 8], in1=nw[:, w_k, :], op=ALU.mult),
                 reads=["mod", "nw"], writes=["AB"])
    mk(0, 1, 0, True)
    S.op("dve", lambda e: e.tensor_copy(out=g.AB[:, 1, :], in_=mod[:, 0:8]), reads=["mod"], writes=["AB"])
    mk(2, 2, 1, False)
    mk(3, 4, 2, True)
    S.op("dve", lambda e: e.tensor_copy(out=g.AB[:, 4, :], in_=mod[:, 24:32]), reads=["mod"], writes=["AB"])
    mk(5, 5, 3, False)


def emit_norm_block(g, S, xt, xkey, n, hT, hkey, sq, ssps, rstd, ia, ib):
    _act(S, sq[:, :, 0:n], xt[:, :, 0:n], AF.Square, [xkey], ["sq"])
    for c in range(NCH):
        _mm(S, ssps[:, 0:n], g.ones_bf, sq[:, c, 0:n], c == 0, c == NCH - 1, ["cb", "sq"], ["ssps"])
    _rstd(S, rstd[:, 0:n], ssps[:, 0:n], 1.0 / D, ["ssps"], ["rstd"])
    for c in range(NCH):
        S.op("dve", lambda e, c=c: e.tensor_tensor(out=xt[:, c, 0:n], in0=xt[:, c, 0:n], in1=rstd[:, 0:n], op=ALU.mult),
             reads=[xkey, "rstd"], writes=[xkey])
    for c in range(NCH):
        _act(S, hT[:, c, 0:n], xt[:, c, 0:n], AF.Identity, [xkey, "AB"], [hkey], scale=g.AB[:, ia, c:c + 1], bias=g.AB[:, ib, c:c + 1])


def load_w_bf16(S, dst, dkey, src, ncols, nrows=NCH):
    for c in range(nrows):
        for c0 in range(0, ncols, 2048):
            c1 = min(ncols, c0 + 2048)
            S.op("pool", lambda e, c=c, c0=c0, c1=c1: e.dma_start(out=dst[:, c, c0:c1], in_=src[:, c, c0:c1]), writes=[dkey], dma_key=dkey)


def emit_phase1(g, S, nc, SEQ):
    BT = 512
    NB = SEQ // BT
    with ExitStack() as es:
        sb = lambda name, shape, dt: es.enter_context(nc.sbuf_tensor("p1_" + name, shape, dt))
        pst = lambda name, dt=F32, n=512: es.enter_context(nc.psum_tensor("p1_" + name, [128, n], dt))
        cf, cb = g.cf, g.cb
        wfm = sb("wfm", [128, NCH, NFM], BF16)
        wlri = sb("wlri", [128, NCH, 16], BF16)
        wtm = sb("wtm", [128, NCH, NTM], BF16)
        load_w_bf16(S, wfm, "wfm", g.d_wfm, NFM)
        load_w_bf16(S, wlri, "wlri", g.d_wlri, 16)
        load_w_bf16(S, wtm, "wtm", g.d_wtm, NTM)
        wlr = sb("wlr", [16, 128], F32)
        sm = sb("sm", [128, 16], F32)
        cw = sb("cw", [128, 6, 4], F32)
        S.op("sp", lambda e: e.dma_start(out=wlr[:], in_=g.d_wlr), writes=["wlr"], dma_key="small")
        S.op("sp", lambda e: e.dma_start(out=sm[:, 0:8], in_=g.d_sm), writes=["sm"], dma_key="small")
        S.op("sp", lambda e: e.dma_start(out=cw[:], in_=g.d_cw), writes=["cw"], dma_key="small")
        S.op("dve", lambda e: e.tensor_scalar(out=sm[:, 8:9], in0=sm[:, 0:1], scalar1=-1.0, scalar2=None, op0=ALU.mult), reads=["sm"], writes=["sm"])
        _act(S, sm[:, 9:11], sm[:, 4:6], AF.Exp, ["sm"], ["sm"])
        S.op("dve", lambda e: e.tensor_scalar(out=sm[:, 9:11], in0=sm[:, 9:11], scalar1=-1.0, scalar2=None, op0=ALU.mult), reads=["sm"], writes=["sm"])
        xt = [sb(f"xt{i}", [128, NCH, BT], F32) for i in range(2)]
        sq = sb("sq", [128, NCH, BT], BF16)
        rstd = sb("rstd", [128, BT], F32)
        hT = sb("hT", [128, NCH, BT], BF16)
        qaT = sb("qaT", [128, BT], F32)
        kaT = sb("kaT", [128, BT], F32)
        lrT = sb("lrT", [16, BT], F32)
        sagT = sb("sagT", [128, 2, BT], F32)
        sdgT = sb("sdgT", [128, 2, BT], F32)
        pre = [sb(f"pre{i}", [128, 6, BT + 3], F32) for i in range(2)]
        yT = sb("yT", [128, 6, BT], F32)
        sq2 = sb("sq2", [128, BT], BF16)
        rs2 = sb("rs2", [128, BT], F32)
        qnT = sb("qnT", [128, 2, BT], BF16)
        knT = sb("knT", [128, 2, BT], BF16)
        vT = sb("vT", [128, 2, BT], BF16)
        va = sb("va", [128, 4, 256], BF16)
        graw = sb("graw", [128, 4, 4], F32)
        gcol = sb("gcol", [128, 4, 2], F32)
        beta = sb("beta", [128, 4, 2], F32)
        nbeta = sb("nbeta", [128, 4, 2], F32)
        gst = sb("gst", [128, 4, 8], F32)
        egc = sb("egc", [128, 8], F32)
        est = sb("est", [128, 8], F32)
        edec = sb("edec", [128, 2, 8], F32)
        bcoef = sb("bcoef", [128, 8], F32)
        el = sb("el", [128, BT], F32)
        ll = sb("ll", [128, BT], F32)
        bpos = sb("bpos", [128, BT], F32)
        ones_f = cf[:, C_ONES, :]
        Eb = sb("Eb", [128, 128], F32)
        Einv = sb("Einv", [128, 128], F32)
        Est = sb("Est", [128, 128], F32)
        nbl = sb("nbl", [128, 1], F32)
        qdT = sb("qdT", [128, 128], BF16)
        kinvT = sb("kinvT", [128, 128], BF16)
        kstT = sb("kstT", [128, 128], BF16)
        kstk = sb("kstk", [128, 128], BF16)
        attT = sb("attT", [128, 128], BF16)
        Sa = sb("Sa", [128, 256], F32)
        Sab = sb("Sab", [128, 256], BF16)
        sqo = sb("sqo", [128, 2, 128], BF16)
        rso = sb("rso", [128, 128], F32)
        t1 = sb("t1", [128, 128], F32)
        oT = [sb(f"oT{i}", [128, 4, BT], BF16) for i in range(2)]
        kst_d = [sb(f"kst_d{i}", [128, 128], BF16) for i in range(2)]
        kbg = sb("kbg", [128, 128], F32)
        vb = sb("vb", [128, 128], F32)
        gbc = sb("gbc", [128, 128], F32)
        e1 = sb("e1", [128, 128], F32)
        DC = sb("DC", [128, 128], F32)
        DSL = sb("DSL", [128, 128], F32)
        EG = sb("EG", [128, 128], F32)
        P = [sb(f"P{i}", [128, 128], F32) for i in range(2)]
        PT = [sb(f"PT{i}", [128, 128], F32) for i in range(2)]
        X = [sb(f"X{i}", [128, 128], F32) for i in range(2)]
        att_d = sb("att_d", [128, 128], BF16)
        attT_d = sb("attT_d", [128, 128], BF16)
        qd_d = sb("qd_d", [128, 128], BF16)
        u_sb = sb("u_sb", [128, 128], F32)
        wT_sb = sb("wT_sb", [128, 128], BF16)
        vnew = sb("vnew", [128, 128], BF16)
        Sd = [sb(f"Sd{h}", [128, 128], F32) for h in range(2)]
        Sdb = [sb(f"Sdb{h}", [128, 128], BF16) for h in range(2)]
        pp = [pst(f"pp{i}") for i in range(2)]
        pss = pst("pss")
        ptm = pst("ptm")
        pg = pst("pg")
        pi = pst("pi")
        pw = pst("pw")
        ptr = pst("ptr", BF16, 1024)
        for t in (Sa, Sab):
            S.op("dve", lambda e, t=t: e.memset(t[:], 0.0), writes=["Sa", "Sab"])
        for h in range(2):
            S.op("dve", lambda e, h=h: e.memset(Sd[h][:], 0.0), writes=[f"Sd{h}"])
            S.op("dve", lambda e, h=h: e.memset(Sdb[h][:], 0.0), writes=[f"Sdb{h}"])
        for i in range(2):
            S.op("dve", lambda e, i=i: e.memset(pre[i][:, :, 0:3], 0.0), writes=[f"pre{i}"])
            S.op("dve", lambda e, i=i: e.memset(kst_d[i][:], 0.0), writes=["kst_d"])
        S.op("dve", lambda e: e.memset(vnew[:], 0.0), writes=["vnew"])

        for blk in range(NB):
            xs = blk % 2
            xk = f"xt{xs}"
            x_ = xt[xs]
            for c in range(NCH):
                S.op("sp", lambda e, c=c, x_=x_, blk=blk: e.dma_start(out=x_[:, c, :], in_=g.d_xT[:, c, blk * BT:(blk + 1) * BT]),
                     writes=[xk], dma_key=xk)
            S.stage("norm")
            emit_norm_block(g, S, x_, xk, BT, hT, "hT", sq, pss, rstd, 0, 1)
            S.stage("proj")
            pr = pre[blk % 2]
            prk = f"pre{blk % 2}"
            prp = pre[(blk + 1) % 2]
            prpk = f"pre{(blk + 1) % 2}"
            if blk > 0:
                S.op("act", lambda e, pr=pr, prp=prp: e.copy(out=pr[:, :, 0:3], in_=prp[:, :, BT:BT + 3]), reads=[prpk], writes=[prk])
            for j in range(12):
                p_ = pp[j % 2]
                pk = f"pp{j % 2}"
                for c in range(NCH):
                    _mm(S, p_[:, :], wfm[:, c, j * 128:(j + 1) * 128], hT[:, c, :], c == 0, c == NCH - 1, ["wfm", "hT"], [pk])
                if j == 0:
                    S.op("act", lambda e, p_=p_: e.copy(out=qaT[:], in_=p_[:, :]), reads=[pk], writes=["qaT"])
                elif j == 1:
                    S.op("dve", lambda e, p_=p_: e.tensor_copy(out=kaT[:], in_=p_[:, :]), reads=[pk], writes=["kaT"])
                elif j in (2, 3):
                    _act(S, sagT[:, j - 2, :], p_[:, :], AF.Silu, [pk], ["sagT"])
                elif j < 10:
                    S.op("dve", lambda e, p_=p_, j=j, pr=pr: e.tensor_copy(out=pr[:, j - 4, 3:3 + BT], in_=p_[:, :]), reads=[pk], writes=[prk])
                else:
                    _act(S, sdgT[:, j - 10, :], p_[:, :], AF.Silu, [pk], ["sdgT"])
            for c in range(NCH):
                _mm(S, pp[0][0:16, :], wlri[:, c, :], hT[:, c, :], c == 0, c == NCH - 1, ["wlri", "hT"], ["pp0"])
            S.op("dve", lambda e: e.tensor_copy(out=lrT[:], in_=pp[0][0:16, :]), reads=["pp0"], writes=["lrT"])
            S.stage("tm")
            for tl in range(4):
                for c in range(NCH):
                    _mm(S, ptm[:, 0:NTM], hT[:, c, tl * 128:(tl + 1) * 128], wtm[:, c, :], c == 0, c == NCH - 1, ["hT", "wtm"], ["ptm"])
                S.stage("tm_a")
                S.op("act", lambda e, tl=tl: e.copy(out=va[:, tl, :], in_=ptm[:, 0:256]), reads=["ptm"], writes=["va"])
                S.stage("tm_d")
                S.op("dve", lambda e, tl=tl: e.tensor_copy(out=graw[:, tl, :], in_=ptm[:, 256:260]), reads=["ptm"], writes=["graw"])
                S.stage("tm")
            S.stage("gates")
            _act(S, beta[:], graw[:, :, 0:2], AF.Sigmoid, ["graw"], ["beta"])
            S.op("dve", lambda e: e.tensor_scalar(out=nbeta[:], in0=beta[:], scalar1=-1.0, scalar2=None, op0=ALU.mult), reads=["beta"], writes=["nbeta"])
            for h in range(2):
                _act(S, gcol[:, :, h], graw[:, :, 2 + h], AF.Exp, ["graw", "sm"], ["gcol"], bias=sm[:, 6 + h:7 + h])
                _act(S, gcol[:, :, h], gcol[:, :, h], AF.Ln, ["gcol"], ["gcol"], bias=1.0)
                S.op("dve", lambda e, h=h: e.tensor_scalar(out=gcol[:, :, h], in0=gcol[:, :, h], scalar1=sm[:, 9 + h:10 + h], scalar2=None, op0=ALU.mult),
                     reads=["gcol", "sm"], writes=["gcol"])
            gflat = gcol[:].rearrange("p a b -> p (a b)")
            for i, cidx in enumerate((C_TRI, C_BLK, C_SEL0, C_SEL1)):
                _mm(S, pg[:, 384 + 8 * i:392 + 8 * i], cf[:, cidx, :], gflat, True, True, ["cf", "gcol"], ["pg.m"])
            S.op("dve", lambda e: e.tensor_copy(out=gst[:].rearrange("p a b -> p (a b)"), in_=pg[:, 384:416]), reads=["pg.m"], writes=["gst"])
            _act(S, egc[:], gst[:, 0, :], AF.Exp, ["gst"], ["egc"])
            S.op("dve", lambda e: e.tensor_tensor(out=est[:], in0=gst[:, 1, :], in1=gst[:, 0, :], op=ALU.subtract), reads=["gst"], writes=["est"])
            _act(S, est[:], est[:], AF.Exp, ["est"], ["est"])
            _act(S, edec[:], gst[:, 2:4, :], AF.Exp, ["gst"], ["edec"])
            S.op("dve", lambda e: e.tensor_tensor(out=bcoef[:], in0=egc[:], in1=beta[:].rearrange("p a b -> p (a b)"), op=ALU.mult),
                 reads=["egc", "beta"], writes=["bcoef"])
            S.stage("conv")
            for ch in range(6):
                S.op("dve", lambda e, ch=ch, pr=pr: e.tensor_scalar(out=yT[:, ch, :], in0=pr[:, ch, 0:BT], scalar1=cw[:, ch, 0:1], scalar2=None, op0=ALU.mult),
                     reads=[prk, "cw"], writes=["yT"])
                for j in range(1, 4):
                    S.op("dve", lambda e, ch=ch, j=j, pr=pr: e.scalar_tensor_tensor(out=yT[:, ch, :], in0=pr[:, ch, j:j + BT], scalar=cw[:, ch, j:j + 1],
                                                                                 in1=yT[:, ch, :], op0=ALU.mult, op1=ALU.add),
                         reads=[prk, "cw", "yT"], writes=["yT"])
            _act(S, yT[:].rearrange("p a b -> p (a b)"), yT[:].rearrange("p a b -> p (a b)"), AF.Silu, ["yT"], ["yT"])
            for ch in range(4):
                _act(S, sq2[:], yT[:, ch, :], AF.Square, ["yT"], ["sq2"])
                _mm(S, pss[:, :], g.ones_bf, sq2[:], True, True, ["cb", "sq2"], ["pss"])
                _rstd(S, rs2[:], pss[:, :], 1.0, ["pss"], ["rs2"])
                if ch < 2:
                    S.op("dve", lambda e, ch=ch: e.scalar_tensor_tensor(out=qnT[:, ch, :], in0=yT[:, ch, :], scalar=128.0 ** -0.5, in1=rs2[:], op0=ALU.mult, op1=ALU.mult),
                         reads=["yT", "rs2"], writes=["qnT"])
                else:
                    S.op("dve", lambda e, ch=ch: e.tensor_tensor(out=knT[:, ch - 2, :], in0=yT[:, ch, :], in1=rs2[:], op=ALU.mult),
                         reads=["yT", "rs2"], writes=["knT"])
            S.op("act", lambda e: e.copy(out=vT[:], in_=yT[:, 4:6, :]), reads=["yT"], writes=["vT"])
            S.stage("glag")
            _mm(S, pss[:, :], wlr[:], lrT[:], True, True, ["wlr", "lrT"], ["pss"])
            _act(S, el[:], pss[:, :], AF.Exp, ["pss", "sm"], ["el"], scale=-1.0, bias=sm[:, 8:9])
            _act(S, ll[:], el[:], AF.Ln, ["el"], ["ll"], bias=1.0)
            for tl in range(4):
                S.op("dve", lambda e, tl=tl: e.tensor_tensor_scan(out=bpos[:, tl * 128:(tl + 1) * 128], data0=ones_f, data1=ll[:, tl * 128:(tl + 1) * 128],
                                                                  initial=0.0, op0=ALU.mult, op1=ALU.add), reads=["cf", "ll"], writes=["bpos"])
            ob = oT[blk % 2]
            obk = f"oT{blk % 2}"
            for tl in range(4):
                ts = slice(tl * 128, (tl + 1) * 128)
                S.skip = "gla" in SKIP
                _act(S, Eb[:], bpos[:, ts], AF.Exp, ["bpos"], ["Eb"], scale=-1.0 / 16)
                _act(S, Einv[:], bpos[:, ts], AF.Exp, ["bpos"], ["Einv"], scale=1.0 / 16)
                S.op("dve", lambda e, tl=tl: e.tensor_scalar(out=nbl[:], in0=bpos[:, tl * 128 + 127:tl * 128 + 128], scalar1=-1.0 / 16, scalar2=None, op0=ALU.mult),
                     reads=["bpos"], writes=["nbl"])
                _act(S, Est[:], bpos[:, ts], AF.Exp, ["bpos", "nbl"], ["Est"], scale=1.0 / 16, bias=nbl[:, 0:1])
                S.op("dve", lambda e, ts=ts: e.scalar_tensor_tensor(out=qdT[:], in0=qaT[:, ts], scalar=128.0 ** -0.5, in1=Eb[:], op0=ALU.mult, op1=ALU.mult),
                     reads=["qaT", "Eb"], writes=["qdT"])
                S.op("dve", lambda e, ts=ts: e.tensor_tensor(out=kinvT[:], in0=kaT[:, ts], in1=Einv[:], op=ALU.mult), reads=["kaT", "Einv"], writes=["kinvT"])
                S.op("dve", lambda e, ts=ts: e.tensor_tensor(out=kstT[:], in0=kaT[:, ts], in1=Est[:], op=ALU.mult), reads=["kaT", "Est"], writes=["kstT"])
                S.op("pe", lambda e: e.transpose(out=ptr[:, 0:128], in_=kstT[:], identity=cb[:, C_ID, :]), reads=["kstT", "cb"], writes=["ptr0"])
                S.op("act", lambda e: e.copy(out=kstk[:], in_=ptr[:, 0:128]), reads=["ptr0"], writes=["kstk"])
                _mm(S, pg[:, 0:128], kinvT[:], qdT[:], True, True, ["kinvT", "qdT"], ["pg.a"])
                S.op("dve", lambda e: e.tensor_tensor(out=attT[:], in0=pg[:, 0:128], in1=cf[:, C_MU, :], op=ALU.mult), reads=["pg.a", "cf"], writes=["attT"])
                for vc in range(2):
                    _mm(S, pw[:, 384:512], va[:, tl, vc * 128:(vc + 1) * 128], attT[:], True, False, ["va", "attT"], ["pw.o"])
                    _mm(S, pw[:, 384:512], Sab[:, vc * 128:(vc + 1) * 128], qdT[:], False, True, ["Sab", "qdT"], ["pw.o"])
                    _act(S, sqo[:, vc, :], pw[:, 384:512], AF.Square, ["pw.o"], ["sqo"])
                    S.op("dve", lambda e, vc=vc: e.tensor_copy(out=EG[:] if vc == 0 else DSL[:], in_=pw[:, 384:512]), reads=["pw.o"], writes=["EG" if vc == 0 else "DSL"])
                _mm(S, pi[:, 0:256], kstk[:], va[:, tl, :], True, True, ["kstk", "va"], ["pi.a", "pi.b"])
                S.op("dve", lambda e: e.scalar_tensor_tensor(out=Sa[:], in0=Sa[:], scalar=Eb[:, 127:128], in1=pi[:, 0:256], op0=ALU.mult, op1=ALU.add),
                     reads=["Sa", "Eb", "pi.a", "pi.b"], writes=["Sa"])
                S.op("act", lambda e: e.copy(out=Sab[:], in_=Sa[:]), reads=["Sa"], writes=["Sab"])
                for vc in range(2):
                    _mm(S, pss[:, 0:128], g.ones_bf, sqo[:, vc, :], vc == 0, vc == 1, ["cb", "sqo"], ["pss"])
                _rstd(S, rso[:], pss[:, 0:128], 1.0 / 256, ["pss"], ["rso"])
                for vc in range(2):
                    src = EG if vc == 0 else DSL
                    sk = "EG" if vc == 0 else "DSL"
                    S.op("dve", lambda e, src=src: e.tensor_tensor(out=t1[:], in0=src[:], in1=rso[:], op=ALU.mult), reads=[sk, "rso"], writes=["t1"])
                    S.op("dve", lambda e, vc=vc, ts=ts, ob=ob: e.scalar_tensor_tensor(out=ob[:, vc, ts], in0=t1[:], scalar=sm[:, 1 + vc:2 + vc], in1=sagT[:, vc, ts],
                                                                                    op0=ALU.mult, op1=ALU.mult), reads=["t1", "sm", "sagT"], writes=[obk])
                S.skip = "gdn" in SKIP
                for h in range(2):
                    idx = tl * 2 + h
                    S.op("pe", lambda e, h=h, ts=ts: e.transpose(out=ptr[:, 128:256], in_=knT[:, h, ts], identity=cb[:, C_ID, :]), reads=["knT", "cb"], writes=["ptr1"])
                    S.op("act", lambda e, idx=idx: e.activation(out=kst_d[0][0:64, :], in_=ptr[0:64, 128:256], func=AF.Identity, scale=est[0:64, idx:idx + 1]),
                         reads=["ptr1", "est"], writes=["kst_d"])
                    S.op("act", lambda e, idx=idx: e.activation(out=kst_d[1][64:128, :], in_=ptr[64:128, 128:256], func=AF.Identity, scale=est[64:128, idx:idx + 1]),
                         reads=["ptr1", "est"], writes=["kst_d"])
                    S.op("dve", lambda e, idx=idx: e.tensor_scalar(out=kbg[:], in0=ptr[:, 128:256], scalar1=bcoef[:, idx:idx + 1], scalar2=None, op0=ALU.mult),
                         reads=["ptr1", "bcoef"], writes=["kbg"])
                    S.op("pe", lambda e, h=h, ts=ts: e.transpose(out=ptr[:, 256:384], in_=vT[:, h, ts], identity=cb[:, C_ID, :]), reads=["vT", "cb"], writes=["ptr2"])
                    S.op("dve", lambda e, tl=tl, h=h: e.tensor_scalar(out=vb[:], in0=ptr[:, 256:384], scalar1=beta[:, tl, h:h + 1], scalar2=None, op0=ALU.mult),
                         reads=["ptr2", "beta"], writes=["vb"])
                    S.op("dve", lambda e, tl=tl, h=h: e.tensor_scalar(out=gbc[:], in0=ones_f, scalar1=gcol[:, tl, h:h + 1], scalar2=None, op0=ALU.mult),
                         reads=["cf", "gcol"], writes=["gbc"])
                    _mm(S, pg[:, 256:384], gbc[:], cf[:, C_TRI, :], True, True, ["gbc", "cf"], ["pg.g"])
                    S.op("dve", lambda e, idx=idx: e.scalar_tensor_tensor(out=e1[:], in0=pg[:, 256:384], scalar=gst[:, 0, idx:idx + 1], in1=cf[:, C_BIGC, :],
                                                                       op0=ALU.subtract, op1=ALU.add), reads=["pg.g", "gst", "cf"], writes=["e1"])
                    _act(S, DC[:], e1[:], AF.Exp, ["e1"], ["DC"], scale=-1.0)
                    _act(S, EG[:], pg[:, 256:384], AF.Exp, ["pg.g"], ["EG"])
                    S.op("dve", lambda e: e.tensor_tensor(out=DSL[:], in0=DC[:], in1=cf[:, C_SL, :], op=ALU.mult), reads=["DC", "cf"], writes=["DSL"])
                    S.op("dve", lambda e, h=h, ts=ts: e.tensor_tensor(out=qd_d[:], in0=qnT[:, h, ts], in1=EG[:], op=ALU.mult), reads=["qnT", "EG"], writes=["qd_d"])
                    _mm(S, pg[:, 0:128], knT[:, h, ts], knT[:, h, ts], True, True, ["knT"], ["pg.a"])
                    _mm(S, pg[:, 128:256], qnT[:, h, ts], knT[:, h, ts], True, True, ["qnT", "knT"], ["pg.b"])
                    S.op("dve", lambda e, tl=tl, h=h: e.scalar_tensor_tensor(out=P[0][:], in0=pg[:, 0:128], scalar=nbeta[:, tl, h:h + 1], in1=DSL[:],
                                                                          op0=ALU.mult, op1=ALU.mult), reads=["pg.a", "nbeta", "DSL"], writes=["P0"])
                    S.op("dve", lambda e: e.tensor_tensor(out=att_d[:], in0=pg[:, 128:256], in1=DC[:], op=ALU.mult), reads=["pg.b", "DC"], writes=["att_d"])
                    S.op("pe", lambda e: e.transpose(out=pi[:, 128:256], in_=P[0][:], identity=cf[:, C_ID, :]), reads=["P0", "cf"], writes=["pi.b"])
                    S.op("act", lambda e: e.copy(out=PT[0][:], in_=pi[:, 128:256]), reads=["pi.b"], writes=["PT0"])
                    S.op("pe", lambda e: e.transpose(out=ptr[:, 512:640], in_=att_d[:], identity=cb[:, C_ID, :]), reads=["att_d", "cb"], writes=["ptr4"])
                    S.op("act", lambda e: e.copy(out=attT_d[:], in_=ptr[:, 512:640]), reads=["ptr4"], writes=["attT_d"])
                    S.op("dve", lambda e: e.tensor_tensor(out=X[0][:], in0=PT[0][:], in1=cf[:, C_ID, :], op=ALU.add), reads=["PT0", "cf"], writes=["X0"])
                    cur = 0
                    for k in range(5):
                        nxt = 1 - cur
                        _mm(S, pi[:, 0:128], PT[cur][:], P[cur][:], True, True, [f"PT{cur}", f"P{cur}"], ["pi.a"])
                        S.op("act", lambda e, nxt=nxt: e.copy(out=P[nxt][:], in_=pi[:, 0:128]), reads=["pi.a"], writes=[f"P{nxt}"])
                        if k < 4:
                            _mm(S, pi[:, 128:256], P[cur][:], PT[cur][:], True, True, [f"PT{cur}", f"P{cur}"], ["pi.b"])
                            S.op("dve", lambda e, nxt=nxt: e.tensor_copy(out=PT[nxt][:], in_=pi[:, 128:256]), reads=["pi.b"], writes=[f"PT{nxt}"])
                        _mm(S, pi[:, 256:384], P[nxt][:], X[cur][:], True, True, [f"P{nxt}", f"X{cur}"], ["pi.c"])
                        S.op("dve", lambda e, nxt=nxt, cur=cur: e.tensor_tensor(out=X[nxt][:], in0=pi[:, 256:384], in1=X[cur][:], op=ALU.add),
                             reads=["pi.c", f"X{cur}"], writes=[f"X{nxt}"])
                        cur = nxt
                    Xf = X[cur]
                    xk2 = f"X{cur}"
                    _mm(S, pi[:, 384:512], Xf[:], vb[:], True, True, [xk2, "vb"], ["pi.u"])
                    S.op("act", lambda e: e.copy(out=u_sb[:], in_=pi[:, 384:512]), reads=["pi.u"], writes=["u_sb"])
                    _mm(S, pw[:, 0:128], kbg[:], Xf[:], True, True, [xk2, "kbg"], ["pw.w"])
                    S.op("act", lambda e: e.copy(out=wT_sb[:], in_=pw[:, 0:128]), reads=["pw.w"], writes=["wT_sb"])
                    sdk, sdbk = f"Sd{h}", f"Sdb{h}"
                    for hf in range(2):
                        rs_ = slice(hf * 64, hf * 64 + 64)
                        _mm(S, pw[rs_, 128:256], wT_sb[:, rs_], Sdb[h][:], True, True, ["wT_sb", sdbk], ["pw.s"])
                        S.op("dve", lambda e, rs_=rs_: e.tensor_tensor(out=vnew[rs_, :], in0=u_sb[rs_, :], in1=pw[rs_, 128:256], op=ALU.subtract),
                             reads=["u_sb", "pw.s"], writes=["vnew"])
                        _mm(S, pw[:, 384 + hf * 64:448 + hf * 64], Sdb[h][:], qd_d[:, rs_], True, False, [sdbk, "qd_d"], ["pw.o"])
                        _mm(S, pw[:, 384 + hf * 64:448 + hf * 64], vnew[:, :], attT_d[:, rs_], False, True, ["vnew", "attT_d"], ["pw.o"])
                        _mm(S, pw[:, 256:384], kst_d[hf][:, :], vnew[:, :], True, True, ["kst_d", "vnew"], ["pw.d"])
                        S.op("dve", lambda e, h=h, hf=hf, idx=idx: e.scalar_tensor_tensor(out=Sd[h][:], in0=Sd[h][:], scalar=edec[:, hf, idx:idx + 1], in1=pw[:, 256:384],
                                                                                       op0=ALU.mult, op1=ALU.add), reads=[sdk, "edec", "pw.d"], writes=[sdk])
                        S.op("act", lambda e, h=h: e.copy(out=Sdb[h][:], in_=Sd[h][:]), reads=[sdk], writes=[sdbk])
                    _act(S, sqo[:, 0, :], pw[:, 384:512], AF.Square, ["pw.o"], ["sqo"])
                    _mm(S, pss[:, 0:128], g.ones_bf, sqo[:, 0, :], True, True, ["cb", "sqo"], ["pss"])
                    _rstd(S, rso[:], pss[:, 0:128], 1.0 / 128, ["pss"], ["rso"])
                    S.op("dve", lambda e: e.tensor_tensor(out=t1[:], in0=pw[:, 384:512], in1=rso[:], op=ALU.mult), reads=["pw.o", "rso"], writes=["t1"])
                    S.op("dve", lambda e, h=h, ts=ts, ob=ob: e.scalar_tensor_tensor(out=ob[:, 2 + h, ts], in0=t1[:], scalar=sm[:, 3:4], in1=sdgT[:, h, ts],
                                                                                  op0=ALU.mult, op1=ALU.mult), reads=["t1", "sm", "sdgT"], writes=[obk])
            S.skip = False
            for j in range(4):
                S.op("sp", lambda e, j=j, ob=ob, blk=blk: e.dma_start(out=g.d_oT[:, j, blk * BT:(blk + 1) * BT], in_=ob[:, j, :]), reads=[obk], dma_key=obk + "s")
        S.barrier()
        S.emit()


def emit_phase2(g, S, nc, SQ, o2_src):
    BT = 256
    NB = SQ // BT
    with ExitStack() as es:
        sb = lambda name, shape, dt: es.enter_context(nc.sbuf_tensor("p2a_" + name, shape, dt))
        pst = lambda name, dt=F32, n=512: es.enter_context(nc.psum_tensor("p2a_" + name, [128, n], dt))
        wm = sb("wm", [128, NCH, 2048], BF16)
        wbg = sb("wbg", [128, NCH, 1024], BF16)
        wbd = sb("wbd", [128, NCH, 1024], BF16)
        wo = sb("wo", [128, NCH, 1024], BF16)
        load_w_bf16(S, wm, "wm", g.d_wm, 2048)
        load_w_bf16(S, wbg, "wbg", g.d_wbg, 1024)
        load_w_bf16(S, wbd, "wbd", g.d_wbd, 1024)
        load_w_bf16(S, wo, "wo", g.d_wo, 1024)
        xt = [sb(f"x2t{i}", [128, NCH, BT], F32) for i in range(2)]
        xr = [sb(f"x2r{i}", [128, NCH, BT], F32) for i in range(2)]
        sq = sb("sq", [128, NCH, BT], BF16)
        rstd = sb("rstd", [128, BT], F32)
        hT = sb("hT", [128, NCH, BT], BF16)
        sg = sb("sg", [128, 16, BT], F32)
        o2 = [sb(f"o2{i}", [128, 16, BT], BF16) for i in range(2)]
        mg = sb("mg", [128, NCH, BT], BF16)
        tmp = sb("tmp", [128, BT], F32)
        zz = sb("zz", [128, NCH, BT], F32)
        pp = [pst(f"pp{i}") for i in range(2)]
        pss = pst("pss")
        for blk in range(NB):
            s_ = blk % 2
            x_, xk = xt[s_], f"x2t{s_}"
            xr_, xrk = xr[s_], f"x2r{s_}"
            o_, ok = o2[s_], f"o2{s_}"
            t0 = blk * BT
            for c in range(NCH):
                S.op("sp", lambda e, c=c, x_=x_, t0=t0: e.dma_start(out=x_[:, c, :], in_=g.d_xT2[:, c, t0:t0 + BT]), writes=[xk], dma_key=xk)
                S.op("sp", lambda e, c=c, xr_=xr_, t0=t0: e.dma_start(out=xr_[:, c, :], in_=g.d_xT2[:, c, t0:t0 + BT]), writes=[xrk], dma_key=xrk)
            for kind in range(2):
                for c in range(NCH):
                    S.op("sp", lambda e, kind=kind, c=c, o_=o_, t0=t0: e.dma_start(out=o_[:, kind * 8 + c, :], in_=o2_src(kind, c, t0, BT)), writes=[ok], dma_key=ok)
            emit_norm_block(g, S, x_, xk, BT, hT, "hT", sq, pss, rstd, 0, 1)
            for j in range(16):
                p_, pk = pp[j % 2], f"pp{j % 2}"
                for c in range(NCH):
                    _mm(S, p_[:, 0:BT], wm[:, c, j * 128:(j + 1) * 128], hT[:, c, :], c == 0, c == NCH - 1, ["wm", "hT"], [pk])
                _act(S, sg[:, j, :], p_[:, 0:BT], AF.Sigmoid, [pk], ["sg"])
            for oc in range(NCH):
                for kind, w_ in ((0, wbg), (1, wbd)):
                    p_, pk = pp[kind], f"pp{kind}"
                    wk = "wbg" if kind == 0 else "wbd"
                    for c in range(NCH):
                        _mm(S, p_[:, 0:BT], w_[:, c, oc * 128:(oc + 1) * 128], o_[:, kind * 8 + c, :], c == 0, c == NCH - 1, [wk, ok], [pk])
                S.op("dve", lambda e, oc=oc: e.tensor_tensor(out=tmp[:], in0=pp[0][:, 0:BT], in1=sg[:, oc, :], op=ALU.mult), reads=["pp0", "sg"], writes=["tmp"])
                S.op("dve", lambda e, oc=oc: e.tensor_tensor(out=zz[:, oc, :], in0=pp[1][:, 0:BT], in1=sg[:, 8 + oc, :], op=ALU.mult), reads=["pp1", "sg"], writes=["zz"])
                S.op("dve", lambda e, oc=oc: e.tensor_tensor(out=mg[:, oc, :], in0=tmp[:], in1=zz[:, oc, :], op=ALU.add), reads=["tmp", "zz"], writes=["mg"])
            for oc in range(NCH):
                p_, pk = pp[oc % 2], f"pp{oc % 2}"
                for c in range(NCH):
                    _mm(S, p_[:, 0:BT], wo[:, c, oc * 128:(oc + 1) * 128], mg[:, c, :], c == 0, c == NCH - 1, ["wo", "mg"], [pk])
                S.op("dve", lambda e, oc=oc, p_=p_: e.tensor_copy(out=zz[:, oc, :], in_=p_[:, 0:BT]), reads=[pk], writes=["zz"])
            _act(S, sq[:], zz[:], AF.Square, ["zz"], ["sq"])
            for c in range(NCH):
                _mm(S, pss[:, 0:BT], g.ones_bf, sq[:, c, :], c == 0, c == NCH - 1, ["cb", "sq"], ["pss"])
            _rstd(S, rstd[:], pss[:, 0:BT], 1.0 / D, ["pss"], ["rstd"])
            for c in range(NCH):
                S.op("dve", lambda e, c=c: e.tensor_tensor(out=zz[:, c, :], in0=zz[:, c, :], in1=rstd[:], op=ALU.mult), reads=["zz", "rstd"], writes=["zz"])
                S.op("dve", lambda e, c=c, xr_=xr_: e.scalar_tensor_tensor(out=xr_[:, c, :], in0=zz[:, c, :], scalar=g.AB[:, 2, c:c + 1], in1=xr_[:, c, :],
                                                                         op0=ALU.mult, op1=ALU.add), reads=["zz", "AB", xrk], writes=[xrk])
                S.op("sp", lambda e, c=c, xr_=xr_, t0=t0: e.dma_start(out=g.d_x1[:, c, t0:t0 + BT], in_=xr_[:, c, :]), reads=[xrk], writes=["x1d"], dma_key=xrk + "s")
        S.barrier()
        S.emit()
    with ExitStack() as es:
        sb = lambda name, shape, dt: es.enter_context(nc.sbuf_tensor("p2b_" + name, shape, dt))
        pst = lambda name, dt=F32, n=512: es.enter_context(nc.psum_tensor("p2b_" + name, [128, n], dt))
        w1 = sb("w1", [128, NCH, 4096], BF16)
        w2 = sb("w2", [128, 32, 1024], BF16)
        load_w_bf16(S, w1, "w1", g.d_w1, 4096)
        load_w_bf16(S, w2, "w2", g.d_w2, 1024, nrows=32)
        xt = [sb(f"x3t{i}", [128, NCH, BT], F32) for i in range(1)]
        xr = [sb(f"x3r{i}", [128, NCH, BT], F32) for i in range(1)]
        sq = sb("sq", [128, NCH, BT], BF16)
        rstd = sb("rstd", [128, BT], F32)
        hT = sb("hT", [128, NCH, BT], BF16)
        rl = sb("rl", [128, BT], F32)
        hid = sb("hid", [128, 32, BT], BF16)
        zz = sb("zz", [128, NCH, BT], F32)
        pp = [pst(f"pp{i}") for i in range(2)]
        pss = pst("pss")
        for blk in range(NB):
            s_ = 0
            x_, xk = xt[s_], f"x3t{s_}"
            xr_, xrk = xr[s_], f"x3r{s_}"
            t0 = blk * BT
            for c in range(NCH):
                S.op("sp", lambda e, c=c, x_=x_, t0=t0: e.dma_start(out=x_[:, c, :], in_=g.d_x1[:, c, t0:t0 + BT]), reads=["x1d"], writes=[xk], dma_key=xk)
                S.op("sp", lambda e, c=c, xr_=xr_, t0=t0: e.dma_start(out=xr_[:, c, :], in_=g.d_x1[:, c, t0:t0 + BT]), reads=["x1d"], writes=[xrk], dma_key=xrk)
            emit_norm_block(g, S, x_, xk, BT, hT, "hT", sq, pss, rstd, 3, 4)
            for fc in range(32):
                p_, pk = pp[fc % 2], f"pp{fc % 2}"
                for c in range(NCH):
                    _mm(S, p_[:, 0:BT], w1[:, c, fc * 128:(fc + 1) * 128], hT[:, c, :], c == 0, c == NCH - 1, ["w1", "hT"], [pk])
                _act(S, rl[:], p_[:, 0:BT], AF.Relu, [pk], ["rl"])
                S.op("dve", lambda e, fc=fc: e.tensor_tensor(out=hid[:, fc, :], in0=rl[:], in1=rl[:], op=ALU.mult), reads=["rl"], writes=["hid"])
            for oc in range(NCH):
                p_, pk = pp[oc % 2], f"pp{oc % 2}"
                for fc in range(32):
                    _mm(S, p_[:, 0:BT], w2[:, fc, oc * 128:(oc + 1) * 128], hid[:, fc, :], fc == 0, fc == 31, ["w2", "hid"], [pk])
                S.op("dve", lambda e, oc=oc, p_=p_: e.tensor_copy(out=zz[:, oc, :], in_=p_[:, 0:BT]), reads=[pk], writes=["zz"])
            _act(S, sq[:], zz[:], AF.Square, ["zz"], ["sq"])
            for c in range(NCH):
                _mm(S, pss[:, 0:BT], g.ones_bf, sq[:, c, :], c == 0, c == NCH - 1, ["cb", "sq"], ["pss"])
            _rstd(S, rstd[:], pss[:, 0:BT], 1.0 / D, ["pss"], ["rstd"])
            for c in range(NCH):
                S.op("dve", lambda e, c=c: e.tensor_tensor(out=zz[:, c, :], in0=zz[:, c, :], in1=rstd[:], op=ALU.mult), reads=["zz", "rstd"], writes=["zz"])
                S.op("dve", lambda e, c=c, xr_=xr_: e.scalar_tensor_tensor(out=xr_[:, c, :], in0=zz[:, c, :], scalar=g.AB[:, 5, c:c + 1], in1=xr_[:, c, :],
                                                                         op0=ALU.mult, op1=ALU.add), reads=["zz", "AB", xrk], writes=[xrk])
                S.op("sp", lambda e, c=c, xr_=xr_, t0=t0: e.dma_start(out=g.d_out[:, c, t0:t0 + BT], in_=xr_[:, c, :]), reads=[xrk], dma_key=xrk + "s")
        S.barrier()
        S.emit()


def build_program(SEQ, mode):
    SQ = SEQ // 4
    nc = bass.Bass("TRN2", target_bir_lowering=False)
    g = Ctx()
    din = lambda name, shape, dt=F32: nc.dram_tensor(name, shape, dt, kind="ExternalInput").ap()
    g.d_consts = din("consts", [128, NCONST, 128])
    g.d_cT = din("cT", [128, NCH])
    g.d_adab = din("adab", [128, 48])
    g.d_nw = din("nw", [128, 4, NCH])
    g.d_adaw = din("adaw", [128, NCH, 6144])
    if mode in ("A", "F"):
        g.d_xT = din("xT", [128, NCH, SEQ])
        g.d_wfm = din("wfm", [128, NCH, NFM])
        g.d_wlri = din("wlri", [128, NCH, 16])
        g.d_wtm = din("wtm", [128, NCH, NTM])
        g.d_wlr = din("wlr", [16, 128])
        g.d_sm = din("sm", [128, 8])
        g.d_cw = din("cw", [128, 6, 4])
    if mode == "A":
        g.d_oT = nc.dram_tensor("oT", [128, 4, SEQ], BF16, kind="ExternalOutput").ap()
    if mode == "F":
        g.d_oT = nc.dram_tensor("oT", [128, 4, SEQ], BF16, kind="Internal").ap()
        g.d_gath = nc.dram_tensor("gath", [4 * 128, 4, SEQ], BF16, kind="Internal").ap()
    if mode in ("B", "F"):
        g.d_xT2 = din("xT2", [128, NCH, SQ])
        g.d_wm = din("wm", [128, NCH, 2048])
        g.d_wbg = din("wbg", [128, NCH, 1024])
        g.d_wbd = din("wbd", [128, NCH, 1024])
        g.d_wo = din("wo", [128, NCH, 1024])
        g.d_w1 = din("w1", [128, NCH, 4096])
        g.d_w2 = din("w2", [128, 32, 1024])
        g.d_x1 = nc.dram_tensor("x1", [128, NCH, SQ], F32, kind="Internal").ap()
        g.d_out = nc.dram_tensor("out", [128, NCH, SQ], F32, kind="ExternalOutput").ap()
    if mode == "B":
        g.d_o2 = din("o2", [128, 2, NCH, SQ], BF16)
    with ExitStack() as es:
        S = Sched(nc, es)
        emit_common(g, S, es, nc, True)
        if mode in ("A", "F"):
            emit_phase1(g, S, nc, SEQ)
        if mode == "F":
            g.tq = None
            raise NotImplementedError
        if mode == "B":
            emit_phase2(g, S, nc, SQ, lambda kind, c, t0, n: g.d_o2[:, kind, c, t0:t0 + n])
    return nc


def _fm(w):
    r, n = w.shape
    return np.ascontiguousarray(w.reshape(r // 128, 128, n).transpose(1, 0, 2))


def _consts():
    p = np.arange(128)[:, None]
    f = np.arange(128)[None, :]
    same = (p // 64) == (f // 64)
    c = np.zeros((NCONST, 128, 128), np.float32)
    c[C_ID] = (p == f)
    c[C_MU] = (p <= f)
    c[C_SL] = same & (f < p)
    c[C_BIGC] = np.where(same & (f <= p), 0.0, BIG)
    c[C_TRI] = same & (p <= f)
    c[C_BLK] = same
    c[C_SEL0] = (p < 64) & (f >= 0)
    c[C_SEL1] = (p >= 64) & (f >= 0)
    c[C_ONES] = 1.0
    return np.ascontiguousarray(c.transpose(1, 0, 2))


def host_inputs(inp, SEQ):
    x = inp["x"]
    w_in = inp["w_in"][0]
    maps1, maps2 = [], []
    consts = _consts()
    adaw = _fm(inp["ada_w"][0])
    nw = np.stack([inp[k][0].reshape(8, 128).T for k in ("pre_mix_w", "post_mix_w", "pre_mlp_w", "post_mlp_w")], axis=1)
    adab = np.ascontiguousarray(inp["ada_b"][0].reshape(48, 128).T)
    offs = np.cumsum([0, 512, 512, 1024, 1024, 16, 3072, 1024, 8, 8, 1024, 1024])
    o_aq, o_ak, o_av, o_ag, o_lr, o_dqkv, o_dg, o_db, o_da, o_ma, o_md = offs[:11]
    SQ = SEQ // 4
    xTs = [np.ascontiguousarray(x[b, :SEQ].T.reshape(8, 128, SEQ).transpose(1, 0, 2)) for b in range(x.shape[0])]
    for r in range(8):
        b, hg = r // 4, r % 4
        common = {"consts": consts, "cT": np.ascontiguousarray(inp["c"][b].reshape(8, 128).T), "adab": adab,
                  "nw": np.ascontiguousarray(nw), "adaw": adaw}
        cols = []
        cols += list(range(o_aq + hg * 128, o_aq + (hg + 1) * 128))
        cols += list(range(o_ak + hg * 128, o_ak + (hg + 1) * 128))
        cols += list(range(o_ag + hg * 256, o_ag + (hg + 1) * 256))
        for part in range(3):
            cols += list(range(o_dqkv + part * 1024 + hg * 256, o_dqkv + part * 1024 + (hg + 1) * 256))
        cols += list(range(o_dg + hg * 256, o_dg + (hg + 1) * 256))
        tcols = list(range(o_av + hg * 256, o_av + (hg + 1) * 256)) + [o_db + 2 * hg, o_db + 2 * hg + 1, o_da + 2 * hg, o_da + 2 * hg + 1]
        sm = np.zeros((128, 8), np.float32)
        sm[:, 0] = inp["gla_b_lr"][0][hg * 128:(hg + 1) * 128]
        sm[:, 1:3] = inp["gla_onorm_w"][0].reshape(2, 128).T
        sm[:, 3] = inp["gdn_onorm_w"][0]
        sm[:, 4:6] = inp["gdn_a_log"][0][None, 2 * hg:2 * hg + 2]
        sm[:, 6:8] = inp["gdn_dt_bias"][0][None, 2 * hg:2 * hg + 2]
        cwfull = inp["gdn_conv_w"][0]
        cw = np.zeros((128, 6, 4), np.float32)
        for part in range(3):
            for h in range(2):
                c0 = part * 1024 + (2 * hg + h) * 128
                cw[:, part * 2 + h, :] = cwfull[:, c0:c0 + 128].T
        m1 = dict(common)
        m1.update({"xT": xTs[b], "wfm": _fm(w_in[:, cols]), "wlri": _fm(w_in[:, o_lr:o_lr + 16]), "wtm": _fm(np.concatenate([w_in[:, tcols], np.zeros((D, NTM - 260), np.float32)], axis=1)),
                   "wlr": np.ascontiguousarray(inp["gla_w_lr"][0][:, hg * 128:(hg + 1) * 128]), "sm": sm, "cw": cw})
        maps1.append(m1)
        m2 = dict(common)
        m2.update({"xT2": np.ascontiguousarray(xTs[b][:, :, hg * SQ:(hg + 1) * SQ]),
                   "wm": _fm(w_in[:, o_ma:o_ma + 2048]), "wbg": _fm(inp["w_branch_gla"][0]), "wbd": _fm(inp["w_branch_gdn"][0]),
                   "wo": _fm(inp["w_out"][0]), "w1": _fm(inp["mlp_w1"][0]), "w2": _fm(inp["mlp_w2"][0])})
        maps2.append(m2)
    return maps1, maps2


_CACHE = {}


def _prog(SEQ, mode):
    if (SEQ, mode) not in _CACHE:
        _CACHE[(SEQ, mode)] = build_program(SEQ, mode)
    return _CACHE[(SEQ, mode)]


def run_unfused(inp, SEQ):
    maps1, maps2 = host_inputs(inp, SEQ)
    SQ = SEQ // 4
    resA = run_bass_kernel_spmd(_prog(SEQ, "A"), maps1, core_ids=list(range(8)))
    oTs = [np.asarray(r["oT"]) for r in resA.results]
    for r in range(8):
        b, tq = r // 4, r % 4
        o2 = np.zeros((128, 2, 8, SQ), oTs[0].dtype)
        for hg in range(4):
            src = oTs[b * 4 + hg][:, :, tq * SQ:(tq + 1) * SQ]
            o2[:, 0, hg * 2:hg * 2 + 2] = src[:, 0:2]
            o2[:, 1, hg * 2:hg * 2 + 2] = src[:, 2:4]
        maps2[r]["o2"] = o2
    resB = run_bass_kernel_spmd(_prog(SEQ, "B"), maps2, core_ids=list(range(8)))
    B = inp["x"].shape[0]
    out = np.zeros((B, SEQ, D), np.float32)
    for r in range(8):
        b, tq = r // 4, r % 4
        o = np.asarray(resB.results[r]["out"])
        out[b, tq * SQ:(tq + 1) * SQ, :] = o.transpose(2, 1, 0).reshape(SQ, D)
    return out, oTs


def kernel(**inputs):
    inp = {k: np.asarray(v) for k, v in inputs.items()}
    out, _ = run_unfused(inp, inp["x"].shape[1])
    return out
```

```python
import numpy as np
from contextlib import ExitStack
import concourse.bass as bass
import concourse.mybir as mybir
from concourse.bass_utils import run_bass_kernel_spmd

F32 = mybir.dt.float32
BF16 = mybir.dt.bfloat16
AF = mybir.ActivationFunctionType
ALU = mybir.AluOpType

D = 1024
NCH = 8
EPS = 1e-6
NFM = 1536
NTM = 288
BIG = 30000.0
C_ID, C_MU, C_SL, C_BIGC, C_TRI, C_BLK, C_SEL0, C_SEL1, C_ONES = range(9)
NCONST = 9


import os
SKIP = set(os.environ.get("KSKIP", "").split(","))


SAME_ENG_SYNC = os.environ.get("KSAME", "1") == "1"


class Sched:
    ENGS = ("pe", "act", "dve", "pool", "sp")

    def __init__(self, nc, es):
        self.nc = nc
        self.es = es
        self.sem = {e: es.enter_context(nc.semaphore("s_" + e)) for e in self.ENGS}
        self.cnt = {e: 0 for e in self.ENGS}
        self.prog = {e: [] for e in self.ENGS}
        self.waited = {e: {} for e in self.ENGS}
        self.lastw = {}
        self.readers = {}
        self.dsem = {}
        self.dcnt = {}

    def _need(self, eng, waits, tok):
        if tok is None:
            return
        kind, key, val = tok
        if kind == "e" and key == eng and (eng == "pe" or not SAME_ENG_SYNC):
            return
        w = self.waited[eng]
        if w.get((kind, key), 0) >= val:
            return
        w[(kind, key)] = val
        waits[(kind, key)] = max(waits.get((kind, key), 0), val)

    skip = False

    def stage(self, name):
        self.skip = name in SKIP

    capture = None

    def op(self, eng, fn, reads=(), writes=(), dma_key=None, dma_inc=16):
        if self.skip:
            return None
        if self.capture is not None:
            self.capture.append((eng, fn, tuple(reads), tuple(writes), dma_key, dma_inc))
            return None
        banks = set()
        for k in list(reads) + list(writes):
            for pfx in ("pp0", "pp1", "pss", "ptm", "pg", "pi", "pw", "ptr", "mps"):
                if k == pfx or k.startswith(pfx + ".") or (pfx == "ptr" and k.startswith("ptr")):
                    banks.add("B:" + pfx)
        writes = list(writes) + sorted(banks)
        waits = {}
        for k in reads:
            self._need(eng, waits, self.lastw.get(k))
        for k in writes:
            self._need(eng, waits, self.lastw.get(k))
            for t in self.readers.get(k, ()):
                self._need(eng, waits, t)
        if dma_key is None:
            self.cnt[eng] += 1
            tok = ("e", eng, self.cnt[eng])
        else:
            if dma_key not in self.dsem:
                self.dsem[dma_key] = self.es.enter_context(self.nc.semaphore("d_" + str(dma_key)))
                self.dcnt[dma_key] = 0
            self.dcnt[dma_key] += dma_inc
            tok = ("d", dma_key, self.dcnt[dma_key])
        self.prog[eng].append((waits, fn, tok, dma_inc))
        for k in reads:
            self.readers.setdefault(k, []).append(tok)
        for k in writes:
            self.lastw[k] = tok
            self.readers[k] = []
        return tok

    def replay_interleaved(self, streams):
        pos = [0] * len(streams)
        total = max(len(st) for st in streams)
        for step in range(total):
            for i, st in enumerate(streams):
                upto = (step + 1) * len(st) // total
                while pos[i] < upto:
                    self.op(*st[pos[i]])
                    pos[i] += 1

    def barrier(self, engs=None):
        toks = [("e", e, self.cnt[e]) for e in self.ENGS if self.cnt[e] > 0]
        toks += [("d", k, v) for k, v in self.dcnt.items()]
        for e in (engs or self.ENGS):
            waits = {}
            for t in toks:
                self._need(e, waits, t)
            if waits:
                self.prog[e].append((waits, None, None, 0))

    nblocks = 0

    def emit(self):
        self.nblocks += 1
        nc = self.nc
        prog = self.prog
        self.prog = {e: [] for e in self.ENGS}
        with nc.Block() as block:
            def body(e):
                def f(engine):
                    for waits, fn, tok, dinc in prog[e]:
                        for (kind, key), val in waits.items():
                            s = self.sem[key] if kind == "e" else self.dsem[key]
                            engine.wait_ge(s, val)
                        if fn is None:
                            continue
                        ins = fn(engine)
                        if tok[0] == "e":
                            ins.then_inc(self.sem[tok[1]], 1)
                        else:
                            ins.then_inc(self.dsem[tok[1]], dinc)
                return f
            block.tensor(body("pe"))
            block.scalar(body("act"))
            block.vector(body("dve"))
            block.gpsimd(body("pool"))
            block.sync(body("sp"))


class Ctx:
    pass


def _mm(S, out, lhsT, rhs, start, stop, reads, writes):
    S.op("pe", lambda e: e.matmul(out, lhsT=lhsT, rhs=rhs, start=start, stop=stop), reads=reads, writes=writes)


def _act(S, out, in_, func, reads, writes, scale=None, bias=None):
    kw = {}
    if scale is not None:
        kw["scale"] = scale
    if bias is not None:
        kw["bias"] = bias
    S.op("act", lambda e: e.activation(out=out, in_=in_, func=func, **kw), reads=reads, writes=writes)


def _rstd(S, out, ss, inv_n, reads, writes):
    _act(S, out, ss, AF.Ln, reads, writes, scale=inv_n, bias=EPS)
    _act(S, out, out, AF.Exp, writes, writes, scale=-0.5)


def emit_common(g, S, es, nc, need_mix):
    sb = lambda name, shape, dt: es.enter_context(nc.sbuf_tensor("cm_" + name, shape, dt))
    g.cf = sb("cf", [128, NCONST, 128], F32)
    g.cb = sb("cb", [128, NCONST, 128], BF16)
    S.op("sp", lambda e: e.dma_start(out=g.cf[:], in_=g.d_consts), writes=["cf"], dma_key="cf")
    S.op("dve", lambda e: e.tensor_copy(out=g.cb[:], in_=g.cf[:]), reads=["cf"], writes=["cb"])
    g.ones_bf = g.cb[:, C_ONES, :]
    cT = sb("cT", [128, NCH], F32)
    scT = sb("scT", [128, NCH], F32)
    adab = sb("adab", [128, 48], F32)
    nw = sb("nw", [128, 4, NCH], F32)
    mod = sb("mod", [128, 48], F32)
    S.op("sp", lambda e: e.dma_start(out=cT[:], in_=g.d_cT), writes=["cT"], dma_key="s_cT")
    S.op("sp", lambda e: e.dma_start(out=adab[:], in_=g.d_adab), writes=["adab"], dma_key="s_adab")
    S.op("sp", lambda e: e.dma_start(out=nw[:], in_=g.d_nw), writes=["nw"], dma_key="s_nw")
    _act(S, scT[:], cT[:], AF.Silu, ["cT"], ["scT"])
    S.stage("ada")
    with ExitStack() as es2:
        adaw = [es2.enter_context(nc.sbuf_tensor(f"cm_adaw{i}", [128, NCH, 1024], F32)) for i in range(2)]
        mps = es2.enter_context(nc.psum_tensor("cm_mps", [128, 512], F32))
        for k in range(6):
            aw = adaw[k % 2]
            key = f"adaw{k % 2}"
            for c in range(NCH):
                S.op("sp", lambda e, aw=aw, c=c, k=k: e.dma_start(out=aw[:, c, :], in_=g.d_adaw[:, c, k * 1024:(k + 1) * 1024]),
                     writes=[key], dma_key=key)
            for oc in range(NCH):
                col = k * 8 + oc
                for c in range(NCH):
                    _mm(S, mps[:, col:col + 1], aw[:, c, oc * 128:(oc + 1) * 128], scT[:, c:c + 1], c == 0, c == NCH - 1,
                        [key, "scT"], ["mps"])
        S.stage("none")
        S.op("dve", lambda e: e.tensor_tensor(out=mod[:], in0=mps[:, 0:48], in1=adab[:], op=ALU.add), reads=["mps", "adab"], writes=["mod"])
        S.barrier()
        S.emit()
    g.mod = mod
    g.AB = sb("AB", [128, 6, NCH], F32)
    def mk(idx, scale_k, w_k, plus1):
        if plus1:
            S.op("dve", lambda e: e.scalar_tensor_tensor(out=g.AB[:, idx, :], in0=mod[:, scale_k * 8:scale_k * 8 + 8], scalar=1.0,
                                                         in1=nw[:, w_k, :], op0=ALU.add, op1=ALU.mult), reads=["mod", "nw"], writes=["AB"])
        else:
            S.op("dve", lambda e: e.tensor_tensor(out=g.AB[:, idx, :], in0=mod[:, scale_k * 8:scale_k * 8 + 8], in1=nw[:, w_k, :], op=ALU.mult),
                 reads=["mod", "nw"], writes=["AB"])
    mk(0, 1, 0, True)
    S.op("dve", lambda e: e.tensor_copy(out=g.AB[:, 1, :], in_=mod[:, 0:8]), reads=["mod"], writes=["AB"])
    mk(2, 2, 1, False)
    mk(3, 4, 2, True)
    S.op("dve", lambda e: e.tensor_copy(out=g.AB[:, 4, :], in_=mod[:, 24:32]), reads=["mod"], writes=["AB"])
    mk(5, 5, 3, False)


def emit_norm_block(g, S, xt, xkey, n, hT, hkey, sq, ssps, rstd, ia, ib):
    _act(S, sq[:, :, 0:n], xt[:, :, 0:n], AF.Square, [xkey], ["sq"])
    for c in range(NCH):
        _mm(S, ssps[:, 0:n], g.ones_bf, sq[:, c, 0:n], c == 0, c == NCH - 1, ["cb", "sq"], ["ssps"])
    _rstd(S, rstd[:, 0:n], ssps[:, 0:n], 1.0 / D, ["ssps"], ["rstd"])
    for c in range(NCH):
        S.op("dve", lambda e, c=c: e.tensor_tensor(out=xt[:, c, 0:n], in0=xt[:, c, 0:n], in1=rstd[:, 0:n], op=ALU.mult),
             reads=[xkey, "rstd"], writes=[xkey])
    for c in range(NCH):
        _act(S, hT[:, c, 0:n], xt[:, c, 0:n], AF.Identity, [xkey, "AB"], [hkey], scale=g.AB[:, ia, c:c + 1], bias=g.AB[:, ib, c:c + 1])


def load_w_bf16(S, dst, dkey, src, ncols, nrows=NCH):
    for c in range(nrows):
        for c0 in range(0, ncols, 2048):
            c1 = min(ncols, c0 + 2048)
            S.op("pool", lambda e, c=c, c0=c0, c1=c1: e.dma_start(out=dst[:, c, c0:c1], in_=src[:, c, c0:c1]), writes=[dkey], dma_key=dkey)


def emit_phase1(g, S, nc, SEQ):
    BT = 512
    NB = SEQ // BT
    with ExitStack() as es:
        sb = lambda name, shape, dt: es.enter_context(nc.sbuf_tensor("p1_" + name, shape, dt))
        pst = lambda name, dt=F32, n=512: es.enter_context(nc.psum_tensor("p1_" + name, [128, n], dt))
        cf, cb = g.cf, g.cb
        wfm = sb("wfm", [128, NCH, NFM], BF16)
        wlri = sb("wlri", [128, NCH, 16], BF16)
        wtm = sb("wtm", [128, NCH, NTM], BF16)
        load_w_bf16(S, wfm, "wfm", g.d_wfm, NFM)
        load_w_bf16(S, wlri, "wlri", g.d_wlri, 16)
        load_w_bf16(S, wtm, "wtm", g.d_wtm, NTM)
        wlr = sb("wlr", [16, 128], F32)
        sm = sb("sm", [128, 16], F32)
        cw = sb("cw", [128, 6, 4], F32)
        S.op("sp", lambda e: e.dma_start(out=wlr[:], in_=g.d_wlr), writes=["wlr"], dma_key="s_wlr")
        S.op("sp", lambda e: e.dma_start(out=sm[:, 0:8], in_=g.d_sm), writes=["sm"], dma_key="s_sm")
        S.op("sp", lambda e: e.dma_start(out=cw[:], in_=g.d_cw), writes=["cw"], dma_key="s_cw")
        S.op("dve", lambda e: e.tensor_scalar(out=sm[:, 8:9], in0=sm[:, 0:1], scalar1=-1.0, scalar2=None, op0=ALU.mult), reads=["sm"], writes=["sm"])
        _act(S, sm[:, 9:11], sm[:, 4:6], AF.Exp, ["sm"], ["sm"])
        S.op("dve", lambda e: e.tensor_scalar(out=sm[:, 9:11], in0=sm[:, 9:11], scalar1=-1.0, scalar2=None, op0=ALU.mult), reads=["sm"], writes=["sm"])
        xt = [sb(f"xt{i}", [128, NCH, BT], F32) for i in range(2)]
        sq = sb("sq", [128, NCH, BT], BF16)
        rstd = sb("rstd", [128, BT], F32)
        hT = sb("hT", [128, NCH, BT], BF16)
        qaT = sb("qaT", [128, BT], F32)
        kaT = sb("kaT", [128, BT], F32)
        lrT = sb("lrT", [16, BT], F32)
        sagT = sb("sagT", [128, 2, BT], F32)
        sdgT = sb("sdgT", [128, 2, BT], F32)
        pre = [sb(f"pre{i}", [128, 6, BT + 3], F32) for i in range(2)]
        yT = sb("yT", [128, 6, BT], F32)
        sq2 = sb("sq2", [128, BT], BF16)
        rs2 = sb("rs2", [128, BT], F32)
        qnT = sb("qnT", [128, 2, BT], BF16)
        knT = sb("knT", [128, 2, BT], BF16)
        vT = sb("vT", [128, 2, BT], BF16)
        va = sb("va", [128, 4, 256], BF16)
        graw = sb("graw", [128, 4, 4], F32)
        gcol = sb("gcol", [128, 4, 2], F32)
        beta = sb("beta", [128, 4, 2], F32)
        nbeta = sb("nbeta", [128, 4, 2], F32)
        gst = sb("gst", [128, 4, 8], F32)
        egc = sb("egc", [128, 8], F32)
        est = sb("est", [128, 8], F32)
        edec = sb("edec", [128, 2, 8], F32)
        bcoef = sb("bcoef", [128, 8], F32)
        el = sb("el", [128, BT], F32)
        ll = sb("ll", [128, BT], F32)
        bpos = sb("bpos", [128, BT], F32)
        ones_f = cf[:, C_ONES, :]
        Eb = sb("Eb", [128, 128], F32)
        Einv = sb("Einv", [128, 128], F32)
        Est = sb("Est", [128, 128], F32)
        nbl = sb("nbl", [128, 1], F32)
        qdT = sb("qdT", [128, 128], BF16)
        kinvT = sb("kinvT", [128, 128], BF16)
        kstT = sb("kstT", [128, 128], BF16)
        kstk = sb("kstk", [128, 128], BF16)
        attT = sb("attT", [128, 128], BF16)
        Sa = sb("Sa", [128, 256], F32)
        Sab = sb("Sab", [128, 256], BF16)
        sqo = sb("sqo", [128, 2, 128], BF16)
        rso = sb("rso", [128, 128], F32)
        t1 = sb("t1", [128, 128], F32)
        oT = [sb(f"oT{i}", [128, 4, BT], BF16) for i in range(2)]
        oa = [sb(f"oa{i}", [128, 128], F32) for i in range(2)]
        H2 = lambda name, dt: sb(name, [128, 2, 128], dt)
        kst_d = [H2(f"kst_d{i}", BF16) for i in range(2)]
        kbg = H2("kbg", F32)
        vb = H2("vb", F32)
        gbc = H2("gbc", F32)
        e1 = H2("e1", F32)
        DC = H2("DC", F32)
        DSL = H2("DSL", F32)
        EG = H2("EG", F32)
        P = [H2(f"P{i}", F32) for i in range(2)]
        PT = [H2(f"PT{i}", F32) for i in range(2)]
        X = [H2(f"X{i}", F32) for i in range(2)]
        att_d = H2("att_d", BF16)
        attT_d = H2("attT_d", BF16)
        qd_d = H2("qd_d", BF16)
        u_sb = H2("u_sb", F32)
        wT_sb = H2("wT_sb", BF16)
        vnew = H2("vnew", BF16)
        Sd = H2("Sd", F32)
        Sdb = H2("Sdb", BF16)
        sqo_d = H2("sqo_d", BF16)
        rso_d = H2("rso_d", F32)
        t1_d = H2("t1_d", F32)
        fl = lambda t: t[:].rearrange("p a b -> p (a b)")
        pp = [pst(f"pp{i}") for i in range(2)]
        pss = pst("pss")
        ptm = pst("ptm")
        pg = pst("pg")
        pi = pst("pi")
        pw = pst("pw")
        ptr = pst("ptr", BF16, 1024)
        for t in (Sa, Sab):
            S.op("dve", lambda e, t=t: e.memset(t[:], 0.0), writes=["Sa", "Sab"])
        S.op("dve", lambda e: e.memset(Sd[:], 0.0), writes=["Sd"])
        S.op("dve", lambda e: e.memset(Sdb[:], 0.0), writes=["Sdb"])
        for i in range(2):
            S.op("dve", lambda e, i=i: e.memset(pre[i][:, :, 0:3], 0.0), writes=[f"pre{i}"])
            S.op("dve", lambda e, i=i: e.memset(kst_d[i][:], 0.0), writes=["kst_d"])
        S.op("dve", lambda e: e.memset(vnew[:], 0.0), writes=["vnew"])

        for blk in range(NB):
            xs = blk % 2
            xk = f"xt{xs}"
            x_ = xt[xs]
            for c in range(NCH):
                S.op("sp", lambda e, c=c, x_=x_, blk=blk: e.dma_start(out=x_[:, c, :], in_=g.d_xT[:, c, blk * BT:(blk + 1) * BT]),
                     writes=[xk], dma_key=xk)
            S.stage("norm")
            emit_norm_block(g, S, x_, xk, BT, hT, "hT", sq, pss, rstd, 0, 1)
            S.stage("proj")
            pr = pre[blk % 2]
            prk = f"pre{blk % 2}"
            prp = pre[(blk + 1) % 2]
            prpk = f"pre{(blk + 1) % 2}"
            if blk > 0:
                S.op("act", lambda e, pr=pr, prp=prp: e.copy(out=pr[:, :, 0:3], in_=prp[:, :, BT:BT + 3]), reads=[prpk], writes=[prk])
            for j in range(12):
                p_ = pp[j % 2]
                pk = f"pp{j % 2}"
                for c in range(NCH):
                    _mm(S, p_[:, :], wfm[:, c, j * 128:(j + 1) * 128], hT[:, c, :], c == 0, c == NCH - 1, ["wfm", "hT"], [pk])
                if j == 0:
                    S.op("act", lambda e, p_=p_: e.copy(out=qaT[:], in_=p_[:, :]), reads=[pk], writes=["qaT"])
                elif j == 1:
                    S.op("dve", lambda e, p_=p_: e.tensor_copy(out=kaT[:], in_=p_[:, :]), reads=[pk], writes=["kaT"])
                elif j in (2, 3):
                    _act(S, sagT[:, j - 2, :], p_[:, :], AF.Silu, [pk], ["sagT"])
                elif j < 10:
                    S.op("dve", lambda e, p_=p_, j=j, pr=pr: e.tensor_copy(out=pr[:, j - 4, 3:3 + BT], in_=p_[:, :]), reads=[pk], writes=[prk])
                else:
                    _act(S, sdgT[:, j - 10, :], p_[:, :], AF.Silu, [pk], ["sdgT"])
            for c in range(NCH):
                _mm(S, pp[0][0:16, :], wlri[:, c, :], hT[:, c, :], c == 0, c == NCH - 1, ["wlri", "hT"], ["pp0"])
            S.op("dve", lambda e: e.tensor_copy(out=lrT[:], in_=pp[0][0:16, :]), reads=["pp0"], writes=["lrT"])
            S.stage("tm")
            for tl in range(4):
                for c in range(NCH):
                    _mm(S, ptm[:, 0:NTM], hT[:, c, tl * 128:(tl + 1) * 128], wtm[:, c, :], c == 0, c == NCH - 1, ["hT", "wtm"], ["ptm"])
                S.stage("tm_a")
                S.op("act", lambda e, tl=tl: e.copy(out=va[:, tl, :], in_=ptm[:, 0:256]), reads=["ptm"], writes=["va"])
                S.stage("tm_d")
                S.op("dve", lambda e, tl=tl: e.tensor_copy(out=graw[:, tl, :], in_=ptm[:, 256:260]), reads=["ptm"], writes=["graw"])
                S.stage("tm")
            S.stage("gates")
            _act(S, beta[:], graw[:, :, 0:2], AF.Sigmoid, ["graw"], ["beta"])
            S.op("dve", lambda e: e.tensor_scalar(out=nbeta[:], in0=beta[:], scalar1=-1.0, scalar2=None, op0=ALU.mult), reads=["beta"], writes=["nbeta"])
            for h in range(2):
                _act(S, gcol[:, :, h], graw[:, :, 2 + h], AF.Exp, ["graw", "sm"], ["gcol"], bias=sm[:, 6 + h:7 + h])
                _act(S, gcol[:, :, h], gcol[:, :, h], AF.Ln, ["gcol"], ["gcol"], bias=1.0)
                S.op("dve", lambda e, h=h: e.tensor_scalar(out=gcol[:, :, h], in0=gcol[:, :, h], scalar1=sm[:, 9 + h:10 + h], scalar2=None, op0=ALU.mult),
                     reads=["gcol", "sm"], writes=["gcol"])
            gflat = gcol[:].rearrange("p a b -> p (a b)")
            for i, cidx in enumerate((C_TRI, C_BLK, C_SEL0, C_SEL1)):
                _mm(S, pg[:, 384 + 8 * i:392 + 8 * i], cf[:, cidx, :], gflat, True, True, ["cf", "gcol"], ["pg.m"])
            S.op("dve", lambda e: e.tensor_copy(out=gst[:].rearrange("p a b -> p (a b)"), in_=pg[:, 384:416]), reads=["pg.m"], writes=["gst"])
            _act(S, egc[:], gst[:, 0, :], AF.Exp, ["gst"], ["egc"])
            S.op("dve", lambda e: e.tensor_tensor(out=est[:], in0=gst[:, 1, :], in1=gst[:, 0, :], op=ALU.subtract), reads=["gst"], writes=["est"])
            _act(S, est[:], est[:], AF.Exp, ["est"], ["est"])
            _act(S, edec[:], gst[:, 2:4, :], AF.Exp, ["gst"], ["edec"])
            S.op("dve", lambda e: e.tensor_tensor(out=bcoef[:], in0=egc[:], in1=beta[:].rearrange("p a b -> p (a b)"), op=ALU.mult),
                 reads=["egc", "beta"], writes=["bcoef"])
            S.stage("conv")
            for ch in range(6):
                S.op("dve", lambda e, ch=ch, pr=pr: e.tensor_scalar(out=yT[:, ch, :], in0=pr[:, ch, 0:BT], scalar1=cw[:, ch, 0:1], scalar2=None, op0=ALU.mult),
                     reads=[prk, "cw"], writes=["yT"])
                for j in range(1, 4):
                    S.op("dve", lambda e, ch=ch, j=j, pr=pr: e.scalar_tensor_tensor(out=yT[:, ch, :], in0=pr[:, ch, j:j + BT], scalar=cw[:, ch, j:j + 1],
                                                                                 in1=yT[:, ch, :], op0=ALU.mult, op1=ALU.add),
                         reads=[prk, "cw", "yT"], writes=["yT"])
            _act(S, yT[:].rearrange("p a b -> p (a b)"), yT[:].rearrange("p a b -> p (a b)"), AF.Silu, ["yT"], ["yT"])
            for ch in range(4):
                _act(S, sq2[:], yT[:, ch, :], AF.Square, ["yT"], ["sq2"])
                _mm(S, pss[:, :], g.ones_bf, sq2[:], True, True, ["cb", "sq2"], ["pss"])
                _rstd(S, rs2[:], pss[:, :], 1.0, ["pss"], ["rs2"])
                if ch < 2:
                    S.op("dve", lambda e, ch=ch: e.scalar_tensor_tensor(out=qnT[:, ch, :], in0=yT[:, ch, :], scalar=128.0 ** -0.5, in1=rs2[:], op0=ALU.mult, op1=ALU.mult),
                         reads=["yT", "rs2"], writes=["qnT"])
                else:
                    S.op("dve", lambda e, ch=ch: e.tensor_tensor(out=knT[:, ch - 2, :], in0=yT[:, ch, :], in1=rs2[:], op=ALU.mult),
                         reads=["yT", "rs2"], writes=["knT"])
            S.op("act", lambda e: e.copy(out=vT[:], in_=yT[:, 4:6, :]), reads=["yT"], writes=["vT"])
            S.stage("glag")
            _mm(S, pss[:, :], wlr[:], lrT[:], True, True, ["wlr", "lrT"], ["pss"])
            _act(S, el[:], pss[:, :], AF.Exp, ["pss", "sm"], ["el"], scale=-1.0, bias=sm[:, 8:9])
            _act(S, ll[:], el[:], AF.Ln, ["el"], ["ll"], bias=1.0)
            for tl in range(4):
                S.op("dve", lambda e, tl=tl: e.tensor_tensor_scan(out=bpos[:, tl * 128:(tl + 1) * 128], data0=ones_f, data1=ll[:, tl * 128:(tl + 1) * 128],
                                                                  initial=0.0, op0=ALU.mult, op1=ALU.add), reads=["cf", "ll"], writes=["bpos"])
            ob = oT[blk % 2]
            obk = f"oT{blk % 2}"
            IDb, IDf = cb[:, C_ID, :], cf[:, C_ID, :]
            for tl in range(4):
                ts = slice(tl * 128, (tl + 1) * 128)
                S.skip = "gla" in SKIP
                cap_gla = []
                S.capture = cap_gla
                _act(S, Eb[:], bpos[:, ts], AF.Exp, ["bpos"], ["Eb"], scale=-1.0 / 16)
                _act(S, Einv[:], bpos[:, ts], AF.Exp, ["bpos"], ["Einv"], scale=1.0 / 16)
                S.op("dve", lambda e, tl=tl: e.tensor_scalar(out=nbl[:], in0=bpos[:, tl * 128 + 127:tl * 128 + 128], scalar1=-1.0 / 16, scalar2=None, op0=ALU.mult),
                     reads=["bpos"], writes=["nbl"])
                _act(S, Est[:], bpos[:, ts], AF.Exp, ["bpos", "nbl"], ["Est"], scale=1.0 / 16, bias=nbl[:, 0:1])
                S.op("dve", lambda e, ts=ts: e.scalar_tensor_tensor(out=qdT[:], in0=qaT[:, ts], scalar=128.0 ** -0.5, in1=Eb[:], op0=ALU.mult, op1=ALU.mult),
                     reads=["qaT", "Eb"], writes=["qdT"])
                S.op("dve", lambda e, ts=ts: e.tensor_tensor(out=kinvT[:], in0=kaT[:, ts], in1=Einv[:], op=ALU.mult), reads=["kaT", "Einv"], writes=["kinvT"])
                S.op("dve", lambda e, ts=ts: e.tensor_tensor(out=kstT[:], in0=kaT[:, ts], in1=Est[:], op=ALU.mult), reads=["kaT", "Est"], writes=["kstT"])
                S.op("pe", lambda e: e.transpose(out=ptr[:, 0:128], in_=kstT[:], identity=IDb), reads=["kstT", "cb"], writes=["ptr"])
                S.op("act", lambda e: e.copy(out=kstk[:], in_=ptr[:, 0:128]), reads=["ptr"], writes=["kstk"])
                _mm(S, pss[:, 128:256], kinvT[:], qdT[:], True, True, ["kinvT", "qdT"], ["pss"])
                S.op("dve", lambda e: e.tensor_tensor(out=attT[:], in0=pss[:, 128:256], in1=cf[:, C_MU, :], op=ALU.mult), reads=["pss", "cf"], writes=["attT"])
                for vc in range(2):
                    _mm(S, pss[:, 256 + vc * 128:384 + vc * 128], va[:, tl, vc * 128:(vc + 1) * 128], attT[:], True, False, ["va", "attT"], ["pss"])
                    _mm(S, pss[:, 256 + vc * 128:384 + vc * 128], Sab[:, vc * 128:(vc + 1) * 128], qdT[:], False, True, ["Sab", "qdT"], ["pss"])
                _act(S, sqo[:].rearrange("p a b -> p (a b)"), pss[:, 256:512], AF.Square, ["pss"], ["sqo"])
                for vc in range(2):
                    S.op("dve", lambda e, vc=vc: e.tensor_copy(out=oa[vc][:], in_=pss[:, 256 + vc * 128:384 + vc * 128]), reads=["pss"], writes=[f"oa{vc}"])
                _mm(S, ptm[:, 256:512], kstk[:], va[:, tl, :], True, True, ["kstk", "va"], ["ptm"])
                S.op("dve", lambda e: e.scalar_tensor_tensor(out=Sa[:], in0=Sa[:], scalar=Eb[:, 127:128], in1=ptm[:, 256:512], op0=ALU.mult, op1=ALU.add),
                     reads=["Sa", "Eb", "ptm"], writes=["Sa"])
                S.op("act", lambda e: e.copy(out=Sab[:], in_=Sa[:]), reads=["Sa"], writes=["Sab"])
                for vc in range(2):
                    _mm(S, pss[:, 0:128], g.ones_bf, sqo[:, vc, :], vc == 0, vc == 1, ["cb", "sqo"], ["pss"])
                _rstd(S, rso[:], pss[:, 0:128], 1.0 / 256, ["pss"], ["rso"])
                for vc in range(2):
                    S.op("dve", lambda e, vc=vc: e.tensor_tensor(out=t1[:], in0=oa[vc][:], in1=rso[:], op=ALU.mult), reads=[f"oa{vc}", "rso"], writes=["t1"])
                    S.op("dve", lambda e, vc=vc, ts=ts, ob=ob: e.scalar_tensor_tensor(out=ob[:, vc, ts], in0=t1[:], scalar=sm[:, 1 + vc:2 + vc], in1=sagT[:, vc, ts],
                                                                                    op0=ALU.mult, op1=ALU.mult), reads=["t1", "sm", "sagT"], writes=[obk])
                S.skip = "gdn" in SKIP
                cap_gdn = []
                S.capture = cap_gdn
                HR = range(2)
                hs = lambda h: slice(h * 128, (h + 1) * 128)
                for h in HR:
                    S.op("pe", lambda e, h=h, ts=ts: e.transpose(out=ptr[:, (1 + h) * 128:(2 + h) * 128], in_=knT[:, h, ts], identity=IDb), reads=["knT", "cb"], writes=["ptr"])
                for h in HR:
                    S.op("pe", lambda e, h=h, ts=ts: e.transpose(out=ptr[:, (3 + h) * 128:(4 + h) * 128], in_=vT[:, h, ts], identity=IDb), reads=["vT", "cb"], writes=["ptr"])
                for h in HR:
                    idx = tl * 2 + h
                    ks = slice((1 + h) * 128, (2 + h) * 128)
                    vs = slice((3 + h) * 128, (4 + h) * 128)
                    S.op("act", lambda e, idx=idx, h=h, ks=ks: e.activation(out=kst_d[0][0:64, h, :], in_=ptr[0:64, ks], func=AF.Identity, scale=est[0:64, idx:idx + 1]),
                         reads=["ptr", "est"], writes=["kst_d"])
                    S.op("act", lambda e, idx=idx, h=h, ks=ks: e.activation(out=kst_d[1][64:128, h, :], in_=ptr[64:128, ks], func=AF.Identity, scale=est[64:128, idx:idx + 1]),
                         reads=["ptr", "est"], writes=["kst_d"])
                    S.op("dve", lambda e, idx=idx, h=h, ks=ks: e.tensor_scalar(out=kbg[:, h, :], in0=ptr[:, ks], scalar1=bcoef[:, idx:idx + 1], scalar2=None, op0=ALU.mult),
                         reads=["ptr", "bcoef"], writes=["kbg"])
                    S.op("dve", lambda e, tl=tl, h=h, vs=vs: e.tensor_scalar(out=vb[:, h, :], in0=ptr[:, vs], scalar1=beta[:, tl, h:h + 1], scalar2=None, op0=ALU.mult),
                         reads=["ptr", "beta"], writes=["vb"])
                    S.op("dve", lambda e, tl=tl, h=h: e.tensor_scalar(out=gbc[:, h, :], in0=ones_f, scalar1=gcol[:, tl, h:h + 1], scalar2=None, op0=ALU.mult),
                         reads=["cf", "gcol"], writes=["gbc"])
                for h in HR:
                    _mm(S, pp[1][:, hs(h)], gbc[:, h, :], cf[:, C_TRI, :], True, True, ["gbc", "cf"], ["pp1"])
                for h in HR:
                    idx = tl * 2 + h
                    S.op("dve", lambda e, idx=idx, h=h: e.scalar_tensor_tensor(out=e1[:, h, :], in0=pp[1][:, hs(h)], scalar=gst[:, 0, idx:idx + 1], in1=cf[:, C_BIGC, :],
                                                                            op0=ALU.subtract, op1=ALU.add), reads=["pp1", "gst", "cf"], writes=["e1"])
                _act(S, fl(DC), fl(e1), AF.Exp, ["e1"], ["DC"], scale=-1.0)
                _act(S, fl(EG), pp[1][:, 0:256], AF.Exp, ["pp1"], ["EG"])
                for h in HR:
                    S.op("dve", lambda e, h=h: e.tensor_tensor(out=DSL[:, h, :], in0=DC[:, h, :], in1=cf[:, C_SL, :], op=ALU.mult), reads=["DC", "cf"], writes=["DSL"])
                S.op("dve", lambda e, ts=ts: e.tensor_tensor(out=qd_d[:], in0=qnT[:, :, ts], in1=EG[:], op=ALU.mult), reads=["qnT", "EG"], writes=["qd_d"])
                for h in HR:
                    _mm(S, pp[0][:, hs(h)], knT[:, h, ts], knT[:, h, ts], True, True, ["knT"], ["pp0"])
                for h in HR:
                    _mm(S, pp[0][:, 256 + h * 128:384 + h * 128], qnT[:, h, ts], knT[:, h, ts], True, True, ["qnT", "knT"], ["pp0"])
                for h in HR:
                    S.op("dve", lambda e, tl=tl, h=h: e.scalar_tensor_tensor(out=P[0][:, h, :], in0=pp[0][:, hs(h)], scalar=nbeta[:, tl, h:h + 1], in1=DSL[:, h, :],
                                                                          op0=ALU.mult, op1=ALU.mult), reads=["pp0", "nbeta", "DSL"], writes=["P0"])
                S.op("dve", lambda e: e.tensor_tensor(out=fl(att_d), in0=pp[0][:, 256:512], in1=fl(DC), op=ALU.mult), reads=["pp0", "DC"], writes=["att_d"])
                for h in HR:
                    S.op("pe", lambda e, h=h: e.transpose(out=pp[1][:, 256 + h * 128:384 + h * 128], in_=P[0][:, h, :], identity=IDf), reads=["P0", "cf"], writes=["pp1"])
                S.op("act", lambda e: e.copy(out=fl(PT[0]), in_=pp[1][:, 256:512]), reads=["pp1"], writes=["PT0"])
                for h in HR:
                    S.op("pe", lambda e, h=h: e.transpose(out=ptr[:, (5 + h) * 128:(6 + h) * 128], in_=att_d[:, h, :], identity=IDb), reads=["att_d", "cb"], writes=["ptr"])
                S.op("act", lambda e: e.copy(out=fl(attT_d), in_=ptr[:, 640:896]), reads=["ptr"], writes=["attT_d"])
                for h in HR:
                    S.op("dve", lambda e, h=h: e.tensor_tensor(out=X[0][:, h, :], in0=PT[0][:, h, :], in1=IDf, op=ALU.add), reads=["PT0", "cf"], writes=["X0"])
                cur = 0
                for k in range(5):
                    nxt = 1 - cur
                    for h in HR:
                        _mm(S, ptm[:, hs(h)], PT[cur][:, h, :], P[cur][:, h, :], True, True, [f"PT{cur}", f"P{cur}"], ["ptm"])
                    if k < 4:
                        for h in HR:
                            _mm(S, pi[:, hs(h)], P[cur][:, h, :], PT[cur][:, h, :], True, True, [f"PT{cur}", f"P{cur}"], ["pi"])
                    S.op("act", lambda e, nxt=nxt: e.copy(out=fl(P[nxt]), in_=ptm[:, 0:256]), reads=["ptm"], writes=[f"P{nxt}"])
                    if k < 4:
                        S.op("dve", lambda e, nxt=nxt: e.tensor_copy(out=fl(PT[nxt]), in_=pi[:, 0:256]), reads=["pi"], writes=[f"PT{nxt}"])
                    for h in HR:
                        _mm(S, pg[:, hs(h)], P[nxt][:, h, :], X[cur][:, h, :], True, True, [f"P{nxt}", f"X{cur}"], ["pg"])
                    S.op("dve", lambda e, nxt=nxt, cur=cur: e.tensor_tensor(out=fl(X[nxt]), in0=pg[:, 0:256], in1=fl(X[cur]), op=ALU.add),
                         reads=["pg", f"X{cur}"], writes=[f"X{nxt}"])
                    cur = nxt
                Xf = X[cur]
                xk2 = f"X{cur}"
                for h in HR:
                    _mm(S, pg[:, 256 + h * 128:384 + h * 128], Xf[:, h, :], vb[:, h, :], True, True, [xk2, "vb"], ["pg"])
                S.op("act", lambda e: e.copy(out=fl(u_sb), in_=pg[:, 256:512]), reads=["pg"], writes=["u_sb"])
                for h in HR:
                    _mm(S, pi[:, hs(h)], kbg[:, h, :], Xf[:, h, :], True, True, [xk2, "kbg"], ["pi"])
                S.op("act", lambda e: e.copy(out=fl(wT_sb), in_=pi[:, 0:256]), reads=["pi"], writes=["wT_sb"])
                for hf in range(2):
                    rs_ = slice(hf * 64, hf * 64 + 64)
                    for h in HR:
                        _mm(S, pi[rs_, 256 + h * 128:384 + h * 128], wT_sb[:, h, rs_], Sdb[:, h, :], True, True, ["wT_sb", "Sdb"], ["pi"])
                    S.op("dve", lambda e, rs_=rs_: e.tensor_tensor(out=vnew[rs_, :, :].rearrange("p a b -> p (a b)"), in0=u_sb[rs_, :, :].rearrange("p a b -> p (a b)"),
                                                                   in1=pi[rs_, 256:512], op=ALU.subtract), reads=["u_sb", "pi"], writes=["vnew"])
                    for h in HR:
                        oc = slice(256 + h * 128 + hf * 64, 256 + h * 128 + hf * 64 + 64)
                        _mm(S, pw[:, oc], Sdb[:, h, :], qd_d[:, h, rs_], True, False, ["Sdb", "qd_d"], ["pw"])
                        _mm(S, pw[:, oc], vnew[:, h, :], attT_d[:, h, rs_], False, True, ["vnew", "attT_d"], ["pw"])
                    for h in HR:
                        _mm(S, pw[:, hs(h)], kst_d[hf][:, h, :], vnew[:, h, :], True, True, ["kst_d", "vnew"], ["pw"])
                    for h in HR:
                        idx = tl * 2 + h
                        S.op("dve", lambda e, h=h, hf=hf, idx=idx: e.scalar_tensor_tensor(out=Sd[:, h, :], in0=Sd[:, h, :], scalar=edec[:, hf, idx:idx + 1], in1=pw[:, hs(h)],
                                                                                       op0=ALU.mult, op1=ALU.add), reads=["Sd", "edec", "pw"], writes=["Sd"])
                    S.op("act", lambda e: e.copy(out=fl(Sdb), in_=fl(Sd)), reads=["Sd"], writes=["Sdb"])
                _act(S, fl(sqo_d), pw[:, 256:512], AF.Square, ["pw"], ["sqo_d"])
                for h in HR:
                    _mm(S, pg[:, 256 + h * 128:384 + h * 128], g.ones_bf, sqo_d[:, h, :], True, True, ["cb", "sqo_d"], ["pg"])
                _rstd(S, fl(rso_d), pg[:, 256:512], 1.0 / 128, ["pg"], ["rso_d"])
                S.op("dve", lambda e: e.tensor_tensor(out=fl(t1_d), in0=pw[:, 256:512], in1=fl(rso_d), op=ALU.mult), reads=["pw", "rso_d"], writes=["t1_d"])
                S.op("dve", lambda e, ts=ts, ob=ob: e.scalar_tensor_tensor(out=ob[:, 2:4, ts], in0=t1_d[:], scalar=sm[:, 3:4], in1=sdgT[:, :, ts],
                                                                         op0=ALU.mult, op1=ALU.mult), reads=["t1_d", "sm", "sdgT"], writes=[obk])
                S.capture = None
                S.skip = False
                S.replay_interleaved([cap_gdn, cap_gla])
            S.skip = False
            if not g.fused:
                for j in range(4):
                    S.op("sp", lambda e, j=j, ob=ob, blk=blk: e.dma_start(out=g.d_oT[:, j, blk * BT:(blk + 1) * BT], in_=ob[:, j, :]), reads=[obk], dma_key=obk + "s")
            else:
                psz = min(BT, g.TCH)
                for pc in range(BT // psz):
                    tok0 = blk * BT + pc * psz
                    k, off = tok0 // g.TCH, tok0 % g.TCH
                    for j in range(4):
                        S.op("sp", lambda e, j=j, ob=ob, k=k, off=off, pc=pc: e.dma_start(out=g.d_oT[k, :, j, off:off + psz], in_=ob[:, j, pc * psz:(pc + 1) * psz]),
                             reads=[obk], writes=[f"oTd{k}"], dma_key=obk + "s")
                while g.next_cc < g.NCK and (g.next_cc + 1) * g.TCH <= (blk + 1) * BT:
                    k = g.next_cc
                    S.op("pool", lambda e, k=k: e.collective_compute("AllGather", ALU.bypass, replica_groups=[[0, 1, 2, 3], [4, 5, 6, 7]],
                                                                     ins=[g.d_oT[k].rearrange("p j t -> p (j t)")], outs=[g.d_gath[k].rearrange("q j t -> q (j t)")]),
                         reads=[f"oTd{k}"], writes=[f"gath{k}"], dma_key="cc", dma_inc=1)
                    g.next_cc += 1
        S.barrier()
        S.emit()


def emit_phase2(g, S, nc, SQ, o2_src, o2_reads):
    BT = 256
    NB = SQ // BT
    with ExitStack() as es:
        sb = lambda name, shape, dt: es.enter_context(nc.sbuf_tensor("p2a_" + name, shape, dt))
        pst = lambda name, dt=F32, n=512: es.enter_context(nc.psum_tensor("p2a_" + name, [128, n], dt))
        wm = sb("wm", [128, NCH, 2048], BF16)
        wbg = sb("wbg", [128, NCH, 1024], BF16)
        wbd = sb("wbd", [128, NCH, 1024], BF16)
        wo = sb("wo", [128, NCH, 1024], BF16)
        load_w_bf16(S, wm, "wm", g.d_wm, 2048)
        load_w_bf16(S, wbg, "wbg", g.d_wbg, 1024)
        load_w_bf16(S, wbd, "wbd", g.d_wbd, 1024)
        load_w_bf16(S, wo, "wo", g.d_wo, 1024)
        xt = [sb(f"x2t{i}", [128, NCH, BT], F32) for i in range(2)]
        xr = [sb(f"x2r{i}", [128, NCH, BT], F32) for i in range(2)]
        sq = sb("sq", [128, NCH, BT], BF16)
        rstd = sb("rstd", [128, BT], F32)
        hT = sb("hT", [128, NCH, BT], BF16)
        sg = sb("sg", [128, 16, BT], F32)
        o2 = [sb(f"o2{i}", [128, 16, BT], BF16) for i in range(2)]
        mg = sb("mg", [128, NCH, BT], BF16)
        tmp = sb("tmp", [128, BT], F32)
        zz = sb("zz", [128, NCH, BT], F32)
        pp = [pst(f"pp{i}") for i in range(2)]
        pss = pst("pss")
        for blk in range(NB):
            s_ = blk % 2
            x_, xk = xt[s_], f"x2t{s_}"
            xr_, xrk = xr[s_], f"x2r{s_}"
            o_, ok = o2[s_], f"o2{s_}"
            t0 = blk * BT
            for c in range(NCH):
                S.op("sp", lambda e, c=c, x_=x_, t0=t0: e.dma_start(out=x_[:, c, :], in_=g.d_xT2[:, c, t0:t0 + BT]), writes=[xk], dma_key=xk)
                S.op("sp", lambda e, c=c, xr_=xr_, t0=t0: e.dma_start(out=xr_[:, c, :], in_=g.d_xT2[:, c, t0:t0 + BT]), writes=[xrk], dma_key=xrk)
            for kind in range(2):
                for c in range(NCH):
                    S.op("sp", lambda e, kind=kind, c=c, o_=o_, t0=t0: e.dma_start(out=o_[:, kind * 8 + c, :], in_=o2_src(e, kind, c, t0, BT)), reads=o2_reads, writes=[ok], dma_key=ok)
            emit_norm_block(g, S, x_, xk, BT, hT, "hT", sq, pss, rstd, 0, 1)
            for j in range(16):
                p_, pk = pp[j % 2], f"pp{j % 2}"
                for c in range(NCH):
                    _mm(S, p_[:, 0:BT], wm[:, c, j * 128:(j + 1) * 128], hT[:, c, :], c == 0, c == NCH - 1, ["wm", "hT"], [pk])
                _act(S, sg[:, j, :], p_[:, 0:BT], AF.Sigmoid, [pk], ["sg"])
            for oc in range(NCH):
                for kind, w_ in ((0, wbg), (1, wbd)):
                    p_, pk = pp[kind], f"pp{kind}"
                    wk = "wbg" if kind == 0 else "wbd"
                    for c in range(NCH):
                        _mm(S, p_[:, 0:BT], w_[:, c, oc * 128:(oc + 1) * 128], o_[:, kind * 8 + c, :], c == 0, c == NCH - 1, [wk, ok], [pk])
                S.op("dve", lambda e, oc=oc: e.tensor_tensor(out=tmp[:], in0=pp[0][:, 0:BT], in1=sg[:, oc, :], op=ALU.mult), reads=["pp0", "sg"], writes=["tmp"])
                S.op("dve", lambda e, oc=oc: e.tensor_tensor(out=zz[:, oc, :], in0=pp[1][:, 0:BT], in1=sg[:, 8 + oc, :], op=ALU.mult), reads=["pp1", "sg"], writes=["zz"])
                S.op("dve", lambda e, oc=oc: e.tensor_tensor(out=mg[:, oc, :], in0=tmp[:], in1=zz[:, oc, :], op=ALU.add), reads=["tmp", "zz"], writes=["mg"])
            for oc in range(NCH):
                p_, pk = pp[oc % 2], f"pp{oc % 2}"
                for c in range(NCH):
                    _mm(S, p_[:, 0:BT], wo[:, c, oc * 128:(oc + 1) * 128], mg[:, c, :], c == 0, c == NCH - 1, ["wo", "mg"], [pk])
                S.op("dve", lambda e, oc=oc, p_=p_: e.tensor_copy(out=zz[:, oc, :], in_=p_[:, 0:BT]), reads=[pk], writes=["zz"])
            _act(S, sq[:], zz[:], AF.Square, ["zz"], ["sq"])
            for c in range(NCH):
                _mm(S, pss[:, 0:BT], g.ones_bf, sq[:, c, :], c == 0, c == NCH - 1, ["cb", "sq"], ["pss"])
            _rstd(S, rstd[:], pss[:, 0:BT], 1.0 / D, ["pss"], ["rstd"])
            for c in range(NCH):
                S.op("dve", lambda e, c=c: e.tensor_tensor(out=zz[:, c, :], in0=zz[:, c, :], in1=rstd[:], op=ALU.mult), reads=["zz", "rstd"], writes=["zz"])
                S.op("dve", lambda e, c=c, xr_=xr_: e.scalar_tensor_tensor(out=xr_[:, c, :], in0=zz[:, c, :], scalar=g.AB[:, 2, c:c + 1], in1=xr_[:, c, :],
                                                                         op0=ALU.mult, op1=ALU.add), reads=["zz", "AB", xrk], writes=[xrk])
                S.op("sp", lambda e, c=c, xr_=xr_, t0=t0: e.dma_start(out=g.d_x1[:, c, t0:t0 + BT], in_=xr_[:, c, :]), reads=[xrk], writes=["x1d"], dma_key=xrk + "s")
        S.barrier()
        S.emit()
    with ExitStack() as es:
        sb = lambda name, shape, dt: es.enter_context(nc.sbuf_tensor("p2b_" + name, shape, dt))
        pst = lambda name, dt=F32, n=512: es.enter_context(nc.psum_tensor("p2b_" + name, [128, n], dt))
        w1 = sb("w1", [128, NCH, 4096], BF16)
        w2 = sb("w2", [128, 32, 1024], BF16)
        load_w_bf16(S, w1, "w1", g.d_w1, 4096)
        load_w_bf16(S, w2, "w2", g.d_w2, 1024, nrows=32)
        xt = [sb(f"x3t{i}", [128, NCH, BT], F32) for i in range(1)]
        xr = [sb(f"x3r{i}", [128, NCH, BT], F32) for i in range(1)]
        sq = sb("sq", [128, NCH, BT], BF16)
        rstd = sb("rstd", [128, BT], F32)
        hT = sb("hT", [128, NCH, BT], BF16)
        rl = [sb(f"rl{i}", [128, BT], F32) for i in range(2)]
        hid = sb("hid", [128, 32, BT], BF16)
        zz = sb("zz", [128, NCH, BT], F32)
        pp = [pst(f"pp{i}") for i in range(2)]
        pss = pst("pss")
        for blk in range(NB):
            s_ = 0
            x_, xk = xt[s_], f"x3t{s_}"
            xr_, xrk = xr[s_], f"x3r{s_}"
            t0 = blk * BT
            for c in range(NCH):
                S.op("sp", lambda e, c=c, x_=x_, t0=t0: e.dma_start(out=x_[:, c, :], in_=g.d_x1[:, c, t0:t0 + BT]), reads=["x1d"], writes=[xk], dma_key=xk)
                S.op("sp", lambda e, c=c, xr_=xr_, t0=t0: e.dma_start(out=xr_[:, c, :], in_=g.d_x1[:, c, t0:t0 + BT]), reads=["x1d"], writes=[xrk], dma_key=xrk)
            emit_norm_block(g, S, x_, xk, BT, hT, "hT", sq, pss, rstd, 3, 4)
            for fc in range(32):
                p_, pk = pp[fc % 2], f"pp{fc % 2}"
                for c in range(NCH):
                    _mm(S, p_[:, 0:BT], w1[:, c, fc * 128:(fc + 1) * 128], hT[:, c, :], c == 0, c == NCH - 1, ["w1", "hT"], [pk])
                _act(S, rl[fc % 2][:], p_[:, 0:BT], AF.Relu, [pk], [f"rl{fc % 2}"])
                S.op("dve", lambda e, fc=fc: e.tensor_tensor(out=hid[:, fc, :], in0=rl[fc % 2][:], in1=rl[fc % 2][:], op=ALU.mult), reads=[f"rl{fc % 2}"], writes=["hid"])
            for oc in range(NCH):
                p_, pk = pp[oc % 2], f"pp{oc % 2}"
                for fc in range(32):
                    _mm(S, p_[:, 0:BT], w2[:, fc, oc * 128:(oc + 1) * 128], hid[:, fc, :], fc == 0, fc == 31, ["w2", "hid"], [pk])
                S.op("dve", lambda e, oc=oc, p_=p_: e.tensor_copy(out=zz[:, oc, :], in_=p_[:, 0:BT]), reads=[pk], writes=["zz"])
            _act(S, sq[:], zz[:], AF.Square, ["zz"], ["sq"])
            for c in range(NCH):
                _mm(S, pss[:, 0:BT], g.ones_bf, sq[:, c, :], c == 0, c == NCH - 1, ["cb", "sq"], ["pss"])
            _rstd(S, rstd[:], pss[:, 0:BT], 1.0 / D, ["pss"], ["rstd"])
            for c in range(NCH):
                S.op("dve", lambda e, c=c: e.tensor_tensor(out=zz[:, c, :], in0=zz[:, c, :], in1=rstd[:], op=ALU.mult), reads=["zz", "rstd"], writes=["zz"])
                S.op("dve", lambda e, c=c, xr_=xr_: e.scalar_tensor_tensor(out=xr_[:, c, :], in0=zz[:, c, :], scalar=g.AB[:, 5, c:c + 1], in1=xr_[:, c, :],
                                                                         op0=ALU.mult, op1=ALU.add), reads=["zz", "AB", xrk], writes=[xrk])
                S.op("sp", lambda e, c=c, xr_=xr_, t0=t0: e.dma_start(out=g.d_out[:, c, t0:t0 + BT], in_=xr_[:, c, :]), reads=[xrk], dma_key=xrk + "s")
        S.barrier()
        S.emit()


def build_program(SEQ, mode):
    SQ = SEQ // 4
    nc = bass.Bass("TRN2", target_bir_lowering=False)
    g = Ctx()
    din = lambda name, shape, dt=F32: nc.dram_tensor(name, shape, dt, kind="ExternalInput").ap()
    g.d_consts = din("consts", [128, NCONST, 128])
    g.d_cT = din("cT", [128, NCH])
    g.d_adab = din("adab", [128, 48])
    g.d_nw = din("nw", [128, 4, NCH])
    g.d_adaw = din("adaw", [128, NCH, 6144])
    if mode in ("A", "F"):
        g.d_xT = din("xT", [128, NCH, SEQ])
        g.d_wfm = din("wfm", [128, NCH, NFM])
        g.d_wlri = din("wlri", [128, NCH, 16])
        g.d_wtm = din("wtm", [128, NCH, NTM])
        g.d_wlr = din("wlr", [16, 128])
        g.d_sm = din("sm", [128, 8])
        g.d_cw = din("cw", [128, 6, 4])
    if mode == "A":
        g.d_oT = nc.dram_tensor("oT", [128, 4, SEQ], BF16, kind="ExternalOutput").ap()
    if mode == "F":
        g.TCH = min(1024, SQ)
        g.NCK = SEQ // g.TCH
        g.d_oT = nc.dram_tensor("oT", [g.NCK, 128, 4, g.TCH], BF16, kind="Internal").ap()
        g.d_gath = nc.dram_tensor("gath", [g.NCK, 4 * 128, 4, g.TCH], BF16, kind="Internal").ap()
        g.d_o2q = nc.dram_tensor("o2q", [4 * 128, 4, SQ], BF16, kind="Internal").ap()
    if mode in ("B", "F"):
        g.d_xT2 = din("xT2", [128, NCH, SQ])
        g.d_wm = din("wm", [128, NCH, 2048])
        g.d_wbg = din("wbg", [128, NCH, 1024])
        g.d_wbd = din("wbd", [128, NCH, 1024])
        g.d_wo = din("wo", [128, NCH, 1024])
        g.d_w1 = din("w1", [128, NCH, 4096])
        g.d_w2 = din("w2", [128, 32, 1024])
        g.d_x1 = nc.dram_tensor("x1", [128, NCH, SQ], F32, kind="Internal").ap()
        g.d_out = nc.dram_tensor("out", [128, NCH, SQ], F32, kind="ExternalOutput").ap()
    if mode == "B":
        g.d_o2 = din("o2", [128, 2, NCH, SQ], BF16)
    with ExitStack() as es:
        S = Sched(nc, es)
        g.fused = mode == "F"
        g.next_cc = 0
        emit_common(g, S, es, nc, True)
        if mode in ("A", "F"):
            emit_phase1(g, S, nc, SEQ)
        if mode == "F":
            CPQ = SQ // g.TCH
            gathv = g.d_gath.rearrange("(q c) r j t -> q c r j t", c=CPQ)

            def mkcopy(cc):
                def f(e):
                    if g.pidc.get("blk") != S.nblocks:
                        g.pidc = {"blk": S.nblocks, "pid4": e.snap(e.partition_id() % 4)}
                    src = gathv[bass.DynSlice(g.pidc["pid4"], 1), cc, :, :, :].rearrange("q r j t -> (q r) j t")
                    return e.dma_start(out=g.d_o2q[:, :, cc * g.TCH:(cc + 1) * g.TCH], in_=src)
                return f
            g.pidc = {}
            for cc in range(CPQ):
                S.op("sp", mkcopy(cc), reads=[f"gath{k}" for k in range(g.NCK)], writes=["o2q"], dma_key="o2q")

            def o2_src(e, kind, c, t0, n):
                hg, j = c // 2, c % 2
                return g.d_o2q[hg * 128:(hg + 1) * 128, kind * 2 + j, t0:t0 + n]
            emit_phase2(g, S, nc, SQ, o2_src, ["o2q"])
        if mode == "B":
            emit_phase2(g, S, nc, SQ, lambda e, kind, c, t0, n: g.d_o2[:, kind, c, t0:t0 + n], [])
    return nc


def _fm(w):
    r, n = w.shape
    return np.ascontiguousarray(w.reshape(r // 128, 128, n).transpose(1, 0, 2))


def _consts():
    p = np.arange(128)[:, None]
    f = np.arange(128)[None, :]
    same = (p // 64) == (f // 64)
    c = np.zeros((NCONST, 128, 128), np.float32)
    c[C_ID] = (p == f)
    c[C_MU] = (p <= f)
    c[C_SL] = same & (f < p)
    c[C_BIGC] = np.where(same & (f <= p), 0.0, BIG)
    c[C_TRI] = same & (p <= f)
    c[C_BLK] = same
    c[C_SEL0] = (p < 64) & (f >= 0)
    c[C_SEL1] = (p >= 64) & (f >= 0)
    c[C_ONES] = 1.0
    return np.ascontiguousarray(c.transpose(1, 0, 2))


def host_inputs(inp, SEQ):
    x = inp["x"]
    w_in = inp["w_in"][0]
    maps1, maps2 = [], []
    consts = _consts()
    adaw = _fm(inp["ada_w"][0])
    nw = np.stack([inp[k][0].reshape(8, 128).T for k in ("pre_mix_w", "post_mix_w", "pre_mlp_w", "post_mlp_w")], axis=1)
    adab = np.ascontiguousarray(inp["ada_b"][0].reshape(48, 128).T)
    offs = np.cumsum([0, 512, 512, 1024, 1024, 16, 3072, 1024, 8, 8, 1024, 1024])
    o_aq, o_ak, o_av, o_ag, o_lr, o_dqkv, o_dg, o_db, o_da, o_ma, o_md = offs[:11]
    SQ = SEQ // 4
    xTs = [np.ascontiguousarray(x[b, :SEQ].T.reshape(8, 128, SEQ).transpose(1, 0, 2)) for b in range(x.shape[0])]
    for r in range(8):
        b, hg = r // 4, r % 4
        common = {"consts": consts, "cT": np.ascontiguousarray(inp["c"][b].reshape(8, 128).T), "adab": adab,
                  "nw": np.ascontiguousarray(nw), "adaw": adaw}
        cols = []
        cols += list(range(o_aq + hg * 128, o_aq + (hg + 1) * 128))
        cols += list(range(o_ak + hg * 128, o_ak + (hg + 1) * 128))
        cols += list(range(o_ag + hg * 256, o_ag + (hg + 1) * 256))
        for part in range(3):
            cols += list(range(o_dqkv + part * 1024 + hg * 256, o_dqkv + part * 1024 + (hg + 1) * 256))
        cols += list(range(o_dg + hg * 256, o_dg + (hg + 1) * 256))
        tcols = list(range(o_av + hg * 256, o_av + (hg + 1) * 256)) + [o_db + 2 * hg, o_db + 2 * hg + 1, o_da + 2 * hg, o_da + 2 * hg + 1]
        sm = np.zeros((128, 8), np.float32)
        sm[:, 0] = inp["gla_b_lr"][0][hg * 128:(hg + 1) * 128]
        sm[:, 1:3] = inp["gla_onorm_w"][0].reshape(2, 128).T
        sm[:, 3] = inp["gdn_onorm_w"][0]
        sm[:, 4:6] = inp["gdn_a_log"][0][None, 2 * hg:2 * hg + 2]
        sm[:, 6:8] = inp["gdn_dt_bias"][0][None, 2 * hg:2 * hg + 2]
        cwfull = inp["gdn_conv_w"][0]
        cw = np.zeros((128, 6, 4), np.float32)
        for part in range(3):
            for h in range(2):
                c0 = part * 1024 + (2 * hg + h) * 128
                cw[:, part * 2 + h, :] = cwfull[:, c0:c0 + 128].T
        m1 = dict(common)
        m1.update({"xT": xTs[b], "wfm": _fm(w_in[:, cols]), "wlri": _fm(w_in[:, o_lr:o_lr + 16]), "wtm": _fm(np.concatenate([w_in[:, tcols], np.zeros((D, NTM - 260), np.float32)], axis=1)),
                   "wlr": np.ascontiguousarray(inp["gla_w_lr"][0][:, hg * 128:(hg + 1) * 128]), "sm": sm, "cw": cw})
        maps1.append(m1)
        m2 = dict(common)
        m2.update({"xT2": np.ascontiguousarray(xTs[b][:, :, hg * SQ:(hg + 1) * SQ]),
                   "wm": _fm(w_in[:, o_ma:o_ma + 2048]), "wbg": _fm(inp["w_branch_gla"][0]), "wbd": _fm(inp["w_branch_gdn"][0]),
                   "wo": _fm(inp["w_out"][0]), "w1": _fm(inp["mlp_w1"][0]), "w2": _fm(inp["mlp_w2"][0])})
        maps2.append(m2)
    return maps1, maps2


_CACHE = {}


def _prog(SEQ, mode):
    if (SEQ, mode) not in _CACHE:
        _CACHE[(SEQ, mode)] = build_program(SEQ, mode)
    return _CACHE[(SEQ, mode)]


def run_unfused(inp, SEQ):
    maps1, maps2 = host_inputs(inp, SEQ)
    SQ = SEQ // 4
    resA = run_bass_kernel_spmd(_prog(SEQ, "A"), maps1, core_ids=list(range(8)))
    oTs = [np.asarray(r["oT"]) for r in resA.results]
    for r in range(8):
        b, tq = r // 4, r % 4
        o2 = np.zeros((128, 2, 8, SQ), oTs[0].dtype)
        for hg in range(4):
            src = oTs[b * 4 + hg][:, :, tq * SQ:(tq + 1) * SQ]
            o2[:, 0, hg * 2:hg * 2 + 2] = src[:, 0:2]
            o2[:, 1, hg * 2:hg * 2 + 2] = src[:, 2:4]
        maps2[r]["o2"] = o2
    resB = run_bass_kernel_spmd(_prog(SEQ, "B"), maps2, core_ids=list(range(8)))
    B = inp["x"].shape[0]
    out = np.zeros((B, SEQ, D), np.float32)
    for r in range(8):
        b, tq = r // 4, r % 4
        o = np.asarray(resB.results[r]["out"])
        out[b, tq * SQ:(tq + 1) * SQ, :] = o.transpose(2, 1, 0).reshape(SQ, D)
    return out, oTs


def run_fused(inp, SEQ):
    maps1, maps2 = host_inputs(inp, SEQ)
    SQ = SEQ // 4
    maps = []
    for r in range(8):
        m = dict(maps1[r])
        m.update(maps2[r])
        maps.append(m)
    res = run_bass_kernel_spmd(_prog(SEQ, "F"), maps, core_ids=list(range(8)))
    B = inp["x"].shape[0]
    out = np.zeros((B, SEQ, D), np.float32)
    for r in range(8):
        b, tq = r // 4, r % 4
        o = np.asarray(res.results[r]["out"])
        out[b, tq * SQ:(tq + 1) * SQ, :] = o.transpose(2, 1, 0).reshape(SQ, D)
    return out


def kernel(**inputs):
    inp = {k: np.asarray(v) for k, v in inputs.items()}
    return run_fused(inp, inp["x"].shape[1])
```

```python
import numpy as np
from contextlib import ExitStack
import concourse.bass as bass
import concourse.mybir as mybir
from concourse.bass_utils import run_bass_kernel_spmd

F32 = mybir.dt.float32
BF16 = mybir.dt.bfloat16
AF = mybir.ActivationFunctionType
ALU = mybir.AluOpType

D = 1024
NCH = 8
EPS = 1e-6
NFM = 1536
NTM = 288
BIG = 30000.0
C_ID, C_MU, C_SL, C_BIGC, C_TRI, C_BLK, C_SEL0, C_SEL1, C_ONES = range(9)
NCONST = 9


import os
SKIP = set(os.environ.get("KSKIP", "").split(","))


SAME_ENG_SYNC = os.environ.get("KSAME", "1") == "1"


class Sched:
    ENGS = ("pe", "act", "dve", "pool", "sp")

    def __init__(self, nc, es):
        self.nc = nc
        self.es = es
        self.sem = {e: es.enter_context(nc.semaphore("s_" + e)) for e in self.ENGS}
        self.cnt = {e: 0 for e in self.ENGS}
        self.prog = {e: [] for e in self.ENGS}
        self.waited = {e: {} for e in self.ENGS}
        self.lastw = {}
        self.readers = {}
        self.dsem = {}
        self.dcnt = {}

    def _need(self, eng, waits, tok):
        if tok is None:
            return
        kind, key, val = tok
        if kind == "e" and key == eng and (eng == "pe" or not SAME_ENG_SYNC):
            return
        w = self.waited[eng]
        if w.get((kind, key), 0) >= val:
            return
        w[(kind, key)] = val
        waits[(kind, key)] = max(waits.get((kind, key), 0), val)

    skip = False

    def stage(self, name):
        self.skip = name in SKIP

    capture = None

    def op(self, eng, fn, reads=(), writes=(), dma_key=None, dma_inc=16):
        if self.skip:
            return None
        if self.capture is not None:
            self.capture.append((eng, fn, tuple(reads), tuple(writes), dma_key, dma_inc))
            return None
        banks = set()
        for k in list(reads) + list(writes):
            for pfx in ("pp0", "pp1", "pss", "ptm", "pg", "pi", "pw", "ptr", "mps"):
                if k == pfx or k.startswith(pfx + ".") or (pfx == "ptr" and k.startswith("ptr")):
                    banks.add("B:" + pfx)
        writes = list(writes) + sorted(banks)
        waits = {}
        for k in reads:
            self._need(eng, waits, self.lastw.get(k))
        for k in writes:
            self._need(eng, waits, self.lastw.get(k))
            for t in self.readers.get(k, ()):
                self._need(eng, waits, t)
        if dma_key is None:
            self.cnt[eng] += 1
            tok = ("e", eng, self.cnt[eng])
        else:
            if dma_key not in self.dsem:
                self.dsem[dma_key] = self.es.enter_context(self.nc.semaphore("d_" + str(dma_key)))
                self.dcnt[dma_key] = 0
            self.dcnt[dma_key] += dma_inc
            tok = ("d", dma_key, self.dcnt[dma_key])
        self.prog[eng].append((waits, fn, tok, dma_inc))
        for k in reads:
            self.readers.setdefault(k, []).append(tok)
        for k in writes:
            self.lastw[k] = tok
            self.readers[k] = []
        return tok

    def replay_interleaved(self, streams):
        pos = [0] * len(streams)
        total = max(len(st) for st in streams)
        for step in range(total):
            for i, st in enumerate(streams):
                upto = (step + 1) * len(st) // total
                while pos[i] < upto:
                    self.op(*st[pos[i]])
                    pos[i] += 1

    def barrier(self, engs=None):
        toks = [("e", e, self.cnt[e]) for e in self.ENGS if self.cnt[e] > 0]
        toks += [("d", k, v) for k, v in self.dcnt.items()]
        for e in (engs or self.ENGS):
            waits = {}
            for t in toks:
                self._need(e, waits, t)
            if waits:
                self.prog[e].append((waits, None, None, 0))

    nblocks = 0

    def emit(self):
        self.nblocks += 1
        nc = self.nc
        prog = self.prog
        self.prog = {e: [] for e in self.ENGS}
        with nc.Block() as block:
            def body(e):
                def f(engine):
                    for waits, fn, tok, dinc in prog[e]:
                        for (kind, key), val in waits.items():
                            s = self.sem[key] if kind == "e" else self.dsem[key]
                            engine.wait_ge(s, val)
                        if fn is None:
                            continue
                        ins = fn(engine)
                        if tok[0] == "e":
                            ins.then_inc(self.sem[tok[1]], 1)
                        else:
                            ins.then_inc(self.dsem[tok[1]], dinc)
                return f
            block.tensor(body("pe"))
            block.scalar(body("act"))
            block.vector(body("dve"))
            block.gpsimd(body("pool"))
            block.sync(body("sp"))


class Ctx:
    pass


def _mm(S, out, lhsT, rhs, start, stop, reads, writes):
    S.op("pe", lambda e: e.matmul(out, lhsT=lhsT, rhs=rhs, start=start, stop=stop), reads=reads, writes=writes)


def _act(S, out, in_, func, reads, writes, scale=None, bias=None):
    kw = {}
    if scale is not None:
        kw["scale"] = scale
    if bias is not None:
        kw["bias"] = bias
    S.op("act", lambda e: e.activation(out=out, in_=in_, func=func, **kw), reads=reads, writes=writes)


def _rstd(S, out, ss, inv_n, reads, writes):
    _act(S, out, ss, AF.Ln, reads, writes, scale=inv_n, bias=EPS)
    _act(S, out, out, AF.Exp, writes, writes, scale=-0.5)


def emit_common(g, S, es, nc, need_mix):
    sb = lambda name, shape, dt: es.enter_context(nc.sbuf_tensor("cm_" + name, shape, dt))
    g.cf = sb("cf", [128, NCONST, 128], F32)
    g.cb = sb("cb", [128, NCONST, 128], BF16)
    S.op("sp", lambda e: e.dma_start(out=g.cf[:], in_=g.d_consts), writes=["cf"], dma_key="cf")
    S.op("dve", lambda e: e.tensor_copy(out=g.cb[:], in_=g.cf[:]), reads=["cf"], writes=["cb"])
    g.ones_bf = g.cb[:, C_ONES, :]
    cT = sb("cT", [128, NCH], F32)
    scT = sb("scT", [128, NCH], F32)
    adab = sb("adab", [128, 48], F32)
    nw = sb("nw", [128, 4, NCH], F32)
    mod = sb("mod", [128, 48], F32)
    S.op("sp", lambda e: e.dma_start(out=cT[:], in_=g.d_cT), writes=["cT"], dma_key="s_cT")
    S.op("sp", lambda e: e.dma_start(out=adab[:], in_=g.d_adab), writes=["adab"], dma_key="s_adab")
    S.op("sp", lambda e: e.dma_start(out=nw[:], in_=g.d_nw), writes=["nw"], dma_key="s_nw")
    _act(S, scT[:], cT[:], AF.Silu, ["cT"], ["scT"])
    S.stage("ada")
    with ExitStack() as es2:
        adaw = [es2.enter_context(nc.sbuf_tensor(f"cm_adaw{i}", [128, NCH, 1024], F32)) for i in range(2)]
        mps = es2.enter_context(nc.psum_tensor("cm_mps", [128, 512], F32))
        for k in range(6):
            aw = adaw[k % 2]
            key = f"adaw{k % 2}"
            for c in range(NCH):
                S.op("sp", lambda e, aw=aw, c=c, k=k: e.dma_start(out=aw[:, c, :], in_=g.d_adaw[:, c, k * 1024:(k + 1) * 1024]),
                     writes=[key], dma_key=key)
            for oc in range(NCH):
                col = k * 8 + oc
                for c in range(NCH):
                    _mm(S, mps[:, col:col + 1], aw[:, c, oc * 128:(oc + 1) * 128], scT[:, c:c + 1], c == 0, c == NCH - 1,
                        [key, "scT"], ["mps"])
        S.stage("none")
        S.op("dve", lambda e: e.tensor_tensor(out=mod[:], in0=mps[:, 0:48], in1=adab[:], op=ALU.add), reads=["mps", "adab"], writes=["mod"])
        S.barrier()
        S.emit()
    g.mod = mod
    g.AB = sb("AB", [128, 6, NCH], F32)
    def mk(idx, scale_k, w_k, plus1):
        if plus1:
            S.op("dve", lambda e: e.scalar_tensor_tensor(out=g.AB[:, idx, :], in0=mod[:, scale_k * 8:scale_k * 8 + 8], scalar=1.0,
                                                         in1=nw[:, w_k, :], op0=ALU.add, op1=ALU.mult), reads=["mod", "nw"], writes=["AB"])
        else:
            S.op("dve", lambda e: e.tensor_tensor(out=g.AB[:, idx, :], in0=mod[:, scale_k * 8:scale_k * 8 + 8], in1=nw[:, w_k, :], op=ALU.mult),
                 reads=["mod", "nw"], writes=["AB"])
    mk(0, 1, 0, True)
    S.op("dve", lambda e: e.tensor_copy(out=g.AB[:, 1, :], in_=mod[:, 0:8]), reads=["mod"], writes=["AB"])
    mk(2, 2, 1, False)
    mk(3, 4, 2, True)
    S.op("dve", lambda e: e.tensor_copy(out=g.AB[:, 4, :], in_=mod[:, 24:32]), reads=["mod"], writes=["AB"])
    mk(5, 5, 3, False)


def emit_norm_block(g, S, xt, xkey, n, hT, hkey, sq, ssps, rstd, ia, ib):
    _act(S, sq[:, :, 0:n], xt[:, :, 0:n], AF.Square, [xkey], ["sq"])
    for c in range(NCH):
        _mm(S, ssps[:, 0:n], g.ones_bf, sq[:, c, 0:n], c == 0, c == NCH - 1, ["cb", "sq"], ["ssps"])
    _rstd(S, rstd[:, 0:n], ssps[:, 0:n], 1.0 / D, ["ssps"], ["rstd"])
    for c in range(NCH):
        S.op("dve", lambda e, c=c: e.tensor_tensor(out=xt[:, c, 0:n], in0=xt[:, c, 0:n], in1=rstd[:, 0:n], op=ALU.mult),
             reads=[xkey, "rstd"], writes=[xkey])
    for c in range(NCH):
        _act(S, hT[:, c, 0:n], xt[:, c, 0:n], AF.Identity, [xkey, "AB"], [hkey], scale=g.AB[:, ia, c:c + 1], bias=g.AB[:, ib, c:c + 1])


def load_w_bf16(S, dst, dkey, src, ncols, nrows=NCH):
    for c in range(nrows):
        for c0 in range(0, ncols, 2048):
            c1 = min(ncols, c0 + 2048)
            S.op("pool", lambda e, c=c, c0=c0, c1=c1: e.dma_start(out=dst[:, c, c0:c1], in_=src[:, c, c0:c1]), writes=[dkey], dma_key=dkey)


def emit_phase1(g, S, nc, SEQ):
    BT = 512
    NB = SEQ // BT
    with ExitStack() as es:
        sb = lambda name, shape, dt: es.enter_context(nc.sbuf_tensor("p1_" + name, shape, dt))
        pst = lambda name, dt=F32, n=512: es.enter_context(nc.psum_tensor("p1_" + name, [128, n], dt))
        cf, cb = g.cf, g.cb
        wfm = sb("wfm", [128, NCH, NFM], BF16)
        wlri = sb("wlri", [128, NCH, 16], BF16)
        wtm = sb("wtm", [128, NCH, NTM], BF16)
        load_w_bf16(S, wfm, "wfm", g.d_wfm, NFM)
        load_w_bf16(S, wlri, "wlri", g.d_wlri, 16)
        load_w_bf16(S, wtm, "wtm", g.d_wtm, NTM)
        wlr = sb("wlr", [16, 128], F32)
        sm = sb("sm", [128, 16], F32)
        cw = sb("cw", [128, 6, 4], F32)
        S.op("sp", lambda e: e.dma_start(out=wlr[:], in_=g.d_wlr), writes=["wlr"], dma_key="s_wlr")
        S.op("sp", lambda e: e.dma_start(out=sm[:, 0:8], in_=g.d_sm), writes=["sm"], dma_key="s_sm")
        S.op("sp", lambda e: e.dma_start(out=cw[:], in_=g.d_cw), writes=["cw"], dma_key="s_cw")
        S.op("dve", lambda e: e.tensor_scalar(out=sm[:, 8:9], in0=sm[:, 0:1], scalar1=-1.0, scalar2=None, op0=ALU.mult), reads=["sm"], writes=["sm"])
        _act(S, sm[:, 9:11], sm[:, 4:6], AF.Exp, ["sm"], ["sm"])
        S.op("dve", lambda e: e.tensor_scalar(out=sm[:, 9:11], in0=sm[:, 9:11], scalar1=-1.0, scalar2=None, op0=ALU.mult), reads=["sm"], writes=["sm"])
        xt = [sb(f"xt{i}", [128, NCH, BT], F32) for i in range(2)]
        sq = sb("sq", [128, NCH, BT], BF16)
        rstd = sb("rstd", [128, BT], F32)
        hT = sb("hT", [128, NCH, BT], BF16)
        qaT = sb("qaT", [128, BT], F32)
        kaT = sb("kaT", [128, BT], F32)
        lrT = sb("lrT", [16, BT], F32)
        sagT = sb("sagT", [128, 2, BT], F32)
        sdgT = sb("sdgT", [128, 2, BT], F32)
        pre = [sb(f"pre{i}", [128, 6, BT + 3], F32) for i in range(2)]
        yT = sb("yT", [128, 6, BT], F32)
        sq2 = sb("sq2", [128, BT], BF16)
        rs2 = sb("rs2", [128, BT], F32)
        qnT = sb("qnT", [128, 2, BT], BF16)
        knT = sb("knT", [128, 2, BT], BF16)
        vT = sb("vT", [128, 2, BT], BF16)
        va = sb("va", [128, 4, 256], BF16)
        graw = sb("graw", [128, 4, 4], F32)
        gcol = sb("gcol", [128, 4, 2], F32)
        beta = sb("beta", [128, 4, 2], F32)
        nbeta = sb("nbeta", [128, 4, 2], F32)
        gst = sb("gst", [128, 4, 8], F32)
        egc = sb("egc", [128, 8], F32)
        est = sb("est", [128, 8], F32)
        edec = sb("edec", [128, 2, 8], F32)
        bcoef = sb("bcoef", [128, 8], F32)
        el = sb("el", [128, BT], F32)
        ll = sb("ll", [128, BT], F32)
        bpos = sb("bpos", [128, BT], F32)
        ones_f = cf[:, C_ONES, :]
        Eb = sb("Eb", [128, 128], F32)
        Einv = sb("Einv", [128, 128], F32)
        Est = sb("Est", [128, 128], F32)
        nbl = sb("nbl", [128, 1], F32)
        qdT = sb("qdT", [128, 128], BF16)
        kinvT = sb("kinvT", [128, 128], BF16)
        kstT = sb("kstT", [128, 128], BF16)
        kstk = sb("kstk", [128, 128], BF16)
        attT = sb("attT", [128, 128], BF16)
        Sa = sb("Sa", [128, 256], F32)
        Sab = sb("Sab", [128, 256], BF16)
        sqo = sb("sqo", [128, 2, 128], BF16)
        rso = sb("rso", [128, 128], F32)
        t1 = sb("t1", [128, 128], F32)
        oT = [sb(f"oT{i}", [128, 4, BT], BF16) for i in range(2)]
        oa = [sb(f"oa{i}", [128, 128], F32) for i in range(2)]
        H2 = lambda name, dt: sb(name, [128, 2, 128], dt)
        kst_d = [H2(f"kst_d{i}", BF16) for i in range(2)]
        kbg = H2("kbg", F32)
        vb = H2("vb", F32)
        gbc = H2("gbc", F32)
        e1 = H2("e1", F32)
        DC = H2("DC", F32)
        DSL = H2("DSL", F32)
        EG = H2("EG", F32)
        P = [H2(f"P{i}", F32) for i in range(2)]
        PT = [H2(f"PT{i}", F32) for i in range(2)]
        X = [H2(f"X{i}", F32) for i in range(2)]
        att_d = H2("att_d", BF16)
        attT_d = H2("attT_d", BF16)
        qd_d = H2("qd_d", BF16)
        u_sb = H2("u_sb", F32)
        wT_sb = H2("wT_sb", BF16)
        vnew = H2("vnew", BF16)
        Sd = H2("Sd", F32)
        Sdb = H2("Sdb", BF16)
        sqo_d = H2("sqo_d", BF16)
        rso_d = H2("rso_d", F32)
        t1_d = H2("t1_d", F32)
        fl = lambda t: t[:].rearrange("p a b -> p (a b)")
        pp = [pst(f"pp{i}") for i in range(2)]
        pss = pst("pss")
        ptm = pst("ptm")
        pg = pst("pg")
        pi = pst("pi")
        pw = pst("pw")
        ptr = pst("ptr", BF16, 1024)
        for t in (Sa, Sab):
            S.op("dve", lambda e, t=t: e.memset(t[:], 0.0), writes=["Sa", "Sab"])
        S.op("dve", lambda e: e.memset(Sd[:], 0.0), writes=["Sd"])
        S.op("dve", lambda e: e.memset(Sdb[:], 0.0), writes=["Sdb"])
        for i in range(2):
            S.op("dve", lambda e, i=i: e.memset(pre[i][:, :, 0:3], 0.0), writes=[f"pre{i}"])
            S.op("dve", lambda e, i=i: e.memset(kst_d[i][:], 0.0), writes=["kst_d"])
        S.op("dve", lambda e: e.memset(vnew[:], 0.0), writes=["vnew"])

        def load_x(b_):
            for c in range(NCH):
                S.op("sp", lambda e, c=c, b_=b_: e.dma_start(out=xt[b_ % 2][:, c, :], in_=g.d_xT[:, c, b_ * BT:(b_ + 1) * BT]),
                     writes=[f"xt{b_ % 2}"], dma_key=f"xt{b_ % 2}")
        load_x(0)
        for blk in range(NB):
            xs = blk % 2
            xk = f"xt{xs}"
            x_ = xt[xs]
            if blk + 1 < NB:
                load_x(blk + 1)
            S.stage("norm")
            emit_norm_block(g, S, x_, xk, BT, hT, "hT", sq, pss, rstd, 0, 1)
            S.stage("proj")
            pr = pre[blk % 2]
            prk = f"pre{blk % 2}"
            prp = pre[(blk + 1) % 2]
            prpk = f"pre{(blk + 1) % 2}"
            if blk > 0:
                S.op("act", lambda e, pr=pr, prp=prp: e.copy(out=pr[:, :, 0:3], in_=prp[:, :, BT:BT + 3]), reads=[prpk], writes=[prk])
            cap_rest, cap_conv = [], []
            for jn, j in enumerate((4, 5, 6, 7, 8, 9, 0, 1, 2, 3, 10, 11)):
                if jn == 6:
                    S.capture = cap_rest
                p_ = pp[jn % 2]
                pk = f"pp{jn % 2}"
                for c in range(NCH):
                    _mm(S, p_[:, :], wfm[:, c, j * 128:(j + 1) * 128], hT[:, c, :], c == 0, c == NCH - 1, ["wfm", "hT"], [pk])
                if j == 0:
                    S.op("act", lambda e, p_=p_: e.copy(out=qaT[:], in_=p_[:, :]), reads=[pk], writes=["qaT"])
                elif j == 1:
                    S.op("dve", lambda e, p_=p_: e.tensor_copy(out=kaT[:], in_=p_[:, :]), reads=[pk], writes=["kaT"])
                elif j in (2, 3):
                    _act(S, sagT[:, j - 2, :], p_[:, :], AF.Silu, [pk], ["sagT"])
                elif j < 10:
                    S.op("dve", lambda e, p_=p_, j=j, pr=pr: e.tensor_copy(out=pr[:, j - 4, 3:3 + BT], in_=p_[:, :]), reads=[pk], writes=[prk])
                else:
                    _act(S, sdgT[:, j - 10, :], p_[:, :], AF.Silu, [pk], ["sdgT"])
            for c in range(NCH):
                _mm(S, pp[0][0:16, :], wlri[:, c, :], hT[:, c, :], c == 0, c == NCH - 1, ["wlri", "hT"], ["pp0"])
            S.op("dve", lambda e: e.tensor_copy(out=lrT[:], in_=pp[0][0:16, :]), reads=["pp0"], writes=["lrT"])
            S.stage("tm")
            for tl in range(4):
                for c in range(NCH):
                    _mm(S, ptm[:, 0:NTM], hT[:, c, tl * 128:(tl + 1) * 128], wtm[:, c, :], c == 0, c == NCH - 1, ["hT", "wtm"], ["ptm"])
                S.stage("tm_a")
                S.op("act", lambda e, tl=tl: e.copy(out=va[:, tl, :], in_=ptm[:, 0:256]), reads=["ptm"], writes=["va"])
                S.stage("tm_d")
                S.op("dve", lambda e, tl=tl: e.tensor_copy(out=graw[:, tl, :], in_=ptm[:, 256:260]), reads=["ptm"], writes=["graw"])
                S.stage("tm")
            S.stage("gates")
            _act(S, beta[:], graw[:, :, 0:2], AF.Sigmoid, ["graw"], ["beta"])
            S.op("dve", lambda e: e.tensor_scalar(out=nbeta[:], in0=beta[:], scalar1=-1.0, scalar2=None, op0=ALU.mult), reads=["beta"], writes=["nbeta"])
            for h in range(2):
                _act(S, gcol[:, :, h], graw[:, :, 2 + h], AF.Exp, ["graw", "sm"], ["gcol"], bias=sm[:, 6 + h:7 + h])
                _act(S, gcol[:, :, h], gcol[:, :, h], AF.Ln, ["gcol"], ["gcol"], bias=1.0)
                S.op("dve", lambda e, h=h: e.tensor_scalar(out=gcol[:, :, h], in0=gcol[:, :, h], scalar1=sm[:, 9 + h:10 + h], scalar2=None, op0=ALU.mult),
                     reads=["gcol", "sm"], writes=["gcol"])
            gflat = gcol[:].rearrange("p a b -> p (a b)")
            for i, cidx in enumerate((C_TRI, C_BLK, C_SEL0, C_SEL1)):
                _mm(S, pg[:, 384 + 8 * i:392 + 8 * i], cf[:, cidx, :], gflat, True, True, ["cf", "gcol"], ["pg.m"])
            S.op("dve", lambda e: e.tensor_copy(out=gst[:].rearrange("p a b -> p (a b)"), in_=pg[:, 384:416]), reads=["pg.m"], writes=["gst"])
            _act(S, egc[:], gst[:, 0, :], AF.Exp, ["gst"], ["egc"])
            S.op("dve", lambda e: e.tensor_tensor(out=est[:], in0=gst[:, 1, :], in1=gst[:, 0, :], op=ALU.subtract), reads=["gst"], writes=["est"])
            _act(S, est[:], est[:], AF.Exp, ["est"], ["est"])
            _act(S, edec[:], gst[:, 2:4, :], AF.Exp, ["gst"], ["edec"])
            S.op("dve", lambda e: e.tensor_tensor(out=bcoef[:], in0=egc[:], in1=beta[:].rearrange("p a b -> p (a b)"), op=ALU.mult),
                 reads=["egc", "beta"], writes=["bcoef"])
            S.capture = cap_conv
            for ch in range(6):
                S.op("dve", lambda e, ch=ch, pr=pr: e.tensor_scalar(out=yT[:, ch, :], in0=pr[:, ch, 0:BT], scalar1=cw[:, ch, 0:1], scalar2=None, op0=ALU.mult),
                     reads=[prk, "cw"], writes=["yT"])
                for j in range(1, 4):
                    S.op("dve", lambda e, ch=ch, j=j, pr=pr: e.scalar_tensor_tensor(out=yT[:, ch, :], in0=pr[:, ch, j:j + BT], scalar=cw[:, ch, j:j + 1],
                                                                                 in1=yT[:, ch, :], op0=ALU.mult, op1=ALU.add),
                         reads=[prk, "cw", "yT"], writes=["yT"])
            _act(S, yT[:].rearrange("p a b -> p (a b)"), yT[:].rearrange("p a b -> p (a b)"), AF.Silu, ["yT"], ["yT"])
            for ch in range(4):
                _act(S, sq2[:], yT[:, ch, :], AF.Square, ["yT"], ["sq2"])
                _mm(S, pss[:, :], g.ones_bf, sq2[:], True, True, ["cb", "sq2"], ["pss"])
                _rstd(S, rs2[:], pss[:, :], 1.0, ["pss"], ["rs2"])
                if ch < 2:
                    S.op("dve", lambda e, ch=ch: e.scalar_tensor_tensor(out=qnT[:, ch, :], in0=yT[:, ch, :], scalar=128.0 ** -0.5, in1=rs2[:], op0=ALU.mult, op1=ALU.mult),
                         reads=["yT", "rs2"], writes=["qnT"])
                else:
                    S.op("dve", lambda e, ch=ch: e.tensor_tensor(out=knT[:, ch - 2, :], in0=yT[:, ch, :], in1=rs2[:], op=ALU.mult),
                         reads=["yT", "rs2"], writes=["knT"])
            S.op("act", lambda e: e.copy(out=vT[:], in_=yT[:, 4:6, :]), reads=["yT"], writes=["vT"])
            S.capture = cap_rest
            _mm(S, pi[:, :], wlr[:], lrT[:], True, True, ["wlr", "lrT"], ["pi"])
            _act(S, el[:], pi[:, :], AF.Exp, ["pi", "sm"], ["el"], scale=-1.0, bias=sm[:, 8:9])
            _act(S, ll[:], el[:], AF.Ln, ["el"], ["ll"], bias=1.0)
            for tl in range(4):
                S.op("dve", lambda e, tl=tl: e.tensor_tensor_scan(out=bpos[:, tl * 128:(tl + 1) * 128], data0=ones_f, data1=ll[:, tl * 128:(tl + 1) * 128],
                                                                  initial=0.0, op0=ALU.mult, op1=ALU.add), reads=["cf", "ll"], writes=["bpos"])
            S.capture = None
            S.replay_interleaved([cap_rest, cap_conv])
            ob = oT[blk % 2]
            obk = f"oT{blk % 2}"
            IDb, IDf = cb[:, C_ID, :], cf[:, C_ID, :]
            for tl in range(4):
                ts = slice(tl * 128, (tl + 1) * 128)
                S.skip = "gla" in SKIP
                cap_gla = []
                S.capture = cap_gla
                _act(S, Eb[:], bpos[:, ts], AF.Exp, ["bpos"], ["Eb"], scale=-1.0 / 16)
                _act(S, Einv[:], bpos[:, ts], AF.Exp, ["bpos"], ["Einv"], scale=1.0 / 16)
                S.op("dve", lambda e, tl=tl: e.tensor_scalar(out=nbl[:], in0=bpos[:, tl * 128 + 127:tl * 128 + 128], scalar1=-1.0 / 16, scalar2=None, op0=ALU.mult),
                     reads=["bpos"], writes=["nbl"])
                _act(S, Est[:], bpos[:, ts], AF.Exp, ["bpos", "nbl"], ["Est"], scale=1.0 / 16, bias=nbl[:, 0:1])
                S.op("dve", lambda e, ts=ts: e.scalar_tensor_tensor(out=qdT[:], in0=qaT[:, ts], scalar=128.0 ** -0.5, in1=Eb[:], op0=ALU.mult, op1=ALU.mult),
                     reads=["qaT", "Eb"], writes=["qdT"])
                S.op("dve", lambda e, ts=ts: e.tensor_tensor(out=kinvT[:], in0=kaT[:, ts], in1=Einv[:], op=ALU.mult), reads=["kaT", "Einv"], writes=["kinvT"])
                S.op("dve", lambda e, ts=ts: e.tensor_tensor(out=kstT[:], in0=kaT[:, ts], in1=Est[:], op=ALU.mult), reads=["kaT", "Est"], writes=["kstT"])
                S.op("pe", lambda e: e.transpose(out=ptr[:, 0:128], in_=kstT[:], identity=IDb), reads=["kstT", "cb"], writes=["ptr"])
                S.op("act", lambda e: e.copy(out=kstk[:], in_=ptr[:, 0:128]), reads=["ptr"], writes=["kstk"])
                _mm(S, pss[:, 128:256], kinvT[:], qdT[:], True, True, ["kinvT", "qdT"], ["pss"])
                S.op("dve", lambda e: e.tensor_tensor(out=attT[:], in0=pss[:, 128:256], in1=cf[:, C_MU, :], op=ALU.mult), reads=["pss", "cf"], writes=["attT"])
                for vc in range(2):
                    _mm(S, pss[:, 256 + vc * 128:384 + vc * 128], va[:, tl, vc * 128:(vc + 1) * 128], attT[:], True, False, ["va", "attT"], ["pss"])
                    _mm(S, pss[:, 256 + vc * 128:384 + vc * 128], Sab[:, vc * 128:(vc + 1) * 128], qdT[:], False, True, ["Sab", "qdT"], ["pss"])
                _act(S, sqo[:].rearrange("p a b -> p (a b)"), pss[:, 256:512], AF.Square, ["pss"], ["sqo"])
                for vc in range(2):
                    S.op("dve", lambda e, vc=vc: e.tensor_copy(out=oa[vc][:], in_=pss[:, 256 + vc * 128:384 + vc * 128]), reads=["pss"], writes=[f"oa{vc}"])
                _mm(S, ptm[:, 256:512], kstk[:], va[:, tl, :], True, True, ["kstk", "va"], ["ptm"])
                S.op("dve", lambda e: e.scalar_tensor_tensor(out=Sa[:], in0=Sa[:], scalar=Eb[:, 127:128], in1=ptm[:, 256:512], op0=ALU.mult, op1=ALU.add),
                     reads=["Sa", "Eb", "ptm"], writes=["Sa"])
                S.op("act", lambda e: e.copy(out=Sab[:], in_=Sa[:]), reads=["Sa"], writes=["Sab"])
                for vc in range(2):
                    _mm(S, pss[:, 0:128], g.ones_bf, sqo[:, vc, :], vc == 0, vc == 1, ["cb", "sqo"], ["pss"])
                _rstd(S, rso[:], pss[:, 0:128], 1.0 / 256, ["pss"], ["rso"])
                for vc in range(2):
                    S.op("dve", lambda e, vc=vc: e.tensor_tensor(out=t1[:], in0=oa[vc][:], in1=rso[:], op=ALU.mult), reads=[f"oa{vc}", "rso"], writes=["t1"])
                    S.op("dve", lambda e, vc=vc, ts=ts, ob=ob: e.scalar_tensor_tensor(out=ob[:, vc, ts], in0=t1[:], scalar=sm[:, 1 + vc:2 + vc], in1=sagT[:, vc, ts],
                                                                                    op0=ALU.mult, op1=ALU.mult), reads=["t1", "sm", "sagT"], writes=[obk])
                S.skip = "gdn" in SKIP
                cap_gdn = []
                S.capture = cap_gdn
                HR = range(2)
                hs = lambda h: slice(h * 128, (h + 1) * 128)
                for h in HR:
                    S.op("pe", lambda e, h=h, ts=ts: e.transpose(out=ptr[:, (1 + h) * 128:(2 + h) * 128], in_=knT[:, h, ts], identity=IDb), reads=["knT", "cb"], writes=["ptr"])
                for h in HR:
                    S.op("pe", lambda e, h=h, ts=ts: e.transpose(out=ptr[:, (3 + h) * 128:(4 + h) * 128], in_=vT[:, h, ts], identity=IDb), reads=["vT", "cb"], writes=["ptr"])
                for h in HR:
                    idx = tl * 2 + h
                    ks = slice((1 + h) * 128, (2 + h) * 128)
                    vs = slice((3 + h) * 128, (4 + h) * 128)
                    S.op("act", lambda e, idx=idx, h=h, ks=ks: e.activation(out=kst_d[0][0:64, h, :], in_=ptr[0:64, ks], func=AF.Identity, scale=est[0:64, idx:idx + 1]),
                         reads=["ptr", "est"], writes=["kst_d"])
                    S.op("act", lambda e, idx=idx, h=h, ks=ks: e.activation(out=kst_d[1][64:128, h, :], in_=ptr[64:128, ks], func=AF.Identity, scale=est[64:128, idx:idx + 1]),
                         reads=["ptr", "est"], writes=["kst_d"])
                    S.op("dve", lambda e, idx=idx, h=h, ks=ks: e.tensor_scalar(out=kbg[:, h, :], in0=ptr[:, ks], scalar1=bcoef[:, idx:idx + 1], scalar2=None, op0=ALU.mult),
                         reads=["ptr", "bcoef"], writes=["kbg"])
                    S.op("dve", lambda e, tl=tl, h=h, vs=vs: e.tensor_scalar(out=vb[:, h, :], in0=ptr[:, vs], scalar1=beta[:, tl, h:h + 1], scalar2=None, op0=ALU.mult),
                         reads=["ptr", "beta"], writes=["vb"])
                    S.op("dve", lambda e, tl=tl, h=h: e.tensor_scalar(out=gbc[:, h, :], in0=ones_f, scalar1=gcol[:, tl, h:h + 1], scalar2=None, op0=ALU.mult),
                         reads=["cf", "gcol"], writes=["gbc"])
                for h in HR:
                    _mm(S, pp[1][:, hs(h)], gbc[:, h, :], cf[:, C_TRI, :], True, True, ["gbc", "cf"], ["pp1"])
                for h in HR:
                    idx = tl * 2 + h
                    S.op("dve", lambda e, idx=idx, h=h: e.scalar_tensor_tensor(out=e1[:, h, :], in0=pp[1][:, hs(h)], scalar=gst[:, 0, idx:idx + 1], in1=cf[:, C_BIGC, :],
                                                                            op0=ALU.subtract, op1=ALU.add), reads=["pp1", "gst", "cf"], writes=["e1"])
                _act(S, fl(DC), fl(e1), AF.Exp, ["e1"], ["DC"], scale=-1.0)
                _act(S, fl(EG), pp[1][:, 0:256], AF.Exp, ["pp1"], ["EG"])
                for h in HR:
                    S.op("dve", lambda e, h=h: e.tensor_tensor(out=DSL[:, h, :], in0=DC[:, h, :], in1=cf[:, C_SL, :], op=ALU.mult), reads=["DC", "cf"], writes=["DSL"])
                S.op("dve", lambda e, ts=ts: e.tensor_tensor(out=qd_d[:], in0=qnT[:, :, ts], in1=EG[:], op=ALU.mult), reads=["qnT", "EG"], writes=["qd_d"])
                for h in HR:
                    _mm(S, pp[0][:, hs(h)], knT[:, h, ts], knT[:, h, ts], True, True, ["knT"], ["pp0"])
                for h in HR:
                    _mm(S, pp[0][:, 256 + h * 128:384 + h * 128], qnT[:, h, ts], knT[:, h, ts], True, True, ["qnT", "knT"], ["pp0"])
                for h in HR:
                    S.op("dve", lambda e, tl=tl, h=h: e.scalar_tensor_tensor(out=P[0][:, h, :], in0=pp[0][:, hs(h)], scalar=nbeta[:, tl, h:h + 1], in1=DSL[:, h, :],
                                                                          op0=ALU.mult, op1=ALU.mult), reads=["pp0", "nbeta", "DSL"], writes=["P0"])
                S.op("dve", lambda e: e.tensor_tensor(out=fl(att_d), in0=pp[0][:, 256:512], in1=fl(DC), op=ALU.mult), reads=["pp0", "DC"], writes=["att_d"])
                for h in HR:
                    S.op("pe", lambda e, h=h: e.transpose(out=pp[1][:, 256 + h * 128:384 + h * 128], in_=P[0][:, h, :], identity=IDf), reads=["P0", "cf"], writes=["pp1"])
                S.op("act", lambda e: e.copy(out=fl(PT[0]), in_=pp[1][:, 256:512]), reads=["pp1"], writes=["PT0"])
                for h in HR:
                    S.op("pe", lambda e, h=h: e.transpose(out=ptr[:, (5 + h) * 128:(6 + h) * 128], in_=att_d[:, h, :], identity=IDb), reads=["att_d", "cb"], writes=["ptr"])
                S.op("act", lambda e: e.copy(out=fl(attT_d), in_=ptr[:, 640:896]), reads=["ptr"], writes=["attT_d"])
                for h in HR:
                    S.op("dve", lambda e, h=h: e.tensor_tensor(out=X[0][:, h, :], in0=PT[0][:, h, :], in1=IDf, op=ALU.add), reads=["PT0", "cf"], writes=["X0"])
                cur = 0
                for k in range(5):
                    nxt = 1 - cur
                    for h in HR:
                        _mm(S, ptm[:, hs(h)], PT[cur][:, h, :], P[cur][:, h, :], True, True, [f"PT{cur}", f"P{cur}"], ["ptm"])
                    if k < 4:
                        for h in HR:
                            _mm(S, pi[:, hs(h)], P[cur][:, h, :], PT[cur][:, h, :], True, True, [f"PT{cur}", f"P{cur}"], ["pi"])
                    S.op("act", lambda e, nxt=nxt: e.copy(out=fl(P[nxt]), in_=ptm[:, 0:256]), reads=["ptm"], writes=[f"P{nxt}"])
                    if k < 4:
                        S.op("dve", lambda e, nxt=nxt: e.tensor_copy(out=fl(PT[nxt]), in_=pi[:, 0:256]), reads=["pi"], writes=[f"PT{nxt}"])
                    for h in HR:
                        _mm(S, pg[:, hs(h)], P[nxt][:, h, :], X[cur][:, h, :], True, True, [f"P{nxt}", f"X{cur}"], ["pg"])
                    S.op("dve", lambda e, nxt=nxt, cur=cur: e.tensor_tensor(out=fl(X[nxt]), in0=pg[:, 0:256], in1=fl(X[cur]), op=ALU.add),
                         reads=["pg", f"X{cur}"], writes=[f"X{nxt}"])
                    cur = nxt
                Xf = X[cur]
                xk2 = f"X{cur}"
                for h in HR:
                    _mm(S, pg[:, 256 + h * 128:384 + h * 128], Xf[:, h, :], vb[:, h, :], True, True, [xk2, "vb"], ["pg"])
                S.op("act", lambda e: e.copy(out=fl(u_sb), in_=pg[:, 256:512]), reads=["pg"], writes=["u_sb"])
                for h in HR:
                    _mm(S, pi[:, hs(h)], kbg[:, h, :], Xf[:, h, :], True, True, [xk2, "kbg"], ["pi"])
                S.op("act", lambda e: e.copy(out=fl(wT_sb), in_=pi[:, 0:256]), reads=["pi"], writes=["wT_sb"])
                for hf in range(2):
                    rs_ = slice(hf * 64, hf * 64 + 64)
                    for h in HR:
                        _mm(S, pi[rs_, 256 + h * 128:384 + h * 128], wT_sb[:, h, rs_], Sdb[:, h, :], True, True, ["wT_sb", "Sdb"], ["pi"])
                    S.op("dve", lambda e, rs_=rs_: e.tensor_tensor(out=vnew[rs_, :, :].rearrange("p a b -> p (a b)"), in0=u_sb[rs_, :, :].rearrange("p a b -> p (a b)"),
                                                                   in1=pi[rs_, 256:512], op=ALU.subtract), reads=["u_sb", "pi"], writes=["vnew"])
                    for h in HR:
                        oc = slice(256 + h * 128 + hf * 64, 256 + h * 128 + hf * 64 + 64)
                        _mm(S, pw[:, oc], Sdb[:, h, :], qd_d[:, h, rs_], True, False, ["Sdb", "qd_d"], ["pw"])
                        _mm(S, pw[:, oc], vnew[:, h, :], attT_d[:, h, rs_], False, True, ["vnew", "attT_d"], ["pw"])
                    for h in HR:
                        _mm(S, pw[:, hs(h)], kst_d[hf][:, h, :], vnew[:, h, :], True, True, ["kst_d", "vnew"], ["pw"])
                    for h in HR:
                        idx = tl * 2 + h
                        S.op("dve", lambda e, h=h, hf=hf, idx=idx: e.scalar_tensor_tensor(out=Sd[:, h, :], in0=Sd[:, h, :], scalar=edec[:, hf, idx:idx + 1], in1=pw[:, hs(h)],
                                                                                       op0=ALU.mult, op1=ALU.add), reads=["Sd", "edec", "pw"], writes=["Sd"])
                    S.op("act", lambda e: e.copy(out=fl(Sdb), in_=fl(Sd)), reads=["Sd"], writes=["Sdb"])
                _act(S, fl(sqo_d), pw[:, 256:512], AF.Square, ["pw"], ["sqo_d"])
                for h in HR:
                    _mm(S, pg[:, 256 + h * 128:384 + h * 128], g.ones_bf, sqo_d[:, h, :], True, True, ["cb", "sqo_d"], ["pg"])
                _rstd(S, fl(rso_d), pg[:, 256:512], 1.0 / 128, ["pg"], ["rso_d"])
                S.op("dve", lambda e: e.tensor_tensor(out=fl(t1_d), in0=pw[:, 256:512], in1=fl(rso_d), op=ALU.mult), reads=["pw", "rso_d"], writes=["t1_d"])
                S.op("dve", lambda e, ts=ts, ob=ob: e.scalar_tensor_tensor(out=ob[:, 2:4, ts], in0=t1_d[:], scalar=sm[:, 3:4], in1=sdgT[:, :, ts],
                                                                         op0=ALU.mult, op1=ALU.mult), reads=["t1_d", "sm", "sdgT"], writes=[obk])
                S.capture = None
                S.skip = False
                S.replay_interleaved([cap_gdn, cap_gla])
            S.skip = False
            if not g.fused:
                for j in range(4):
                    S.op("sp", lambda e, j=j, ob=ob, blk=blk: e.dma_start(out=g.d_oT[:, j, blk * BT:(blk + 1) * BT], in_=ob[:, j, :]), reads=[obk], dma_key=obk + "s")
            else:
                psz = min(BT, g.TCH)
                for pc in range(BT // psz):
                    tok0 = blk * BT + pc * psz
                    k, off = tok0 // g.TCH, tok0 % g.TCH
                    for j in range(4):
                        S.op("sp", lambda e, j=j, ob=ob, k=k, off=off, pc=pc: e.dma_start(out=g.d_oT[k, :, j, off:off + psz], in_=ob[:, j, pc * psz:(pc + 1) * psz]),
                             reads=[obk], writes=[f"oTd{k}"], dma_key=obk + "s")
                while g.next_cc < g.NCK and (g.next_cc + 1) * g.TCH <= (blk + 1) * BT:
                    k = g.next_cc
                    S.op("pool", lambda e, k=k: e.collective_compute("AllGather", ALU.bypass, replica_groups=[[0, 1, 2, 3], [4, 5, 6, 7]],
                                                                     ins=[g.d_oT[k].rearrange("p j t -> p (j t)")], outs=[g.d_gath[k].rearrange("q j t -> q (j t)")]),
                         reads=[f"oTd{k}"], writes=[f"gath{k}"], dma_key="cc", dma_inc=1)
                    g.next_cc += 1
        S.barrier()
        S.emit()


def emit_phase2(g, S, nc, SQ, o2_src, o2_reads):
    BT = 256
    NB = SQ // BT
    with ExitStack() as es:
        sb = lambda name, shape, dt: es.enter_context(nc.sbuf_tensor("p2a_" + name, shape, dt))
        pst = lambda name, dt=F32, n=512: es.enter_context(nc.psum_tensor("p2a_" + name, [128, n], dt))
        wm = sb("wm", [128, NCH, 2048], BF16)
        wbg = sb("wbg", [128, NCH, 1024], BF16)
        wbd = sb("wbd", [128, NCH, 1024], BF16)
        wo = sb("wo", [128, NCH, 1024], BF16)
        load_w_bf16(S, wm, "wm", g.d_wm, 2048)
        load_w_bf16(S, wbg, "wbg", g.d_wbg, 1024)
        load_w_bf16(S, wbd, "wbd", g.d_wbd, 1024)
        load_w_bf16(S, wo, "wo", g.d_wo, 1024)
        xt = [sb(f"x2t{i}", [128, NCH, BT], F32) for i in range(2)]
        xr = [sb(f"x2r{i}", [128, NCH, BT], F32) for i in range(2)]
        sq = sb("sq", [128, NCH, BT], BF16)
        rstd = sb("rstd", [128, BT], F32)
        hT = sb("hT", [128, NCH, BT], BF16)
        sg = sb("sg", [128, 16, BT], F32)
        o2 = [sb(f"o2{i}", [128, 16, BT], BF16) for i in range(2)]
        mg = sb("mg", [128, NCH, BT], BF16)
        tmp = sb("tmp", [128, BT], F32)
        zz = sb("zz", [128, NCH, BT], F32)
        pp = [pst(f"pp{i}") for i in range(2)]
        pss = pst("pss")
        def load2a(b_):
            s_ = b_ % 2
            x_, xk = xt[s_], f"x2t{s_}"
            xr_, xrk = xr[s_], f"x2r{s_}"
            o_, ok = o2[s_], f"o2{s_}"
            t0 = b_ * BT
            for c in range(NCH):
                S.op("sp", lambda e, c=c, x_=x_, t0=t0: e.dma_start(out=x_[:, c, :], in_=g.d_xT2[:, c, t0:t0 + BT]), writes=[xk], dma_key=xk)
                S.op("sp", lambda e, c=c, xr_=xr_, t0=t0: e.dma_start(out=xr_[:, c, :], in_=g.d_xT2[:, c, t0:t0 + BT]), writes=[xrk], dma_key=xrk)
            for kind in range(2):
                for c in range(NCH):
                    S.op("sp", lambda e, kind=kind, c=c, o_=o_, t0=t0: e.dma_start(out=o_[:, kind * 8 + c, :], in_=o2_src(e, kind, c, t0, BT)), reads=o2_reads, writes=[ok], dma_key=ok)
        load2a(0)
        for blk in range(NB):
            s_ = blk % 2
            x_, xk = xt[s_], f"x2t{s_}"
            xr_, xrk = xr[s_], f"x2r{s_}"
            o_, ok = o2[s_], f"o2{s_}"
            t0 = blk * BT
            if blk + 1 < NB:
                load2a(blk + 1)
            emit_norm_block(g, S, x_, xk, BT, hT, "hT", sq, pss, rstd, 0, 1)
            for j in range(16):
                p_, pk = pp[j % 2], f"pp{j % 2}"
                for c in range(NCH):
                    _mm(S, p_[:, 0:BT], wm[:, c, j * 128:(j + 1) * 128], hT[:, c, :], c == 0, c == NCH - 1, ["wm", "hT"], [pk])
                _act(S, sg[:, j, :], p_[:, 0:BT], AF.Sigmoid, [pk], ["sg"])
            for oc in range(NCH):
                for kind, w_ in ((0, wbg), (1, wbd)):
                    p_, pk = pp[kind], f"pp{kind}"
                    wk = "wbg" if kind == 0 else "wbd"
                    for c in range(NCH):
                        _mm(S, p_[:, 0:BT], w_[:, c, oc * 128:(oc + 1) * 128], o_[:, kind * 8 + c, :], c == 0, c == NCH - 1, [wk, ok], [pk])
                S.op("dve", lambda e, oc=oc: e.tensor_tensor(out=tmp[:], in0=pp[0][:, 0:BT], in1=sg[:, oc, :], op=ALU.mult), reads=["pp0", "sg"], writes=["tmp"])
                S.op("dve", lambda e, oc=oc: e.tensor_tensor(out=zz[:, oc, :], in0=pp[1][:, 0:BT], in1=sg[:, 8 + oc, :], op=ALU.mult), reads=["pp1", "sg"], writes=["zz"])
                S.op("dve", lambda e, oc=oc: e.tensor_tensor(out=mg[:, oc, :], in0=tmp[:], in1=zz[:, oc, :], op=ALU.add), reads=["tmp", "zz"], writes=["mg"])
            for oc in range(NCH):
                p_, pk = pp[oc % 2], f"pp{oc % 2}"
                for c in range(NCH):
                    _mm(S, p_[:, 0:BT], wo[:, c, oc * 128:(oc + 1) * 128], mg[:, c, :], c == 0, c == NCH - 1, ["wo", "mg"], [pk])
                S.op("dve", lambda e, oc=oc, p_=p_: e.tensor_copy(out=zz[:, oc, :], in_=p_[:, 0:BT]), reads=[pk], writes=["zz"])
            _act(S, sq[:], zz[:], AF.Square, ["zz"], ["sq"])
            for c in range(NCH):
                _mm(S, pss[:, 0:BT], g.ones_bf, sq[:, c, :], c == 0, c == NCH - 1, ["cb", "sq"], ["pss"])
            _rstd(S, rstd[:], pss[:, 0:BT], 1.0 / D, ["pss"], ["rstd"])
            for c in range(NCH):
                S.op("dve", lambda e, c=c: e.tensor_tensor(out=zz[:, c, :], in0=zz[:, c, :], in1=rstd[:], op=ALU.mult), reads=["zz", "rstd"], writes=["zz"])
                S.op("dve", lambda e, c=c, xr_=xr_: e.scalar_tensor_tensor(out=xr_[:, c, :], in0=zz[:, c, :], scalar=g.AB[:, 2, c:c + 1], in1=xr_[:, c, :],
                                                                         op0=ALU.mult, op1=ALU.add), reads=["zz", "AB", xrk], writes=[xrk])
                S.op("sp", lambda e, c=c, xr_=xr_, t0=t0: e.dma_start(out=g.d_x1[:, c, t0:t0 + BT], in_=xr_[:, c, :]), reads=[xrk], writes=["x1d"], dma_key=xrk + "s")
        S.barrier()
        S.emit()
    with ExitStack() as es:
        sb = lambda name, shape, dt: es.enter_context(nc.sbuf_tensor("p2b_" + name, shape, dt))
        pst = lambda name, dt=F32, n=512: es.enter_context(nc.psum_tensor("p2b_" + name, [128, n], dt))
        w1 = sb("w1", [128, NCH, 4096], BF16)
        w2 = sb("w2", [128, 32, 1024], BF16)
        load_w_bf16(S, w1, "w1", g.d_w1, 4096)
        load_w_bf16(S, w2, "w2", g.d_w2, 1024, nrows=32)
        xt = [sb(f"x3t{i}", [128, NCH, BT], F32) for i in range(1)]
        xr = [sb(f"x3r{i}", [128, NCH, BT], F32) for i in range(1)]
        sq = sb("sq", [128, NCH, BT], BF16)
        rstd = sb("rstd", [128, BT], F32)
        hT = sb("hT", [128, NCH, BT], BF16)
        rl = [sb(f"rl{i}", [128, BT], F32) for i in range(2)]
        hid = sb("hid", [128, 32, BT], BF16)
        zz = sb("zz", [128, NCH, BT], F32)
        pp = [pst(f"pp{i}") for i in range(2)]
        pss = pst("pss")
        for blk in range(NB):
            s_ = 0
            x_, xk = xt[s_], f"x3t{s_}"
            xr_, xrk = xr[s_], f"x3r{s_}"
            t0 = blk * BT
            for c in range(NCH):
                S.op("sp", lambda e, c=c, x_=x_, t0=t0: e.dma_start(out=x_[:, c, :], in_=g.d_x1[:, c, t0:t0 + BT]), reads=["x1d"], writes=[xk], dma_key=xk)
                S.op("sp", lambda e, c=c, xr_=xr_, t0=t0: e.dma_start(out=xr_[:, c, :], in_=g.d_x1[:, c, t0:t0 + BT]), reads=["x1d"], writes=[xrk], dma_key=xrk)
            emit_norm_block(g, S, x_, xk, BT, hT, "hT", sq, pss, rstd, 3, 4)
            for fc in range(32):
                p_, pk = pp[fc % 2], f"pp{fc % 2}"
                for c in range(NCH):
                    _mm(S, p_[:, 0:BT], w1[:, c, fc * 128:(fc + 1) * 128], hT[:, c, :], c == 0, c == NCH - 1, ["w1", "hT"], [pk])
                _act(S, rl[fc % 2][:], p_[:, 0:BT], AF.Relu, [pk], [f"rl{fc % 2}"])
                S.op("dve", lambda e, fc=fc: e.tensor_tensor(out=hid[:, fc, :], in0=rl[fc % 2][:], in1=rl[fc % 2][:], op=ALU.mult), reads=[f"rl{fc % 2}"], writes=["hid"])
            for oc in range(NCH):
                p_, pk = pp[oc % 2], f"pp{oc % 2}"
                for fc in range(32):
                    _mm(S, p_[:, 0:BT], w2[:, fc, oc * 128:(oc + 1) * 128], hid[:, fc, :], fc == 0, fc == 31, ["w2", "hid"], [pk])
                S.op("dve", lambda e, oc=oc, p_=p_: e.tensor_copy(out=zz[:, oc, :], in_=p_[:, 0:BT]), reads=[pk], writes=["zz"])
            _act(S, sq[:], zz[:], AF.Square, ["zz"], ["sq"])
            for c in range(NCH):
                _mm(S, pss[:, 0:BT], g.ones_bf, sq[:, c, :], c == 0, c == NCH - 1, ["cb", "sq"], ["pss"])
            _rstd(S, rstd[:], pss[:, 0:BT], 1.0 / D, ["pss"], ["rstd"])
            for c in range(NCH):
                S.op("dve", lambda e, c=c: e.tensor_tensor(out=zz[:, c, :], in0=zz[:, c, :], in1=rstd[:], op=ALU.mult), reads=["zz", "rstd"], writes=["zz"])
                S.op("dve", lambda e, c=c, xr_=xr_: e.scalar_tensor_tensor(out=xr_[:, c, :], in0=zz[:, c, :], scalar=g.AB[:, 5, c:c + 1], in1=xr_[:, c, :],
                                                                         op0=ALU.mult, op1=ALU.add), reads=["zz", "AB", xrk], writes=[xrk])
                S.op("sp", lambda e, c=c, xr_=xr_, t0=t0: e.dma_start(out=g.d_out[:, c, t0:t0 + BT], in_=xr_[:, c, :]), reads=[xrk], dma_key=xrk + "s")
        S.barrier()
        S.emit()


def build_program(SEQ, mode):
    SQ = SEQ // 4
    nc = bass.Bass("TRN2", target_bir_lowering=False)
    g = Ctx()
    din = lambda name, shape, dt=F32: nc.dram_tensor(name, shape, dt, kind="ExternalInput").ap()
    g.d_consts = din("consts", [128, NCONST, 128])
    g.d_cT = din("cT", [128, NCH])
    g.d_adab = din("adab", [128, 48])
    g.d_nw = din("nw", [128, 4, NCH])
    g.d_adaw = din("adaw", [128, NCH, 6144])
    if mode in ("A", "F"):
        g.d_xT = din("xT", [128, NCH, SEQ])
        g.d_wfm = din("wfm", [128, NCH, NFM])
        g.d_wlri = din("wlri", [128, NCH, 16])
        g.d_wtm = din("wtm", [128, NCH, NTM])
        g.d_wlr = din("wlr", [16, 128])
        g.d_sm = din("sm", [128, 8])
        g.d_cw = din("cw", [128, 6, 4])
    if mode == "A":
        g.d_oT = nc.dram_tensor("oT", [128, 4, SEQ], BF16, kind="ExternalOutput").ap()
    if mode == "F":
        g.TCH = min(1024, SQ)
        g.NCK = SEQ // g.TCH
        g.d_oT = nc.dram_tensor("oT", [g.NCK, 128, 4, g.TCH], BF16, kind="Internal").ap()
        g.d_gath = nc.dram_tensor("gath", [g.NCK, 4 * 128, 4, g.TCH], BF16, kind="Internal").ap()
        g.d_o2q = nc.dram_tensor("o2q", [4 * 128, 4, SQ], BF16, kind="Internal").ap()
    if mode in ("B", "F"):
        g.d_xT2 = din("xT2", [128, NCH, SQ])
        g.d_wm = din("wm", [128, NCH, 2048])
        g.d_wbg = din("wbg", [128, NCH, 1024])
        g.d_wbd = din("wbd", [128, NCH, 1024])
        g.d_wo = din("wo", [128, NCH, 1024])
        g.d_w1 = din("w1", [128, NCH, 4096])
        g.d_w2 = din("w2", [128, 32, 1024])
        g.d_x1 = nc.dram_tensor("x1", [128, NCH, SQ], F32, kind="Internal").ap()
        g.d_out = nc.dram_tensor("out", [128, NCH, SQ], F32, kind="ExternalOutput").ap()
    if mode == "B":
        g.d_o2 = din("o2", [128, 2, NCH, SQ], BF16)
    with ExitStack() as es:
        S = Sched(nc, es)
        g.fused = mode == "F"
        g.next_cc = 0
        emit_common(g, S, es, nc, True)
        if mode in ("A", "F"):
            emit_phase1(g, S, nc, SEQ)
        if mode == "F":
            CPQ = SQ // g.TCH
            gathv = g.d_gath.rearrange("(q c) r j t -> q c r j t", c=CPQ)

            def mkcopy(cc):
                def f(e):
                    if g.pidc.get("blk") != S.nblocks:
                        g.pidc = {"blk": S.nblocks, "pid4": e.snap(e.partition_id() % 4)}
                    src = gathv[bass.DynSlice(g.pidc["pid4"], 1), cc, :, :, :].rearrange("q r j t -> (q r) j t")
                    return e.dma_start(out=g.d_o2q[:, :, cc * g.TCH:(cc + 1) * g.TCH], in_=src)
                return f
            g.pidc = {}
            for cc in range(CPQ):
                S.op("sp", mkcopy(cc), reads=[f"gath{k}" for k in range(g.NCK)], writes=["o2q"], dma_key="o2q")

            def o2_src(e, kind, c, t0, n):
                hg, j = c // 2, c % 2
                return g.d_o2q[hg * 128:(hg + 1) * 128, kind * 2 + j, t0:t0 + n]
            emit_phase2(g, S, nc, SQ, o2_src, ["o2q"])
        if mode == "B":
            emit_phase2(g, S, nc, SQ, lambda e, kind, c, t0, n: g.d_o2[:, kind, c, t0:t0 + n], [])
    return nc


def _fm(w):
    r, n = w.shape
    return np.ascontiguousarray(w.reshape(r // 128, 128, n).transpose(1, 0, 2))


def _consts():
    p = np.arange(128)[:, None]
    f = np.arange(128)[None, :]
    same = (p // 64) == (f // 64)
    c = np.zeros((NCONST, 128, 128), np.float32)
    c[C_ID] = (p == f)
    c[C_MU] = (p <= f)
    c[C_SL] = same & (f < p)
    c[C_BIGC] = np.where(same & (f <= p), 0.0, BIG)
    c[C_TRI] = same & (p <= f)
    c[C_BLK] = same
    c[C_SEL0] = (p < 64) & (f >= 0)
    c[C_SEL1] = (p >= 64) & (f >= 0)
    c[C_ONES] = 1.0
    return np.ascontiguousarray(c.transpose(1, 0, 2))


def host_inputs(inp, SEQ):
    x = inp["x"]
    w_in = inp["w_in"][0]
    maps1, maps2 = [], []
    consts = _consts()
    adaw = _fm(inp["ada_w"][0])
    nw = np.stack([inp[k][0].reshape(8, 128).T for k in ("pre_mix_w", "post_mix_w", "pre_mlp_w", "post_mlp_w")], axis=1)
    adab = np.ascontiguousarray(inp["ada_b"][0].reshape(48, 128).T)
    offs = np.cumsum([0, 512, 512, 1024, 1024, 16, 3072, 1024, 8, 8, 1024, 1024])
    o_aq, o_ak, o_av, o_ag, o_lr, o_dqkv, o_dg, o_db, o_da, o_ma, o_md = offs[:11]
    SQ = SEQ // 4
    xTs = [np.ascontiguousarray(x[b, :SEQ].T.reshape(8, 128, SEQ).transpose(1, 0, 2)) for b in range(x.shape[0])]
    for r in range(8):
        b, hg = r // 4, r % 4
        common = {"consts": consts, "cT": np.ascontiguousarray(inp["c"][b].reshape(8, 128).T), "adab": adab,
                  "nw": np.ascontiguousarray(nw), "adaw": adaw}
        cols = []
        cols += list(range(o_aq + hg * 128, o_aq + (hg + 1) * 128))
        cols += list(range(o_ak + hg * 128, o_ak + (hg + 1) * 128))
        cols += list(range(o_ag + hg * 256, o_ag + (hg + 1) * 256))
        for part in range(3):
            cols += list(range(o_dqkv + part * 1024 + hg * 256, o_dqkv + part * 1024 + (hg + 1) * 256))
        cols += list(range(o_dg + hg * 256, o_dg + (hg + 1) * 256))
        tcols = list(range(o_av + hg * 256, o_av + (hg + 1) * 256)) + [o_db + 2 * hg, o_db + 2 * hg + 1, o_da + 2 * hg, o_da + 2 * hg + 1]
        sm = np.zeros((128, 8), np.float32)
        sm[:, 0] = inp["gla_b_lr"][0][hg * 128:(hg + 1) * 128]
        sm[:, 1:3] = inp["gla_onorm_w"][0].reshape(2, 128).T
        sm[:, 3] = inp["gdn_onorm_w"][0]
        sm[:, 4:6] = inp["gdn_a_log"][0][None, 2 * hg:2 * hg + 2]
        sm[:, 6:8] = inp["gdn_dt_bias"][0][None, 2 * hg:2 * hg + 2]
        cwfull = inp["gdn_conv_w"][0]
        cw = np.zeros((128, 6, 4), np.float32)
        for part in range(3):
            for h in range(2):
                c0 = part * 1024 + (2 * hg + h) * 128
                cw[:, part * 2 + h, :] = cwfull[:, c0:c0 + 128].T
        m1 = dict(common)
        m1.update({"xT": xTs[b], "wfm": _fm(w_in[:, cols]), "wlri": _fm(w_in[:, o_lr:o_lr + 16]), "wtm": _fm(np.concatenate([w_in[:, tcols], np.zeros((D, NTM - 260), np.float32)], axis=1)),
                   "wlr": np.ascontiguousarray(inp["gla_w_lr"][0][:, hg * 128:(hg + 1) * 128]), "sm": sm, "cw": cw})
        maps1.append(m1)
        m2 = dict(common)
        m2.update({"xT2": np.ascontiguousarray(xTs[b][:, :, hg * SQ:(hg + 1) * SQ]),
                   "wm": _fm(w_in[:, o_ma:o_ma + 2048]), "wbg": _fm(inp["w_branch_gla"][0]), "wbd": _fm(inp["w_branch_gdn"][0]),
                   "wo": _fm(inp["w_out"][0]), "w1": _fm(inp["mlp_w1"][0]), "w2": _fm(inp["mlp_w2"][0])})
        maps2.append(m2)
    return maps1, maps2


_CACHE = {}


def _prog(SEQ, mode):
    if (SEQ, mode) not in _CACHE:
        _CACHE[(SEQ, mode)] = build_program(SEQ, mode)
    return _CACHE[(SEQ, mode)]


def run_unfused(inp, SEQ):
    maps1, maps2 = host_inputs(inp, SEQ)
    SQ = SEQ // 4
    resA = run_bass_kernel_spmd(_prog(SEQ, "A"), maps1, core_ids=list(range(8)))
    oTs = [np.asarray(r["oT"]) for r in resA.results]
    for r in range(8):
        b, tq = r // 4, r % 4
        o2 = np.zeros((128, 2, 8, SQ), oTs[0].dtype)
        for hg in range(4):
            src = oTs[b * 4 + hg][:, :, tq * SQ:(tq + 1) * SQ]
            o2[:, 0, hg * 2:hg * 2 + 2] = src[:, 0:2]
            o2[:, 1, hg * 2:hg * 2 + 2] = src[:, 2:4]
        maps2[r]["o2"] = o2
    resB = run_bass_kernel_spmd(_prog(SEQ, "B"), maps2, core_ids=list(range(8)))
    B = inp["x"].shape[0]
    out = np.zeros((B, SEQ, D), np.float32)
    for r in range(8):
        b, tq = r // 4, r % 4
        o = np.asarray(resB.results[r]["out"])
        out[b, tq * SQ:(tq + 1) * SQ, :] = o.transpose(2, 1, 0).reshape(SQ, D)
    return out, oTs


def run_fused(inp, SEQ):
    maps1, maps2 = host_inputs(inp, SEQ)
    SQ = SEQ // 4
    maps = []
    for r in range(8):
        m = dict(maps1[r])
        m.update(maps2[r])
        maps.append(m)
    res = run_bass_kernel_spmd(_prog(SEQ, "F"), maps, core_ids=list(range(8)))
    B = inp["x"].shape[0]
    out = np.zeros((B, SEQ, D), np.float32)
    for r in range(8):
        b, tq = r // 4, r % 4
        o = np.asarray(res.results[r]["out"])
        out[b, tq * SQ:(tq + 1) * SQ, :] = o.transpose(2, 1, 0).reshape(SQ, D)
    return out


def kernel(**inputs):
    inp = {k: np.asarray(v) for k, v in inputs.items()}
    return run_fused(inp, inp["x"].shape[1])
```

```python
import numpy as np
from contextlib import ExitStack
import concourse.bass as bass
import concourse.mybir as mybir
from concourse.bass_utils import run_bass_kernel_spmd

F32 = mybir.dt.float32
BF16 = mybir.dt.bfloat16
AF = mybir.ActivationFunctionType
ALU = mybir.AluOpType

D = 1024
NCH = 8
EPS = 1e-6
NFM = 1536
NTM = 288
BIG = 30000.0
C_ID, C_MU, C_SL, C_BIGC, C_TRI, C_BLK, C_SEL0, C_SEL1, C_ONES = range(9)
NCONST = 9


import os
SKIP = set(os.environ.get("KSKIP", "").split(","))


SAME_ENG_SYNC = os.environ.get("KSAME", "1") == "1"


class Sched:
    ENGS = ("pe", "act", "dve", "pool", "sp")

    def __init__(self, nc, es):
        self.nc = nc
        self.es = es
        self.sem = {e: es.enter_context(nc.semaphore("s_" + e)) for e in self.ENGS}
        self.cnt = {e: 0 for e in self.ENGS}
        self.prog = {e: [] for e in self.ENGS}
        self.waited = {e: {} for e in self.ENGS}
        self.lastw = {}
        self.readers = {}
        self.dsem = {}
        self.dcnt = {}

    def _need(self, eng, waits, tok):
        if tok is None:
            return
        kind, key, val = tok
        if kind == "e" and key == eng and (eng == "pe" or not SAME_ENG_SYNC):
            return
        w = self.waited[eng]
        if w.get((kind, key), 0) >= val:
            return
        w[(kind, key)] = val
        waits[(kind, key)] = max(waits.get((kind, key), 0), val)

    skip = False

    def stage(self, name):
        self.skip = name in SKIP

    capture = None

    def op(self, eng, fn, reads=(), writes=(), dma_key=None, dma_inc=16):
        if self.skip:
            return None
        if self.capture is not None:
            self.capture.append((eng, fn, tuple(reads), tuple(writes), dma_key, dma_inc))
            return None
        banks = set()
        for k in list(reads) + list(writes):
            for pfx in ("pp0", "pp1", "pss", "ptm", "pg", "pi", "pw", "ptr", "mps"):
                if k == pfx or k.startswith(pfx + ".") or (pfx == "ptr" and k.startswith("ptr")):
                    banks.add("B:" + pfx)
        writes = list(writes) + sorted(banks)
        waits = {}
        for k in reads:
            self._need(eng, waits, self.lastw.get(k))
        for k in writes:
            self._need(eng, waits, self.lastw.get(k))
            for t in self.readers.get(k, ()):
                self._need(eng, waits, t)
        if dma_key is None:
            self.cnt[eng] += 1
            tok = ("e", eng, self.cnt[eng])
        else:
            if dma_key not in self.dsem:
                self.dsem[dma_key] = self.es.enter_context(self.nc.semaphore("d_" + str(dma_key)))
                self.dcnt[dma_key] = 0
            self.dcnt[dma_key] += dma_inc
            tok = ("d", dma_key, self.dcnt[dma_key])
        self.prog[eng].append((waits, fn, tok, dma_inc))
        for k in reads:
            self.readers.setdefault(k, []).append(tok)
        for k in writes:
            self.lastw[k] = tok
            self.readers[k] = []
        return tok

    def replay_interleaved(self, streams):
        pos = [0] * len(streams)
        total = max(len(st) for st in streams)
        for step in range(total):
            for i, st in enumerate(streams):
                upto = (step + 1) * len(st) // total
                while pos[i] < upto:
                    self.op(*st[pos[i]])
                    pos[i] += 1

    def barrier(self, engs=None):
        toks = [("e", e, self.cnt[e]) for e in self.ENGS if self.cnt[e] > 0]
        toks += [("d", k, v) for k, v in self.dcnt.items()]
        for e in (engs or self.ENGS):
            waits = {}
            for t in toks:
                self._need(e, waits, t)
            if waits:
                self.prog[e].append((waits, None, None, 0))

    nblocks = 0

    def emit(self):
        self.nblocks += 1
        nc = self.nc
        prog = self.prog
        self.prog = {e: [] for e in self.ENGS}
        with nc.Block() as block:
            def body(e):
                def f(engine):
                    for waits, fn, tok, dinc in prog[e]:
                        for (kind, key), val in waits.items():
                            s = self.sem[key] if kind == "e" else self.dsem[key]
                            engine.wait_ge(s, val)
                        if fn is None:
                            continue
                        ins = fn(engine)
                        if tok[0] == "e":
                            ins.then_inc(self.sem[tok[1]], 1)
                        else:
                            ins.then_inc(self.dsem[tok[1]], dinc)
                return f
            block.tensor(body("pe"))
            block.scalar(body("act"))
            block.vector(body("dve"))
            block.gpsimd(body("pool"))
            block.sync(body("sp"))


class Ctx:
    pass


def _mm(S, out, lhsT, rhs, start, stop, reads, writes):
    S.op("pe", lambda e: e.matmul(out, lhsT=lhsT, rhs=rhs, start=start, stop=stop), reads=reads, writes=writes)


def _act(S, out, in_, func, reads, writes, scale=None, bias=None):
    kw = {}
    if scale is not None:
        kw["scale"] = scale
    if bias is not None:
        kw["bias"] = bias
    S.op("act", lambda e: e.activation(out=out, in_=in_, func=func, **kw), reads=reads, writes=writes)


def _rstd(S, out, ss, inv_n, reads, writes):
    _act(S, out, ss, AF.Ln, reads, writes, scale=inv_n, bias=EPS)
    _act(S, out, out, AF.Exp, writes, writes, scale=-0.5)


def emit_common(g, S, es, nc, need_mix):
    sb = lambda name, shape, dt: es.enter_context(nc.sbuf_tensor("cm_" + name, shape, dt))
    g.cf = sb("cf", [128, NCONST, 128], F32)
    g.cb = sb("cb", [128, NCONST, 128], BF16)
    S.op("sp", lambda e: e.dma_start(out=g.cf[:], in_=g.d_consts), writes=["cf"], dma_key="cf")
    S.op("dve", lambda e: e.tensor_copy(out=g.cb[:], in_=g.cf[:]), reads=["cf"], writes=["cb"])
    g.ones_bf = g.cb[:, C_ONES, :]
    cT = sb("cT", [128, NCH], F32)
    scT = sb("scT", [128, NCH], F32)
    adab = sb("adab", [128, 48], F32)
    nw = sb("nw", [128, 4, NCH], F32)
    mod = sb("mod", [128, 48], F32)
    S.op("sp", lambda e: e.dma_start(out=cT[:], in_=g.d_cT), writes=["cT"], dma_key="s_cT")
    S.op("sp", lambda e: e.dma_start(out=adab[:], in_=g.d_adab), writes=["adab"], dma_key="s_adab")
    S.op("sp", lambda e: e.dma_start(out=nw[:], in_=g.d_nw), writes=["nw"], dma_key="s_nw")
    _act(S, scT[:], cT[:], AF.Silu, ["cT"], ["scT"])
    S.stage("ada")
    with ExitStack() as es2:
        adaw = [es2.enter_context(nc.sbuf_tensor(f"cm_adaw{i}", [128, NCH, 1024], F32)) for i in range(2)]
        mps = es2.enter_context(nc.psum_tensor("cm_mps", [128, 512], F32))
        for k in range(6):
            aw = adaw[k % 2]
            key = f"adaw{k % 2}"
            for c in range(NCH):
                S.op("sp", lambda e, aw=aw, c=c, k=k: e.dma_start(out=aw[:, c, :], in_=g.d_adaw[:, c, k * 1024:(k + 1) * 1024]),
                     writes=[key], dma_key=key)
            for oc in range(NCH):
                col = k * 8 + oc
                for c in range(NCH):
                    _mm(S, mps[:, col:col + 1], aw[:, c, oc * 128:(oc + 1) * 128], scT[:, c:c + 1], c == 0, c == NCH - 1,
                        [key, "scT"], ["mps"])
        S.stage("none")
        S.op("dve", lambda e: e.tensor_tensor(out=mod[:], in0=mps[:, 0:48], in1=adab[:], op=ALU.add), reads=["mps", "adab"], writes=["mod"])
        S.barrier()
        S.emit()
    g.mod = mod
    g.AB = sb("AB", [128, 6, NCH], F32)
    def mk(idx, scale_k, w_k, plus1):
        if plus1:
            S.op("dve", lambda e: e.scalar_tensor_tensor(out=g.AB[:, idx, :], in0=mod[:, scale_k * 8:scale_k * 8 + 8], scalar=1.0,
                                                         in1=nw[:, w_k, :], op0=ALU.add, op1=ALU.mult), reads=["mod", "nw"], writes=["AB"])
        else:
            S.op("dve", lambda e: e.tensor_tensor(out=g.AB[:, idx, :], in0=mod[:, scale_k * 8:scale_k * 8 + 8], in1=nw[:, w_k, :], op=ALU.mult),
                 reads=["mod", "nw"], writes=["AB"])
    mk(0, 1, 0, True)
    S.op("dve", lambda e: e.tensor_copy(out=g.AB[:, 1, :], in_=mod[:, 0:8]), reads=["mod"], writes=["AB"])
    mk(2, 2, 1, False)
    mk(3, 4, 2, True)
    S.op("dve", lambda e: e.tensor_copy(out=g.AB[:, 4, :], in_=mod[:, 24:32]), reads=["mod"], writes=["AB"])
    mk(5, 5, 3, False)


def emit_norm_block(g, S, xt, xkey, n, hT, hkey, sq, ssps, rstd, ia, ib, tdst=None, tkey=None):
    _act(S, sq[:, :, 0:n], xt[:, :, 0:n], AF.Square, [xkey], ["sq"])
    for c in range(NCH):
        _mm(S, ssps[:, 0:n], g.ones_bf, sq[:, c, 0:n], c == 0, c == NCH - 1, ["cb", "sq"], ["ssps"])
    _rstd(S, rstd[:, 0:n], ssps[:, 0:n], 1.0 / D, ["ssps"], ["rstd"])
    if tdst is None:
        tdst, tkey = xt, xkey
    for c in range(NCH):
        S.op("dve", lambda e, c=c: e.tensor_tensor(out=tdst[:, c, 0:n], in0=xt[:, c, 0:n], in1=rstd[:, 0:n], op=ALU.mult),
             reads=[xkey, "rstd"], writes=[tkey])
    for c in range(NCH):
        _act(S, hT[:, c, 0:n], tdst[:, c, 0:n], AF.Identity, [tkey, "AB"], [hkey], scale=g.AB[:, ia, c:c + 1], bias=g.AB[:, ib, c:c + 1])


def load_w_bf16(S, dst, dkey, src, ncols, nrows=NCH):
    for c in range(nrows):
        for c0 in range(0, ncols, 2048):
            c1 = min(ncols, c0 + 2048)
            S.op("pool", lambda e, c=c, c0=c0, c1=c1: e.dma_start(out=dst[:, c, c0:c1], in_=src[:, c, c0:c1]), writes=[dkey], dma_key=dkey)


def emit_phase1(g, S, nc, SEQ):
    BT = 512
    NB = SEQ // BT
    with ExitStack() as es:
        sb = lambda name, shape, dt: es.enter_context(nc.sbuf_tensor("p1_" + name, shape, dt))
        pst = lambda name, dt=F32, n=512: es.enter_context(nc.psum_tensor("p1_" + name, [128, n], dt))
        cf, cb = g.cf, g.cb
        wfm = sb("wfm", [128, NCH, NFM], BF16)
        wlri = sb("wlri", [128, NCH, 16], BF16)
        wtm = sb("wtm", [128, NCH, NTM], BF16)
        load_w_bf16(S, wfm, "wfm", g.d_wfm, NFM)
        load_w_bf16(S, wlri, "wlri", g.d_wlri, 16)
        load_w_bf16(S, wtm, "wtm", g.d_wtm, NTM)
        wlr = sb("wlr", [16, 128], F32)
        sm = sb("sm", [128, 16], F32)
        cw = sb("cw", [128, 6, 4], F32)
        S.op("sp", lambda e: e.dma_start(out=wlr[:], in_=g.d_wlr), writes=["wlr"], dma_key="s_wlr")
        S.op("sp", lambda e: e.dma_start(out=sm[:, 0:8], in_=g.d_sm), writes=["sm"], dma_key="s_sm")
        S.op("sp", lambda e: e.dma_start(out=cw[:], in_=g.d_cw), writes=["cw"], dma_key="s_cw")
        S.op("dve", lambda e: e.tensor_scalar(out=sm[:, 8:9], in0=sm[:, 0:1], scalar1=-1.0, scalar2=None, op0=ALU.mult), reads=["sm"], writes=["sm"])
        _act(S, sm[:, 9:11], sm[:, 4:6], AF.Exp, ["sm"], ["sm"])
        S.op("dve", lambda e: e.tensor_scalar(out=sm[:, 9:11], in0=sm[:, 9:11], scalar1=-1.0, scalar2=None, op0=ALU.mult), reads=["sm"], writes=["sm"])
        xt = [sb(f"xt{i}", [128, NCH, BT], F32) for i in range(2)]
        sq = sb("sq", [128, NCH, BT], BF16)
        rstd = sb("rstd", [128, BT], F32)
        hT = sb("hT", [128, NCH, BT], BF16)
        qaT = sb("qaT", [128, BT], F32)
        kaT = sb("kaT", [128, BT], F32)
        lrT = sb("lrT", [16, BT], F32)
        sagT = sb("sagT", [128, 2, BT], F32)
        sdgT = sb("sdgT", [128, 2, BT], F32)
        pre = [sb(f"pre{i}", [128, 6, BT + 3], F32) for i in range(2)]
        yT = sb("yT", [128, 6, BT], F32)
        sq2 = sb("sq2", [128, BT], BF16)
        rs2 = sb("rs2", [128, BT], F32)
        qnT = sb("qnT", [128, 2, BT], BF16)
        knT = sb("knT", [128, 2, BT], BF16)
        vT = sb("vT", [128, 2, BT], BF16)
        va = sb("va", [128, 4, 256], BF16)
        graw = sb("graw", [128, 4, 4], F32)
        gcol = sb("gcol", [128, 4, 2], F32)
        beta = sb("beta", [128, 4, 2], F32)
        nbeta = sb("nbeta", [128, 4, 2], F32)
        gst = sb("gst", [128, 4, 8], F32)
        egc = sb("egc", [128, 8], F32)
        est = sb("est", [128, 8], F32)
        edec = sb("edec", [128, 2, 8], F32)
        bcoef = sb("bcoef", [128, 8], F32)
        el = sb("el", [128, BT], F32)
        ll = sb("ll", [128, BT], F32)
        bpos = sb("bpos", [128, BT], F32)
        ones_f = cf[:, C_ONES, :]
        Eb = sb("Eb", [128, 128], F32)
        Einv = sb("Einv", [128, 128], F32)
        Est = sb("Est", [128, 128], F32)
        nbl = sb("nbl", [128, 1], F32)
        qdT = sb("qdT", [128, 128], BF16)
        kinvT = sb("kinvT", [128, 128], BF16)
        kstT = sb("kstT", [128, 128], BF16)
        kstk = sb("kstk", [128, 128], BF16)
        attT = sb("attT", [128, 128], BF16)
        Sa = sb("Sa", [128, 256], F32)
        Sab = sb("Sab", [128, 256], BF16)
        sqo = sb("sqo", [128, 2, 128], BF16)
        rso = sb("rso", [128, 128], F32)
        t1 = sb("t1", [128, 128], F32)
        oT = [sb(f"oT{i}", [128, 4, BT], BF16) for i in range(2)]
        oa = [sb(f"oa{i}", [128, 128], F32) for i in range(2)]
        H2 = lambda name, dt: sb(name, [128, 2, 128], dt)
        kst_dd = [[H2(f"kst_d{sl}_{i}", BF16) for i in range(2)] for sl in range(2)]
        kbg = H2("kbg", F32)
        vb = H2("vb", F32)
        gbc = H2("gbc", F32)
        e1 = H2("e1", F32)
        DC = H2("DC", F32)
        DSL = H2("DSL", F32)
        EG = H2("EG", F32)
        P = [H2(f"P{i}", F32) for i in range(2)]
        PT = [H2(f"PT{i}", F32) for i in range(2)]
        X = [H2(f"X{i}", F32) for i in range(2)]
        att_d = H2("att_d", BF16)
        attT_dd = [H2(f"attT_d{sl}", BF16) for sl in range(2)]
        qd_dd = [H2(f"qd_d{sl}", BF16) for sl in range(2)]
        u_sbd = [H2(f"u_sb{sl}", F32) for sl in range(2)]
        wT_sbd = [H2(f"wT_sb{sl}", BF16) for sl in range(2)]
        vnew = H2("vnew", BF16)
        Sd = H2("Sd", F32)
        Sdb = H2("Sdb", BF16)
        sqo_d = H2("sqo_d", BF16)
        rso_d = H2("rso_d", F32)
        t1_d = H2("t1_d", F32)
        fl = lambda t: t[:].rearrange("p a b -> p (a b)")
        pp = [pst(f"pp{i}") for i in range(2)]
        pss = pst("pss")
        ptm = pst("ptm")
        pg = pst("pg")
        pi = pst("pi")
        pw = pst("pw")
        ptr = pst("ptr", BF16, 1024)
        for t in (Sa, Sab):
            S.op("dve", lambda e, t=t: e.memset(t[:], 0.0), writes=["Sa", "Sab"])
        S.op("dve", lambda e: e.memset(Sd[:], 0.0), writes=["Sd"])
        S.op("dve", lambda e: e.memset(Sdb[:], 0.0), writes=["Sdb"])
        for i in range(2):
            S.op("dve", lambda e, i=i: e.memset(pre[i][:, :, 0:3], 0.0), writes=[f"pre{i}"])
            for sl in range(2):
                S.op("dve", lambda e, i=i, sl=sl: e.memset(kst_dd[sl][i][:], 0.0), writes=[f"kst_d{sl}"])
        S.op("dve", lambda e: e.memset(vnew[:], 0.0), writes=["vnew"])

        def load_x(b_):
            for c in range(NCH):
                S.op("sp", lambda e, c=c, b_=b_: e.dma_start(out=xt[b_ % 2][:, c, :], in_=g.d_xT[:, c, b_ * BT:(b_ + 1) * BT]),
                     writes=[f"xt{b_ % 2}"], dma_key=f"xt{b_ % 2}")
        load_x(0)
        for blk in range(NB):
            xs = blk % 2
            xk = f"xt{xs}"
            x_ = xt[xs]
            if blk + 1 < NB:
                load_x(blk + 1)
            S.stage("norm")
            emit_norm_block(g, S, x_, xk, BT, hT, "hT", sq, pss, rstd, 0, 1)
            S.stage("proj")
            pr = pre[blk % 2]
            prk = f"pre{blk % 2}"
            prp = pre[(blk + 1) % 2]
            prpk = f"pre{(blk + 1) % 2}"
            if blk > 0:
                S.op("act", lambda e, pr=pr, prp=prp: e.copy(out=pr[:, :, 0:3], in_=prp[:, :, BT:BT + 3]), reads=[prpk], writes=[prk])
            cap_rest, cap_conv = [], []
            for jn, j in enumerate((4, 5, 6, 7, 8, 9, 0, 1, 2, 3, 10, 11)):
                if jn == 6:
                    S.capture = cap_rest
                p_ = pp[jn % 2]
                pk = f"pp{jn % 2}"
                for c in range(NCH):
                    _mm(S, p_[:, :], wfm[:, c, j * 128:(j + 1) * 128], hT[:, c, :], c == 0, c == NCH - 1, ["wfm", "hT"], [pk])
                if j == 0:
                    S.op("act", lambda e, p_=p_: e.copy(out=qaT[:], in_=p_[:, :]), reads=[pk], writes=["qaT"])
                elif j == 1:
                    S.op("dve", lambda e, p_=p_: e.tensor_copy(out=kaT[:], in_=p_[:, :]), reads=[pk], writes=["kaT"])
                elif j in (2, 3):
                    _act(S, sagT[:, j - 2, :], p_[:, :], AF.Silu, [pk], ["sagT"])
                elif j < 10:
                    S.op("dve", lambda e, p_=p_, j=j, pr=pr: e.tensor_copy(out=pr[:, j - 4, 3:3 + BT], in_=p_[:, :]), reads=[pk], writes=[prk])
                else:
                    _act(S, sdgT[:, j - 10, :], p_[:, :], AF.Silu, [pk], ["sdgT"])
            for c in range(NCH):
                _mm(S, pp[0][0:16, :], wlri[:, c, :], hT[:, c, :], c == 0, c == NCH - 1, ["wlri", "hT"], ["pp0"])
            S.op("dve", lambda e: e.tensor_copy(out=lrT[:], in_=pp[0][0:16, :]), reads=["pp0"], writes=["lrT"])
            S.stage("tm")
            for tl in range(4):
                for c in range(NCH):
                    _mm(S, ptm[:, 0:NTM], hT[:, c, tl * 128:(tl + 1) * 128], wtm[:, c, :], c == 0, c == NCH - 1, ["hT", "wtm"], ["ptm"])
                S.stage("tm_a")
                S.op("act", lambda e, tl=tl: e.copy(out=va[:, tl, :], in_=ptm[:, 0:256]), reads=["ptm"], writes=["va"])
                S.stage("tm_d")
                S.op("dve", lambda e, tl=tl: e.tensor_copy(out=graw[:, tl, :], in_=ptm[:, 256:260]), reads=["ptm"], writes=["graw"])
                S.stage("tm")
            S.stage("gates")
            _act(S, beta[:], graw[:, :, 0:2], AF.Sigmoid, ["graw"], ["beta"])
            S.op("dve", lambda e: e.tensor_scalar(out=nbeta[:], in0=beta[:], scalar1=-1.0, scalar2=None, op0=ALU.mult), reads=["beta"], writes=["nbeta"])
            for h in range(2):
                _act(S, gcol[:, :, h], graw[:, :, 2 + h], AF.Exp, ["graw", "sm"], ["gcol"], bias=sm[:, 6 + h:7 + h])
                _act(S, gcol[:, :, h], gcol[:, :, h], AF.Ln, ["gcol"], ["gcol"], bias=1.0)
                S.op("dve", lambda e, h=h: e.tensor_scalar(out=gcol[:, :, h], in0=gcol[:, :, h], scalar1=sm[:, 9 + h:10 + h], scalar2=None, op0=ALU.mult),
                     reads=["gcol", "sm"], writes=["gcol"])
            gflat = gcol[:].rearrange("p a b -> p (a b)")
            for i, cidx in enumerate((C_TRI, C_BLK, C_SEL0, C_SEL1)):
                _mm(S, pg[:, 384 + 8 * i:392 + 8 * i], cf[:, cidx, :], gflat, True, True, ["cf", "gcol"], ["pg.m"])
            S.op("dve", lambda e: e.tensor_copy(out=gst[:].rearrange("p a b -> p (a b)"), in_=pg[:, 384:416]), reads=["pg.m"], writes=["gst"])
            _act(S, egc[:], gst[:, 0, :], AF.Exp, ["gst"], ["egc"])
            S.op("dve", lambda e: e.tensor_tensor(out=est[:], in0=gst[:, 1, :], in1=gst[:, 0, :], op=ALU.subtract), reads=["gst"], writes=["est"])
            _act(S, est[:], est[:], AF.Exp, ["est"], ["est"])
            _act(S, edec[:], gst[:, 2:4, :], AF.Exp, ["gst"], ["edec"])
            S.op("dve", lambda e: e.tensor_tensor(out=bcoef[:], in0=egc[:], in1=beta[:].rearrange("p a b -> p (a b)"), op=ALU.mult),
                 reads=["egc", "beta"], writes=["bcoef"])
            S.capture = cap_conv
            for ch in range(6):
                S.op("dve", lambda e, ch=ch, pr=pr: e.tensor_scalar(out=yT[:, ch, :], in0=pr[:, ch, 0:BT], scalar1=cw[:, ch, 0:1], scalar2=None, op0=ALU.mult),
                     reads=[prk, "cw"], writes=["yT"])
                for j in range(1, 4):
                    S.op("dve", lambda e, ch=ch, j=j, pr=pr: e.scalar_tensor_tensor(out=yT[:, ch, :], in0=pr[:, ch, j:j + BT], scalar=cw[:, ch, j:j + 1],
                                                                                 in1=yT[:, ch, :], op0=ALU.mult, op1=ALU.add),
                         reads=[prk, "cw", "yT"], writes=["yT"])
            _act(S, yT[:].rearrange("p a b -> p (a b)"), yT[:].rearrange("p a b -> p (a b)"), AF.Silu, ["yT"], ["yT"])
            for ch in range(4):
                _act(S, sq2[:], yT[:, ch, :], AF.Square, ["yT"], ["sq2"])
                _mm(S, pss[:, :], g.ones_bf, sq2[:], True, True, ["cb", "sq2"], ["pss"])
                _rstd(S, rs2[:], pss[:, :], 1.0, ["pss"], ["rs2"])
                if ch < 2:
                    S.op("dve", lambda e, ch=ch: e.scalar_tensor_tensor(out=qnT[:, ch, :], in0=yT[:, ch, :], scalar=128.0 ** -0.5, in1=rs2[:], op0=ALU.mult, op1=ALU.mult),
                         reads=["yT", "rs2"], writes=["qnT"])
                else:
                    S.op("dve", lambda e, ch=ch: e.tensor_tensor(out=knT[:, ch - 2, :], in0=yT[:, ch, :], in1=rs2[:], op=ALU.mult),
                         reads=["yT", "rs2"], writes=["knT"])
            S.op("act", lambda e: e.copy(out=vT[:], in_=yT[:, 4:6, :]), reads=["yT"], writes=["vT"])
            S.capture = cap_rest
            _mm(S, pi[:, :], wlr[:], lrT[:], True, True, ["wlr", "lrT"], ["pi"])
            _act(S, el[:], pi[:, :], AF.Exp, ["pi", "sm"], ["el"], scale=-1.0, bias=sm[:, 8:9])
            _act(S, ll[:], el[:], AF.Ln, ["el"], ["ll"], bias=1.0)
            for tl in range(4):
                S.op("dve", lambda e, tl=tl: e.tensor_tensor_scan(out=bpos[:, tl * 128:(tl + 1) * 128], data0=ones_f, data1=ll[:, tl * 128:(tl + 1) * 128],
                                                                  initial=0.0, op0=ALU.mult, op1=ALU.add), reads=["cf", "ll"], writes=["bpos"])
            S.capture = None
            S.replay_interleaved([cap_rest, cap_conv])
            ob = oT[blk % 2]
            obk = f"oT{blk % 2}"
            IDb, IDf = cb[:, C_ID, :], cf[:, C_ID, :]
            for tl in range(4):
                ts = slice(tl * 128, (tl + 1) * 128)
                S.skip = "gla" in SKIP
                cap_gla = []
                S.capture = cap_gla
                _act(S, Eb[:], bpos[:, ts], AF.Exp, ["bpos"], ["Eb"], scale=-1.0 / 16)
                _act(S, Einv[:], bpos[:, ts], AF.Exp, ["bpos"], ["Einv"], scale=1.0 / 16)
                S.op("dve", lambda e, tl=tl: e.tensor_scalar(out=nbl[:], in0=bpos[:, tl * 128 + 127:tl * 128 + 128], scalar1=-1.0 / 16, scalar2=None, op0=ALU.mult),
                     reads=["bpos"], writes=["nbl"])
                _act(S, Est[:], bpos[:, ts], AF.Exp, ["bpos", "nbl"], ["Est"], scale=1.0 / 16, bias=nbl[:, 0:1])
                S.op("dve", lambda e, ts=ts: e.scalar_tensor_tensor(out=qdT[:], in0=qaT[:, ts], scalar=128.0 ** -0.5, in1=Eb[:], op0=ALU.mult, op1=ALU.mult),
                     reads=["qaT", "Eb"], writes=["qdT"])
                S.op("dve", lambda e, ts=ts: e.tensor_tensor(out=kinvT[:], in0=kaT[:, ts], in1=Einv[:], op=ALU.mult), reads=["kaT", "Einv"], writes=["kinvT"])
                S.op("dve", lambda e, ts=ts: e.tensor_tensor(out=kstT[:], in0=kaT[:, ts], in1=Est[:], op=ALU.mult), reads=["kaT", "Est"], writes=["kstT"])
                S.op("pe", lambda e: e.transpose(out=ptr[:, 0:128], in_=kstT[:], identity=IDb), reads=["kstT", "cb"], writes=["ptr"])
                S.op("act", lambda e: e.copy(out=kstk[:], in_=ptr[:, 0:128]), reads=["ptr"], writes=["kstk"])
                _mm(S, pss[:, 128:256], kinvT[:], qdT[:], True, True, ["kinvT", "qdT"], ["pss"])
                S.op("dve", lambda e: e.tensor_tensor(out=attT[:], in0=pss[:, 128:256], in1=cf[:, C_MU, :], op=ALU.mult), reads=["pss", "cf"], writes=["attT"])
                for vc in range(2):
                    _mm(S, pss[:, 256 + vc * 128:384 + vc * 128], va[:, tl, vc * 128:(vc + 1) * 128], attT[:], True, False, ["va", "attT"], ["pss"])
                    _mm(S, pss[:, 256 + vc * 128:384 + vc * 128], Sab[:, vc * 128:(vc + 1) * 128], qdT[:], False, True, ["Sab", "qdT"], ["pss"])
                _act(S, sqo[:].rearrange("p a b -> p (a b)"), pss[:, 256:512], AF.Square, ["pss"], ["sqo"])
                for vc in range(2):
                    S.op("dve", lambda e, vc=vc: e.tensor_copy(out=oa[vc][:], in_=pss[:, 256 + vc * 128:384 + vc * 128]), reads=["pss"], writes=[f"oa{vc}"])
                _mm(S, ptm[:, 256:512], kstk[:], va[:, tl, :], True, True, ["kstk", "va"], ["ptm"])
                S.op("dve", lambda e: e.scalar_tensor_tensor(out=Sa[:], in0=Sa[:], scalar=Eb[:, 127:128], in1=ptm[:, 256:512], op0=ALU.mult, op1=ALU.add),
                     reads=["Sa", "Eb", "ptm"], writes=["Sa"])
                S.op("act", lambda e: e.copy(out=Sab[:], in_=Sa[:]), reads=["Sa"], writes=["Sab"])
                for vc in range(2):
                    _mm(S, pss[:, 0:128], g.ones_bf, sqo[:, vc, :], vc == 0, vc == 1, ["cb", "sqo"], ["pss"])
                _rstd(S, rso[:], pss[:, 0:128], 1.0 / 256, ["pss"], ["rso"])
                for vc in range(2):
                    S.op("dve", lambda e, vc=vc: e.tensor_tensor(out=t1[:], in0=oa[vc][:], in1=rso[:], op=ALU.mult), reads=[f"oa{vc}", "rso"], writes=["t1"])
                    S.op("dve", lambda e, vc=vc, ts=ts, ob=ob: e.scalar_tensor_tensor(out=ob[:, vc, ts], in0=t1[:], scalar=sm[:, 1 + vc:2 + vc], in1=sagT[:, vc, ts],
                                                                                    op0=ALU.mult, op1=ALU.mult), reads=["t1", "sm", "sagT"], writes=[obk])
                S.skip = "gdn" in SKIP
                cap_gdn = []
                cap_chain = []
                S.capture = cap_gdn
                sl = tl % 2
                kst_d, attT_d, qd_d, u_sb, wT_sb = kst_dd[sl], attT_dd[sl], qd_dd[sl], u_sbd[sl], wT_sbd[sl]
                K_kst, K_att, K_qd, K_u, K_w = f"kst_d{sl}", f"attT_d{sl}", f"qd_d{sl}", f"u_sb{sl}", f"wT_sb{sl}"
                HR = range(2)
                hs = lambda h: slice(h * 128, (h + 1) * 128)
                for h in HR:
                    S.op("pe", lambda e, h=h, ts=ts: e.transpose(out=ptr[:, (1 + h) * 128:(2 + h) * 128], in_=knT[:, h, ts], identity=IDb), reads=["knT", "cb"], writes=["ptr"])
                for h in HR:
                    S.op("pe", lambda e, h=h, ts=ts: e.transpose(out=ptr[:, (3 + h) * 128:(4 + h) * 128], in_=vT[:, h, ts], identity=IDb), reads=["vT", "cb"], writes=["ptr"])
                for h in HR:
                    idx = tl * 2 + h
                    ks = slice((1 + h) * 128, (2 + h) * 128)
                    vs = slice((3 + h) * 128, (4 + h) * 128)
                    S.op("act", lambda e, idx=idx, h=h, ks=ks, kst_d=kst_d: e.activation(out=kst_d[0][0:64, h, :], in_=ptr[0:64, ks], func=AF.Identity, scale=est[0:64, idx:idx + 1]),
                         reads=["ptr", "est"], writes=[K_kst])
                    S.op("act", lambda e, idx=idx, h=h, ks=ks, kst_d=kst_d: e.activation(out=kst_d[1][64:128, h, :], in_=ptr[64:128, ks], func=AF.Identity, scale=est[64:128, idx:idx + 1]),
                         reads=["ptr", "est"], writes=[K_kst])
                    S.op("dve", lambda e, idx=idx, h=h, ks=ks: e.tensor_scalar(out=kbg[:, h, :], in0=ptr[:, ks], scalar1=bcoef[:, idx:idx + 1], scalar2=None, op0=ALU.mult),
                         reads=["ptr", "bcoef"], writes=["kbg"])
                    S.op("dve", lambda e, tl=tl, h=h, vs=vs: e.tensor_scalar(out=vb[:, h, :], in0=ptr[:, vs], scalar1=beta[:, tl, h:h + 1], scalar2=None, op0=ALU.mult),
                         reads=["ptr", "beta"], writes=["vb"])
                    S.op("dve", lambda e, tl=tl, h=h: e.tensor_scalar(out=gbc[:, h, :], in0=ones_f, scalar1=gcol[:, tl, h:h + 1], scalar2=None, op0=ALU.mult),
                         reads=["cf", "gcol"], writes=["gbc"])
                for h in HR:
                    _mm(S, pp[1][:, hs(h)], gbc[:, h, :], cf[:, C_TRI, :], True, True, ["gbc", "cf"], ["pp1"])
                for h in HR:
                    idx = tl * 2 + h
                    S.op("dve", lambda e, idx=idx, h=h: e.scalar_tensor_tensor(out=e1[:, h, :], in0=pp[1][:, hs(h)], scalar=gst[:, 0, idx:idx + 1], in1=cf[:, C_BIGC, :],
                                                                            op0=ALU.subtract, op1=ALU.add), reads=["pp1", "gst", "cf"], writes=["e1"])
                _act(S, fl(DC), fl(e1), AF.Exp, ["e1"], ["DC"], scale=-1.0)
                _act(S, fl(EG), pp[1][:, 0:256], AF.Exp, ["pp1"], ["EG"])
                for h in HR:
                    S.op("dve", lambda e, h=h: e.tensor_tensor(out=DSL[:, h, :], in0=DC[:, h, :], in1=cf[:, C_SL, :], op=ALU.mult), reads=["DC", "cf"], writes=["DSL"])
                S.op("dve", lambda e, ts=ts, qd_d=qd_d: e.tensor_tensor(out=qd_d[:], in0=qnT[:, :, ts], in1=EG[:], op=ALU.mult), reads=["qnT", "EG"], writes=[K_qd])
                for h in HR:
                    _mm(S, pp[0][:, hs(h)], knT[:, h, ts], knT[:, h, ts], True, True, ["knT"], ["pp0"])
                for h in HR:
                    _mm(S, pp[0][:, 256 + h * 128:384 + h * 128], qnT[:, h, ts], knT[:, h, ts], True, True, ["qnT", "knT"], ["pp0"])
                for h in HR:
                    S.op("dve", lambda e, tl=tl, h=h: e.scalar_tensor_tensor(out=P[0][:, h, :], in0=pp[0][:, hs(h)], scalar=nbeta[:, tl, h:h + 1], in1=DSL[:, h, :],
                                                                          op0=ALU.mult, op1=ALU.mult), reads=["pp0", "nbeta", "DSL"], writes=["P0"])
                S.op("dve", lambda e: e.tensor_tensor(out=fl(att_d), in0=pp[0][:, 256:512], in1=fl(DC), op=ALU.mult), reads=["pp0", "DC"], writes=["att_d"])
                for h in HR:
                    S.op("pe", lambda e, h=h: e.transpose(out=pp[1][:, 256 + h * 128:384 + h * 128], in_=P[0][:, h, :], identity=IDf), reads=["P0", "cf"], writes=["pp1"])
                S.op("act", lambda e: e.copy(out=fl(PT[0]), in_=pp[1][:, 256:512]), reads=["pp1"], writes=["PT0"])
                for h in HR:
                    S.op("pe", lambda e, h=h: e.transpose(out=ptr[:, (5 + h) * 128:(6 + h) * 128], in_=att_d[:, h, :], identity=IDb), reads=["att_d", "cb"], writes=["ptr"])
                S.op("act", lambda e, attT_d=attT_d: e.copy(out=fl(attT_d), in_=ptr[:, 640:896]), reads=["ptr"], writes=[K_att])
                for h in HR:
                    S.op("dve", lambda e, h=h: e.tensor_tensor(out=X[0][:, h, :], in0=PT[0][:, h, :], in1=IDf, op=ALU.add), reads=["PT0", "cf"], writes=["X0"])
                cur = 0
                for k in range(5):
                    nxt = 1 - cur
                    for h in HR:
                        _mm(S, ptm[:, hs(h)], PT[cur][:, h, :], P[cur][:, h, :], True, True, [f"PT{cur}", f"P{cur}"], ["ptm"])
                    if k < 4:
                        for h in HR:
                            _mm(S, pi[:, hs(h)], P[cur][:, h, :], PT[cur][:, h, :], True, True, [f"PT{cur}", f"P{cur}"], ["pi"])
                    S.op("act", lambda e, nxt=nxt: e.copy(out=fl(P[nxt]), in_=ptm[:, 0:256]), reads=["ptm"], writes=[f"P{nxt}"])
                    if k < 4:
                        S.op("dve", lambda e, nxt=nxt: e.tensor_copy(out=fl(PT[nxt]), in_=pi[:, 0:256]), reads=["pi"], writes=[f"PT{nxt}"])
                    for h in HR:
                        _mm(S, pg[:, hs(h)], P[nxt][:, h, :], X[cur][:, h, :], True, True, [f"P{nxt}", f"X{cur}"], ["pg"])
                    S.op("dve", lambda e, nxt=nxt, cur=cur: e.tensor_tensor(out=fl(X[nxt]), in0=pg[:, 0:256], in1=fl(X[cur]), op=ALU.add),
                         reads=["pg", f"X{cur}"], writes=[f"X{nxt}"])
                    cur = nxt
                Xf = X[cur]
                xk2 = f"X{cur}"
                for h in HR:
                    _mm(S, pg[:, 256 + h * 128:384 + h * 128], Xf[:, h, :], vb[:, h, :], True, True, [xk2, "vb"], ["pg"])
                S.op("act", lambda e, u_sb=u_sb: e.copy(out=fl(u_sb), in_=pg[:, 256:512]), reads=["pg"], writes=[K_u])
                for h in HR:
                    _mm(S, pi[:, hs(h)], kbg[:, h, :], Xf[:, h, :], True, True, [xk2, "kbg"], ["pi"])
                S.op("act", lambda e, wT_sb=wT_sb: e.copy(out=fl(wT_sb), in_=pi[:, 0:256]), reads=["pi"], writes=[K_w])
                S.capture = cap_chain
                for hf in range(2):
                    rs_ = slice(hf * 64, hf * 64 + 64)
                    for h in HR:
                        _mm(S, pi[rs_, 256 + h * 128:384 + h * 128], wT_sb[:, h, rs_], Sdb[:, h, :], True, True, [K_w, "Sdb"], ["pi"])
                    S.op("dve", lambda e, rs_=rs_, u_sb=u_sb: e.tensor_tensor(out=vnew[rs_, :, :].rearrange("p a b -> p (a b)"), in0=u_sb[rs_, :, :].rearrange("p a b -> p (a b)"),
                                                                   in1=pi[rs_, 256:512], op=ALU.subtract), reads=[K_u, "pi"], writes=["vnew"])
                    for h in HR:
                        oc = slice(256 + h * 128 + hf * 64, 256 + h * 128 + hf * 64 + 64)
                        _mm(S, pw[:, oc], Sdb[:, h, :], qd_d[:, h, rs_], True, False, ["Sdb", K_qd], ["pw"])
                        _mm(S, pw[:, oc], vnew[:, h, :], attT_d[:, h, rs_], False, True, ["vnew", K_att], ["pw"])
                    for h in HR:
                        _mm(S, pw[:, hs(h)], kst_d[hf][:, h, :], vnew[:, h, :], True, True, [K_kst, "vnew"], ["pw"])
                    for h in HR:
                        idx = tl * 2 + h
                        S.op("dve", lambda e, h=h, hf=hf, idx=idx: e.scalar_tensor_tensor(out=Sd[:, h, :], in0=Sd[:, h, :], scalar=edec[:, hf, idx:idx + 1], in1=pw[:, hs(h)],
                                                                                       op0=ALU.mult, op1=ALU.add), reads=["Sd", "edec", "pw"], writes=["Sd"])
                    S.op("act", lambda e: e.copy(out=fl(Sdb), in_=fl(Sd)), reads=["Sd"], writes=["Sdb"])
                _act(S, fl(sqo_d), pw[:, 256:512], AF.Square, ["pw"], ["sqo_d"])
                for h in HR:
                    _mm(S, pw[:, hs(h)], g.ones_bf, sqo_d[:, h, :], True, True, ["cb", "sqo_d"], ["pw"])
                _rstd(S, fl(rso_d), pw[:, 0:256], 1.0 / 128, ["pw"], ["rso_d"])
                S.op("dve", lambda e: e.tensor_tensor(out=fl(t1_d), in0=pw[:, 256:512], in1=fl(rso_d), op=ALU.mult), reads=["pw", "rso_d"], writes=["t1_d"])
                S.op("dve", lambda e, ts=ts, ob=ob: e.scalar_tensor_tensor(out=ob[:, 2:4, ts], in0=t1_d[:], scalar=sm[:, 3:4], in1=sdgT[:, :, ts],
                                                                         op0=ALU.mult, op1=ALU.mult), reads=["t1_d", "sm", "sdgT"], writes=[obk])
                S.capture = None
                S.skip = False
                if tl == 0:
                    S.replay_interleaved([cap_gdn, cap_gla])
                else:
                    S.replay_interleaved([cap_gdn, prev_chain, cap_gla])
                prev_chain = cap_chain
                if tl == 3:
                    S.replay_interleaved([prev_chain])
            S.skip = False
            if not g.fused:
                for j in range(4):
                    S.op("sp", lambda e, j=j, ob=ob, blk=blk: e.dma_start(out=g.d_oT[:, j, blk * BT:(blk + 1) * BT], in_=ob[:, j, :]), reads=[obk], dma_key=obk + "s")
            else:
                psz = min(BT, g.TCH)
                for pc in range(BT // psz):
                    tok0 = blk * BT + pc * psz
                    k, off = tok0 // g.TCH, tok0 % g.TCH
                    for j in range(4):
                        S.op("sp", lambda e, j=j, ob=ob, k=k, off=off, pc=pc: e.dma_start(out=g.d_oT[k, :, j, off:off + psz], in_=ob[:, j, pc * psz:(pc + 1) * psz]),
                             reads=[obk], writes=[f"oTd{k}"], dma_key=obk + "s")
                while g.next_cc < g.NCK and (g.next_cc + 1) * g.TCH <= (blk + 1) * BT:
                    k = g.next_cc
                    S.op("pool", lambda e, k=k: e.collective_compute("AllGather", ALU.bypass, replica_groups=[[0, 1, 2, 3], [4, 5, 6, 7]],
                                                                     ins=[g.d_oT[k].rearrange("p j t -> p (j t)")], outs=[g.d_gath[k].rearrange("q j t -> q (j t)")]),
                         reads=[f"oTd{k}"], writes=[f"gath{k}"], dma_key="cc", dma_inc=1)
                    g.next_cc += 1
        S.barrier()
        S.emit()


def emit_phase2(g, S, nc, SQ, o2_src, o2_reads):
    BT = 256
    NB = SQ // BT
    with ExitStack() as es:
        sb = lambda name, shape, dt: es.enter_context(nc.sbuf_tensor("p2a_" + name, shape, dt))
        pst = lambda name, dt=F32, n=512: es.enter_context(nc.psum_tensor("p2a_" + name, [128, n], dt))
        wm = sb("wm", [128, NCH, 2048], BF16)
        wbg = sb("wbg", [128, NCH, 1024], BF16)
        wbd = sb("wbd", [128, NCH, 1024], BF16)
        wo = sb("wo", [128, NCH, 1024], BF16)
        load_w_bf16(S, wm, "wm", g.d_wm, 2048)
        load_w_bf16(S, wbg, "wbg", g.d_wbg, 1024)
        load_w_bf16(S, wbd, "wbd", g.d_wbd, 1024)
        load_w_bf16(S, wo, "wo", g.d_wo, 1024)
        xt = [sb(f"x2t{i}", [128, NCH, BT], F32) for i in range(2)]
        xr = [sb(f"x2r{i}", [128, NCH, BT], F32) for i in range(2)]
        sq = sb("sq", [128, NCH, BT], BF16)
        rstd = sb("rstd", [128, BT], F32)
        hT = sb("hT", [128, NCH, BT], BF16)
        sg = sb("sg", [128, 16, BT], F32)
        o2 = [sb(f"o2{i}", [128, 16, BT], BF16) for i in range(2)]
        mg = sb("mg", [128, NCH, BT], BF16)
        tmp = sb("tmp", [128, BT], F32)
        zz = sb("zz", [128, NCH, BT], F32)
        pp = [pst(f"pp{i}") for i in range(2)]
        pss = pst("pss")
        def load2a(b_):
            s_ = b_ % 2
            x_, xk = xt[s_], f"x2t{s_}"
            xr_, xrk = xr[s_], f"x2r{s_}"
            o_, ok = o2[s_], f"o2{s_}"
            t0 = b_ * BT
            for c in range(NCH):
                S.op("sp", lambda e, c=c, x_=x_, t0=t0: e.dma_start(out=x_[:, c, :], in_=g.d_xT2[:, c, t0:t0 + BT]), writes=[xk], dma_key=xk)
                S.op("sp", lambda e, c=c, xr_=xr_, t0=t0: e.dma_start(out=xr_[:, c, :], in_=g.d_xT2[:, c, t0:t0 + BT]), writes=[xrk], dma_key=xrk)
            for kind in range(2):
                for c in range(NCH):
                    S.op("sp", lambda e, kind=kind, c=c, o_=o_, t0=t0: e.dma_start(out=o_[:, kind * 8 + c, :], in_=o2_src(e, kind, c, t0, BT)), reads=o2_reads, writes=[ok], dma_key=ok)
        load2a(0)
        for blk in range(NB):
            s_ = blk % 2
            x_, xk = xt[s_], f"x2t{s_}"
            xr_, xrk = xr[s_], f"x2r{s_}"
            o_, ok = o2[s_], f"o2{s_}"
            t0 = blk * BT
            if blk + 1 < NB:
                load2a(blk + 1)
            emit_norm_block(g, S, x_, xk, BT, hT, "hT", sq, pss, rstd, 0, 1)
            for j in range(16):
                p_, pk = pp[j % 2], f"pp{j % 2}"
                for c in range(NCH):
                    _mm(S, p_[:, 0:BT], wm[:, c, j * 128:(j + 1) * 128], hT[:, c, :], c == 0, c == NCH - 1, ["wm", "hT"], [pk])
                _act(S, sg[:, j, :], p_[:, 0:BT], AF.Sigmoid, [pk], ["sg"])
            for oc in range(NCH):
                for kind, w_ in ((0, wbg), (1, wbd)):
                    p_, pk = pp[kind], f"pp{kind}"
                    wk = "wbg" if kind == 0 else "wbd"
                    for c in range(NCH):
                        _mm(S, p_[:, 0:BT], w_[:, c, oc * 128:(oc + 1) * 128], o_[:, kind * 8 + c, :], c == 0, c == NCH - 1, [wk, ok], [pk])
                S.op("dve", lambda e, oc=oc: e.tensor_tensor(out=tmp[:], in0=pp[0][:, 0:BT], in1=sg[:, oc, :], op=ALU.mult), reads=["pp0", "sg"], writes=["tmp"])
                S.op("dve", lambda e, oc=oc: e.tensor_tensor(out=zz[:, oc, :], in0=pp[1][:, 0:BT], in1=sg[:, 8 + oc, :], op=ALU.mult), reads=["pp1", "sg"], writes=["zz"])
                S.op("dve", lambda e, oc=oc: e.tensor_tensor(out=mg[:, oc, :], in0=tmp[:], in1=zz[:, oc, :], op=ALU.add), reads=["tmp", "zz"], writes=["mg"])
            for oc in range(NCH):
                p_, pk = pp[oc % 2], f"pp{oc % 2}"
                for c in range(NCH):
                    _mm(S, p_[:, 0:BT], wo[:, c, oc * 128:(oc + 1) * 128], mg[:, c, :], c == 0, c == NCH - 1, ["wo", "mg"], [pk])
                S.op("dve", lambda e, oc=oc, p_=p_: e.tensor_copy(out=zz[:, oc, :], in_=p_[:, 0:BT]), reads=[pk], writes=["zz"])
            _act(S, sq[:], zz[:], AF.Square, ["zz"], ["sq"])
            for c in range(NCH):
                _mm(S, pss[:, 0:BT], g.ones_bf, sq[:, c, :], c == 0, c == NCH - 1, ["cb", "sq"], ["pss"])
            _rstd(S, rstd[:], pss[:, 0:BT], 1.0 / D, ["pss"], ["rstd"])
            for c in range(NCH):
                S.op("dve", lambda e, c=c: e.tensor_tensor(out=zz[:, c, :], in0=zz[:, c, :], in1=rstd[:], op=ALU.mult), reads=["zz", "rstd"], writes=["zz"])
                S.op("dve", lambda e, c=c, xr_=xr_: e.scalar_tensor_tensor(out=xr_[:, c, :], in0=zz[:, c, :], scalar=g.AB[:, 2, c:c + 1], in1=xr_[:, c, :],
                                                                         op0=ALU.mult, op1=ALU.add), reads=["zz", "AB", xrk], writes=[xrk])
                S.op("sp", lambda e, c=c, xr_=xr_, t0=t0: e.dma_start(out=g.d_x1[:, c, t0:t0 + BT], in_=xr_[:, c, :]), reads=[xrk], writes=["x1d"], dma_key=xrk + "s")
        S.barrier()
        S.emit()
    with ExitStack() as es:
        sb = lambda name, shape, dt: es.enter_context(nc.sbuf_tensor("p2b_" + name, shape, dt))
        pst = lambda name, dt=F32, n=512: es.enter_context(nc.psum_tensor("p2b_" + name, [128, n], dt))
        w1 = sb("w1", [128, NCH, 4096], BF16)
        w2 = sb("w2", [128, 32, 1024], BF16)
        load_w_bf16(S, w1, "w1", g.d_w1, 4096)
        load_w_bf16(S, w2, "w2", g.d_w2, 1024, nrows=32)
        xr = [sb(f"x3r{i}", [128, NCH, BT], F32) for i in range(2)]
        sq = sb("sq", [128, NCH, BT], BF16)
        rstd = sb("rstd", [128, BT], F32)
        hT = sb("hT", [128, NCH, BT], BF16)
        rl = [sb(f"rl{i}", [128, BT], F32) for i in range(2)]
        hid = sb("hid", [128, 32, BT], BF16)
        zz = sb("zz", [128, NCH, BT], F32)
        pp = [pst(f"pp{i}") for i in range(2)]
        pss = pst("pss")

        def load2b(b_):
            for c in range(NCH):
                S.op("sp", lambda e, c=c, b_=b_: e.dma_start(out=xr[b_ % 2][:, c, :], in_=g.d_x1[:, c, b_ * BT:(b_ + 1) * BT]),
                     reads=["x1d"], writes=[f"x3r{b_ % 2}"], dma_key=f"x3r{b_ % 2}")
        load2b(0)
        for blk in range(NB):
            s_ = blk % 2
            xr_, xrk = xr[s_], f"x3r{s_}"
            t0 = blk * BT
            if blk + 1 < NB:
                load2b(blk + 1)
            emit_norm_block(g, S, xr_, xrk, BT, hT, "hT", sq, pss, rstd, 3, 4, tdst=zz, tkey="zz")
            for fc in range(32):
                p_, pk = pp[fc % 2], f"pp{fc % 2}"
                for c in range(NCH):
                    _mm(S, p_[:, 0:BT], w1[:, c, fc * 128:(fc + 1) * 128], hT[:, c, :], c == 0, c == NCH - 1, ["w1", "hT"], [pk])
                _act(S, rl[fc % 2][:], p_[:, 0:BT], AF.Relu, [pk], [f"rl{fc % 2}"])
                S.op("dve", lambda e, fc=fc: e.tensor_tensor(out=hid[:, fc, :], in0=rl[fc % 2][:], in1=rl[fc % 2][:], op=ALU.mult), reads=[f"rl{fc % 2}"], writes=["hid"])
            for oc in range(NCH):
                p_, pk = pp[oc % 2], f"pp{oc % 2}"
                for fc in range(32):
                    _mm(S, p_[:, 0:BT], w2[:, fc, oc * 128:(oc + 1) * 128], hid[:, fc, :], fc == 0, fc == 31, ["w2", "hid"], [pk])
                S.op("dve", lambda e, oc=oc, p_=p_: e.tensor_copy(out=zz[:, oc, :], in_=p_[:, 0:BT]), reads=[pk], writes=["zz"])
            _act(S, sq[:], zz[:], AF.Square, ["zz"], ["sq"])
            for c in range(NCH):
                _mm(S, pss[:, 0:BT], g.ones_bf, sq[:, c, :], c == 0, c == NCH - 1, ["cb", "sq"], ["pss"])
            _rstd(S, rstd[:], pss[:, 0:BT], 1.0 / D, ["pss"], ["rstd"])
            for c in range(NCH):
                S.op("dve", lambda e, c=c: e.tensor_tensor(out=zz[:, c, :], in0=zz[:, c, :], in1=rstd[:], op=ALU.mult), reads=["zz", "rstd"], writes=["zz"])
                S.op("dve", lambda e, c=c, xr_=xr_: e.scalar_tensor_tensor(out=xr_[:, c, :], in0=zz[:, c, :], scalar=g.AB[:, 5, c:c + 1], in1=xr_[:, c, :],
                                                                         op0=ALU.mult, op1=ALU.add), reads=["zz", "AB", xrk], writes=[xrk])
                S.op("sp", lambda e, c=c, xr_=xr_, t0=t0: e.dma_start(out=g.d_out[:, c, t0:t0 + BT], in_=xr_[:, c, :]), reads=[xrk], dma_key=xrk + "s")
        S.barrier()
        S.emit()


def build_program(SEQ, mode):
    SQ = SEQ // 4
    nc = bass.Bass("TRN2", target_bir_lowering=False)
    g = Ctx()
    din = lambda name, shape, dt=F32: nc.dram_tensor(name, shape, dt, kind="ExternalInput").ap()
    g.d_consts = din("consts", [128, NCONST, 128])
    g.d_cT = din("cT", [128, NCH])
    g.d_adab = din("adab", [128, 48])
    g.d_nw = din("nw", [128, 4, NCH])
    g.d_adaw = din("adaw", [128, NCH, 6144])
    if mode in ("A", "F"):
        g.d_xT = din("xT", [128, NCH, SEQ])
        g.d_wfm = din("wfm", [128, NCH, NFM])
        g.d_wlri = din("wlri", [128, NCH, 16])
        g.d_wtm = din("wtm", [128, NCH, NTM])
        g.d_wlr = din("wlr", [16, 128])
        g.d_sm = din("sm", [128, 8])
        g.d_cw = din("cw", [128, 6, 4])
    if mode == "A":
        g.d_oT = nc.dram_tensor("oT", [128, 4, SEQ], BF16, kind="ExternalOutput").ap()
    if mode == "F":
        g.TCH = min(1024, SQ)
        g.NCK = SEQ // g.TCH
        g.d_oT = nc.dram_tensor("oT", [g.NCK, 128, 4, g.TCH], BF16, kind="Internal").ap()
        g.d_gath = nc.dram_tensor("gath", [g.NCK, 4 * 128, 4, g.TCH], BF16, kind="Internal").ap()
        g.d_o2q = nc.dram_tensor("o2q", [4 * 128, 4, SQ], BF16, kind="Internal").ap()
    if mode in ("B", "F"):
        g.d_xT2 = din("xT2", [128, NCH, SQ])
        g.d_wm = din("wm", [128, NCH, 2048])
        g.d_wbg = din("wbg", [128, NCH, 1024])
        g.d_wbd = din("wbd", [128, NCH, 1024])
        g.d_wo = din("wo", [128, NCH, 1024])
        g.d_w1 = din("w1", [128, NCH, 4096])
        g.d_w2 = din("w2", [128, 32, 1024])
        g.d_x1 = nc.dram_tensor("x1", [128, NCH, SQ], F32, kind="Internal").ap()
        g.d_out = nc.dram_tensor("out", [128, NCH, SQ], F32, kind="ExternalOutput").ap()
    if mode == "B":
        g.d_o2 = din("o2", [128, 2, NCH, SQ], BF16)
    with ExitStack() as es:
        S = Sched(nc, es)
        g.fused = mode == "F"
        g.next_cc = 0
        emit_common(g, S, es, nc, True)
        if mode in ("A", "F"):
            emit_phase1(g, S, nc, SEQ)
        if mode == "F":
            CPQ = SQ // g.TCH
            gathv = g.d_gath.rearrange("(q c) r j t -> q c r j t", c=CPQ)

            def mkcopy(cc):
                def f(e):
                    if g.pidc.get("blk") != S.nblocks:
                        g.pidc = {"blk": S.nblocks, "pid4": e.snap(e.partition_id() % 4)}
                    src = gathv[bass.DynSlice(g.pidc["pid4"], 1), cc, :, :, :].rearrange("q r j t -> (q r) j t")
                    return e.dma_start(out=g.d_o2q[:, :, cc * g.TCH:(cc + 1) * g.TCH], in_=src)
                return f
            g.pidc = {}
            for cc in range(CPQ):
                S.op("sp", mkcopy(cc), reads=[f"gath{k}" for k in range(g.NCK)], writes=["o2q"], dma_key="o2q")

            def o2_src(e, kind, c, t0, n):
                hg, j = c // 2, c % 2
                return g.d_o2q[hg * 128:(hg + 1) * 128, kind * 2 + j, t0:t0 + n]
            emit_phase2(g, S, nc, SQ, o2_src, ["o2q"])
        if mode == "B":
            emit_phase2(g, S, nc, SQ, lambda e, kind, c, t0, n: g.d_o2[:, kind, c, t0:t0 + n], [])
    return nc


def _fm(w):
    r, n = w.shape
    return np.ascontiguousarray(w.reshape(r // 128, 128, n).transpose(1, 0, 2))


def _consts():
    p = np.arange(128)[:, None]
    f = np.arange(128)[None, :]
    same = (p // 64) == (f // 64)
    c = np.zeros((NCONST, 128, 128), np.float32)
    c[C_ID] = (p == f)
    c[C_MU] = (p <= f)
    c[C_SL] = same & (f < p)
    c[C_BIGC] = np.where(same & (f <= p), 0.0, BIG)
    c[C_TRI] = same & (p <= f)
    c[C_BLK] = same
    c[C_SEL0] = (p < 64) & (f >= 0)
    c[C_SEL1] = (p >= 64) & (f >= 0)
    c[C_ONES] = 1.0
    return np.ascontiguousarray(c.transpose(1, 0, 2))


def host_inputs(inp, SEQ):
    x = inp["x"]
    w_in = inp["w_in"][0]
    maps1, maps2 = [], []
    consts = _consts()
    adaw = _fm(inp["ada_w"][0])
    nw = np.stack([inp[k][0].reshape(8, 128).T for k in ("pre_mix_w", "post_mix_w", "pre_mlp_w", "post_mlp_w")], axis=1)
    adab = np.ascontiguousarray(inp["ada_b"][0].reshape(48, 128).T)
    offs = np.cumsum([0, 512, 512, 1024, 1024, 16, 3072, 1024, 8, 8, 1024, 1024])
    o_aq, o_ak, o_av, o_ag, o_lr, o_dqkv, o_dg, o_db, o_da, o_ma, o_md = offs[:11]
    SQ = SEQ // 4
    xTs = [np.ascontiguousarray(x[b, :SEQ].T.reshape(8, 128, SEQ).transpose(1, 0, 2)) for b in range(x.shape[0])]
    for r in range(8):
        b, hg = r // 4, r % 4
        common = {"consts": consts, "cT": np.ascontiguousarray(inp["c"][b].reshape(8, 128).T), "adab": adab,
                  "nw": np.ascontiguousarray(nw), "adaw": adaw}
        cols = []
        cols += list(range(o_aq + hg * 128, o_aq + (hg + 1) * 128))
        cols += list(range(o_ak + hg * 128, o_ak + (hg + 1) * 128))
        cols += list(range(o_ag + hg * 256, o_ag + (hg + 1) * 256))
        for part in range(3):
            cols += list(range(o_dqkv + part * 1024 + hg * 256, o_dqkv + part * 1024 + (hg + 1) * 256))
        cols += list(range(o_dg + hg * 256, o_dg + (hg + 1) * 256))
        tcols = list(range(o_av + hg * 256, o_av + (hg + 1) * 256)) + [o_db + 2 * hg, o_db + 2 * hg + 1, o_da + 2 * hg, o_da + 2 * hg + 1]
        sm = np.zeros((128, 8), np.float32)
        sm[:, 0] = inp["gla_b_lr"][0][hg * 128:(hg + 1) * 128]
        sm[:, 1:3] = inp["gla_onorm_w"][0].reshape(2, 128).T
        sm[:, 3] = inp["gdn_onorm_w"][0]
        sm[:, 4:6] = inp["gdn_a_log"][0][None, 2 * hg:2 * hg + 2]
        sm[:, 6:8] = inp["gdn_dt_bias"][0][None, 2 * hg:2 * hg + 2]
        cwfull = inp["gdn_conv_w"][0]
        cw = np.zeros((128, 6, 4), np.float32)
        for part in range(3):
            for h in range(2):
                c0 = part * 1024 + (2 * hg + h) * 128
                cw[:, part * 2 + h, :] = cwfull[:, c0:c0 + 128].T
        m1 = dict(common)
        m1.update({"xT": xTs[b], "wfm": _fm(w_in[:, cols]), "wlri": _fm(w_in[:, o_lr:o_lr + 16]), "wtm": _fm(np.concatenate([w_in[:, tcols], np.zeros((D, NTM - 260), np.float32)], axis=1)),
                   "wlr": np.ascontiguousarray(inp["gla_w_lr"][0][:, hg * 128:(hg + 1) * 128]), "sm": sm, "cw": cw})
        maps1.append(m1)
        m2 = dict(common)
        m2.update({"xT2": np.ascontiguousarray(xTs[b][:, :, hg * SQ:(hg + 1) * SQ]),
                   "wm": _fm(w_in[:, o_ma:o_ma + 2048]), "wbg": _fm(inp["w_branch_gla"][0]), "wbd": _fm(inp["w_branch_gdn"][0]),
                   "wo": _fm(inp["w_out"][0]), "w1": _fm(inp["mlp_w1"][0]), "w2": _fm(inp["mlp_w2"][0])})
        maps2.append(m2)
    return maps1, maps2


_CACHE = {}


def _prog(SEQ, mode):
    if (SEQ, mode) not in _CACHE:
        _CACHE[(SEQ, mode)] = build_program(SEQ, mode)
    return _CACHE[(SEQ, mode)]


def run_unfused(inp, SEQ):
    maps1, maps2 = host_inputs(inp, SEQ)
    SQ = SEQ // 4
    resA = run_bass_kernel_spmd(_prog(SEQ, "A"), maps1, core_ids=list(range(8)))
    oTs = [np.asarray(r["oT"]) for r in resA.results]
    for r in range(8):
        b, tq = r // 4, r % 4
        o2 = np.zeros((128, 2, 8, SQ), oTs[0].dtype)
        for hg in range(4):
            src = oTs[b * 4 + hg][:, :, tq * SQ:(tq + 1) * SQ]
            o2[:, 0, hg * 2:hg * 2 + 2] = src[:, 0:2]
            o2[:, 1, hg * 2:hg * 2 + 2] = src[:, 2:4]
        maps2[r]["o2"] = o2
    resB = run_bass_kernel_spmd(_prog(SEQ, "B"), maps2, core_ids=list(range(8)))
    B = inp["x"].shape[0]
    out = np.zeros((B, SEQ, D), np.float32)
    for r in range(8):
        b, tq = r // 4, r % 4
        o = np.asarray(resB.results[r]["out"])
        out[b, tq * SQ:(tq + 1) * SQ, :] = o.transpose(2, 1, 0).reshape(SQ, D)
    return out, oTs


def run_fused(inp, SEQ):
    maps1, maps2 = host_inputs(inp, SEQ)
    SQ = SEQ // 4
    maps = []
    for r in range(8):
        m = dict(maps1[r])
        m.update(maps2[r])
        maps.append(m)
    res = run_bass_kernel_spmd(_prog(SEQ, "F"), maps, core_ids=list(range(8)))
    B = inp["x"].shape[0]
    out = np.zeros((B, SEQ, D), np.float32)
    for r in range(8):
        b, tq = r // 4, r % 4
        o = np.asarray(res.results[r]["out"])
        out[b, tq * SQ:(tq + 1) * SQ, :] = o.transpose(2, 1, 0).reshape(SQ, D)
    return out


def kernel(**inputs):
    inp = {k: np.asarray(v) for k, v in inputs.items()}
    return run_fused(inp, inp["x"].shape[1])
```
